# Optimizing a Trainium2 kernel written in Bass

```python
import math
import jax
import jax.numpy as jnp
from jax import lax
import numpy as np

D_MODEL = 1024
BATCH = 8
SEQ = 4096
DEPTH = 4

GRID_W = 64
PLE_DIM = 256
EPS = 1e-6
N_EVEN = (DEPTH + 1) // 2
N_ODD = DEPTH // 2

HY_W = D_MODEL
HY_GROUPS = 8
HY_BANDS = 16
HY_EMB = 1 + 2 * HY_BANDS
HY_HIDDEN = 64
HY_SHORT = 3
HY_FAST_DECAY = 0.3
HY_SLOW_DECAY = 1.5
HY_TARGET = 1e-2

GM_W = D_MODEL
GM_GROUPS = 8
GM_GROUP_CH = GM_W // GM_GROUPS
CHUNK = 128

EVEN_IN = 4 * HY_W + 3 * GM_W
EVEN_MIX = HY_W + GM_W

POOL_WINDOWS = (2, 4, 8, 16)
POOL_GROUPS = len(POOL_WINDOWS)
POOL_W = D_MODEL
POOL_GROUP_CH = POOL_W // POOL_GROUPS

NA_HEADS = 16
NA_HEAD_DIM = 64
NA_W = NA_HEADS * NA_HEAD_DIM
NA_KH_MAX = 8
NA_KW = 16

ODD_IN = 2 * POOL_W + 4 * NA_W
ODD_MIX = POOL_W + NA_W

kernel_name = "hybrid_hyena_gmlp_pool_natten_encoder"


def rmsnorm(x, g):
    xf = x.astype(jnp.float32)
    y = xf * lax.rsqrt(jnp.mean(xf * xf, axis=-1, keepdims=True) + EPS)
    return (y * g.astype(jnp.float32)).astype(x.dtype)


def centred_short_conv(x, w, b):
    xp = jnp.pad(x, ((0, 0), (1, 1), (0, 0)))
    return xp[:, :-2] * w[0] + xp[:, 1:-1] * w[1] + xp[:, 2:] * w[2] + b


def hyena_two_sided_filter(L, w0, b0, w1, b1, w2, b2, w_out, freq):
    f32 = jnp.float32
    t = jnp.linspace(0.0, 1.0, L, dtype=f32)[:, None]
    ang = 2.0 * math.pi * jnp.arange(L, dtype=f32)[:, None] / L
    bands = jnp.linspace(1e-4, HY_BANDS - 1, HY_BANDS, dtype=f32)[None, :]
    feats = jnp.concatenate([t, jnp.cos(bands * ang), -jnp.sin(bands * ang)], axis=-1)
    fr = freq.astype(f32)
    h = jnp.sin(fr * (feats @ w0.astype(f32) + b0.astype(f32)))
    h = jnp.sin(fr * (h @ w1.astype(f32) + b1.astype(f32)))
    h = jnp.sin(fr * (h @ w2.astype(f32) + b2.astype(f32)))
    k = (h @ w_out.astype(f32)).reshape(L, 2, HY_W)
    max_decay = math.log(HY_TARGET) / HY_FAST_DECAY
    min_decay = math.log(HY_TARGET) / HY_SLOW_DECAY
    deltas = jnp.linspace(min_decay, max_decay, HY_W, dtype=f32)
    k = k * jnp.exp(-t * jnp.abs(deltas))[:, None, :]
    k_fwd = k[:, 0]
    k_bwd = k[1:, 1][::-1]
    kc = jnp.concatenate([k_fwd, jnp.zeros((1, HY_W), f32), k_bwd], axis=0)
    return kc * lax.rsqrt(jnp.sum(kc * kc, axis=0, keepdims=True) + EPS)


def fft_long_conv(z, kc, d):
    L = z.shape[1]
    n = 2 * L
    zf = z.astype(jnp.float32)
    spec = jnp.fft.rfft(zf, n=n, axis=1) * jnp.fft.rfft(kc, n=n, axis=0)[None]
    y = jnp.fft.irfft(spec, n=n, axis=1)[:, :L]
    return (y + zf * d.astype(jnp.float32)).astype(z.dtype)


def even_mixer(hn, w_in, conv_w, conv_b, hy_w0, hy_b0, hy_w1, hy_b1, hy_w2, hy_b2,
               hy_wout, hy_freq, hy_d, gm_norm_g, gm_ws, gm_bs, w_out):
    B, L, _ = hn.shape
    proj = hn @ w_in
    hy_in, g_a, u_b, v_b, g_b = jnp.split(
        proj, [3 * HY_W, 4 * HY_W, 4 * HY_W + GM_W, 4 * HY_W + 2 * GM_W], axis=-1)
    hy_in = centred_short_conv(hy_in, conv_w, conv_b)
    x0, x1, v = jnp.split(hy_in, 3, axis=-1)
    kc = hyena_two_sided_filter(L, hy_w0, hy_b0, hy_w1, hy_b1, hy_w2, hy_b2, hy_wout, hy_freq)
    y_a = x0 * fft_long_conv(v * x1, kc, hy_d) * jax.nn.silu(g_a)
    v_c = rmsnorm(v_b, gm_norm_g).reshape(B, L // CHUNK, CHUNK, GM_GROUPS, GM_GROUP_CH)
    s = jnp.einsum('gpq,bnqgc->bnpgc', gm_ws, v_c) + gm_bs.T[None, None, :, :, None]
    y_b = u_b * s.reshape(B, L, GM_W) * jax.nn.silu(g_b)
    return jnp.concatenate([y_a, y_b], axis=-1) @ w_out


def multiscale_pool(xc):
    B, L, C = xc.shape
    xf = xc.astype(jnp.float32)
    cs = jnp.concatenate([jnp.zeros((B, 1, C), jnp.float32), jnp.cumsum(xf, axis=1)], axis=1)
    t = np.arange(L)
    outs = []
    for g, w in enumerate(POOL_WINDOWS):
        lo = np.clip(t - w // 2, 0, L)
        hi = np.clip(t + w // 2, 0, L)
        seg = cs[:, :, g * POOL_GROUP_CH:(g + 1) * POOL_GROUP_CH]
        cnt = jnp.asarray((hi - lo).astype(np.float32))[None, :, None]
        outs.append((jnp.take(seg, hi, axis=1) - jnp.take(seg, lo, axis=1)) / cnt)
    pooled = jnp.concatenate(outs, axis=-1)
    return (pooled - xf).astype(xc.dtype)


def neighbourhood_attention(q, k, v, rpb):
    B, L, H, Dh = q.shape
    rows = L // GRID_W
    kh = min(NA_KH_MAX, rows)
    kw = NA_KW
    qg = q.reshape(B, rows, GRID_W, H, Dh)
    kg = k.reshape(B, rows, GRID_W, H, Dh)
    vg = v.reshape(B, rows, GRID_W, H, Dh)
    cols = np.arange(GRID_W)
    col_start = np.clip(cols - kw // 2, 0, GRID_W - kw)
    col_idx = col_start[:, None] + np.arange(kw)[None, :]
    dc = col_idx - cols[:, None] + (NA_KW - 1)
    rpb_c = rpb[:, :, dc]
    scale = Dh ** -0.5

    def row_block(r):
        rs = jnp.clip(r - kh // 2, 0, rows - kh)
        kb = lax.dynamic_slice_in_dim(kg, rs, kh, axis=1)
        vb = lax.dynamic_slice_in_dim(vg, rs, kh, axis=1)
        k_win = kb[:, :, col_idx]
        v_win = vb[:, :, col_idx]
        q_row = lax.dynamic_index_in_dim(qg, r, axis=1, keepdims=False)
        s = jnp.einsum('bqhd,bjqkhd->bhqjk', q_row, k_win).astype(jnp.float32) * scale
        dr = rs + jnp.arange(kh) - r + (NA_KH_MAX - 1)
        bias = jnp.take(rpb_c, dr, axis=1).astype(jnp.float32)
        s = s + jnp.transpose(bias, (0, 2, 1, 3))[None]
        a = jax.nn.softmax(s.reshape(B, H, GRID_W, kh * kw), axis=-1).reshape(B, H, GRID_W, kh, kw)
        return jnp.einsum('bhqjk,bjqkhd->bqhd', a.astype(v.dtype), v_win)

    out = lax.map(row_block, jnp.arange(rows))
    return jnp.transpose(out, (1, 0, 2, 3, 4)).reshape(B, L, H * Dh)


def odd_mixer(hn, w_in, pool_w, pool_b, pool_scale, rpb, w_out):
    B, L, _ = hn.shape
    proj = hn @ w_in
    xc, g_c, q, k, v, g_d = jnp.split(
        proj, [POOL_W, 2 * POOL_W, 2 * POOL_W + NA_W, 2 * POOL_W + 2 * NA_W, 2 * POOL_W + 3 * NA_W], axis=-1)
    d = multiscale_pool(xc).reshape(B, L, POOL_GROUPS, POOL_GROUP_CH)
    y_c = jnp.einsum('bsgc,gcd->bsgd', d, pool_w).reshape(B, L, POOL_W) + pool_b
    y_c = y_c * pool_scale * jax.nn.silu(g_c)
    shp = (B, L, NA_HEADS, NA_HEAD_DIM)
    y_d = neighbourhood_attention(q.reshape(shp), k.reshape(shp), v.reshape(shp), rpb) * jax.nn.silu(g_d)
    return jnp.concatenate([y_c, y_d], axis=-1) @ w_out


def setup_inputs(seed: int = 0) -> dict:
    key = jax.random.key(seed)
    ks = jax.random.split(key, 32)
    f32 = jnp.float32

    def nrm(k, shape, scale):
        return jax.random.normal(k, shape, f32) * scale

    return {
        "x": nrm(ks[0], (BATCH, SEQ, D_MODEL), 1.0),
        "p": nrm(ks[1], (DEPTH, BATCH, SEQ, PLE_DIM), 1.0),
        "norm_g": 1.0 + nrm(ks[2], (DEPTH, D_MODEL), 0.1),
        "final_g": 1.0 + nrm(ks[3], (D_MODEL,), 0.1),
        "ev_w_in": nrm(ks[4], (N_EVEN, D_MODEL, EVEN_IN), D_MODEL ** -0.5),
        "ev_conv_w": nrm(ks[5], (N_EVEN, HY_SHORT, 3 * HY_W), HY_SHORT ** -0.5),
        "ev_conv_b": nrm(ks[6], (N_EVEN, 3 * HY_W), 0.01),
        "hy_w0": nrm(ks[7], (N_EVEN, HY_EMB, HY_HIDDEN), HY_EMB ** -0.5),
        "hy_b0": nrm(ks[8], (N_EVEN, HY_HIDDEN), 0.1),
        "hy_w1": nrm(ks[9], (N_EVEN, HY_HIDDEN, HY_HIDDEN), HY_HIDDEN ** -0.5),
        "hy_b1": nrm(ks[10], (N_EVEN, HY_HIDDEN), 0.1),
        "hy_w2": nrm(ks[11], (N_EVEN, HY_HIDDEN, HY_HIDDEN), HY_HIDDEN ** -0.5),
        "hy_b2": nrm(ks[12], (N_EVEN, HY_HIDDEN), 0.1),
        "hy_wout": nrm(ks[13], (N_EVEN, HY_HIDDEN, 2 * HY_W), HY_HIDDEN ** -0.5),
        "hy_freq": 1.0 + nrm(ks[14], (N_EVEN, HY_HIDDEN), 0.1),
        "hy_d": nrm(ks[15], (N_EVEN, HY_W), 1.0),
        "gm_norm_g": 1.0 + nrm(ks[16], (N_EVEN, GM_W), 0.1),
        "gm_ws": nrm(ks[17], (N_EVEN, GM_GROUPS, CHUNK, CHUNK), CHUNK ** -0.5),
        "gm_bs": 1.0 + nrm(ks[18], (N_EVEN, GM_GROUPS, CHUNK), 0.1),
        "ev_w_out": nrm(ks[19], (N_EVEN, EVEN_MIX, D_MODEL), EVEN_MIX ** -0.5),
        "od_w_in": nrm(ks[20], (N_ODD, D_MODEL, ODD_IN), D_MODEL ** -0.5),
        "pool_w": nrm(ks[21], (N_ODD, POOL_GROUPS, POOL_GROUP_CH, POOL_GROUP_CH), POOL_GROUP_CH ** -0.5),
        "pool_b": nrm(ks[22], (N_ODD, POOL_W), 0.01),
        "pool_scale": 1.0 + nrm(ks[23], (N_ODD, POOL_W), 0.1),
        "na_rpb": nrm(ks[24], (N_ODD, NA_HEADS, 2 * NA_KH_MAX - 1, 2 * NA_KW - 1), 0.1),
        "od_w_out": nrm(ks[25], (N_ODD, ODD_MIX, D_MODEL), ODD_MIX ** -0.5),
        "ple_up": nrm(ks[26], (DEPTH, PLE_DIM, D_MODEL), PLE_DIM ** -0.5),
        "ple_gate_w": nrm(ks[27], (DEPTH, D_MODEL, D_MODEL), D_MODEL ** -0.5),
        "ple_g": 1.0 + nrm(ks[28], (DEPTH, D_MODEL), 0.1),
    }


def reference(x, p, norm_g, final_g, ev_w_in, ev_conv_w, ev_conv_b, hy_w0, hy_b0, hy_w1, hy_b1,
              hy_w2, hy_b2, hy_wout, hy_freq, hy_d, gm_norm_g, gm_ws, gm_bs, ev_w_out,
              od_w_in, pool_w, pool_b, pool_scale, na_rpb, od_w_out, ple_up, ple_gate_w, ple_g):
    h = x
    for i in range(DEPTH):
        j = i // 2
        hn = rmsnorm(h, norm_g[i])
        if i % 2 == 0:
            mix = even_mixer(hn, ev_w_in[j], ev_conv_w[j], ev_conv_b[j], hy_w0[j], hy_b0[j],
                             hy_w1[j], hy_b1[j], hy_w2[j], hy_b2[j], hy_wout[j], hy_freq[j],
                             hy_d[j], gm_norm_g[j], gm_ws[j], gm_bs[j], ev_w_out[j])
        else:
            mix = odd_mixer(hn, od_w_in[j], pool_w[j], pool_b[j], pool_scale[j], na_rpb[j], od_w_out[j])
        h = h + mix
        gate = jax.nn.sigmoid(rmsnorm(h, ple_g[i]) @ ple_gate_w[i])
        h = h + (p[i] @ ple_up[i]) * gate
    return rmsnorm(h, final_g)
```

```python
import contextlib
import numpy as np
import ml_dtypes
import concourse.bass as bass
import concourse.mybir as mybir
from concourse.bass_utils import run_bass_kernel_spmd

F32 = mybir.dt.float32
BF16 = mybir.dt.bfloat16
AF = mybir.ActivationFunctionType
ALU = mybir.AluOpType

L = 4096
D = 1024
NT = 32
NF = 8192
EPS = 1e-6
TWO_PI = 2.0 * np.pi
MAGIC = 1.5 * 2 ** 23
NEG = -30000.0


class Trk:
    def __init__(self, nc, es):
        self.nc = nc
        self.eng = {"pe": nc.tensor, "act": nc.scalar, "dve": nc.vector, "pool": nc.gpsimd, "sp": nc.sync}
        self.esem = {e: [es.enter_context(nc.semaphore("E_" + e)), 0] for e in self.eng}
        self.bar = [es.enter_context(nc.semaphore("BAR")), 0]
        self.known = {e: {} for e in self.eng}
        self.lastw = {}
        self.readers = {}
        self.dsem = {}
        self.es = es
        self.sems = {}
        self.same_engine_sync = True

    def _wait(self, e, deps):
        own = self.esem[e][0]
        for sem, val in deps:
            if sem is own and (e in ("pe", "sp") or not self.same_engine_sync):
                continue
            k = id(sem)
            if self.known[e].get(k, 0) < val:
                self.eng[e].wait_ge(sem, val)
                self.known[e][k] = val

    def _deps(self, reads, writes, own=None):
        d = []
        for k in reads:
            if k in self.lastw:
                d.append(self.lastw[k])
        for k in writes:
            if k in self.lastw and self.lastw[k][0] is not own:
                d.append(self.lastw[k])
            d.extend(self.readers.get(k, {}).values())
        return d

    def _commit(self, ev, reads, writes):
        for k in reads:
            r = self.readers.setdefault(k, {})
            if r.get(id(ev[0]), (None, 0))[1] < ev[1]:
                r[id(ev[0])] = ev
        for k in writes:
            self.lastw[k] = ev
            self.readers[k] = {}

    def op(self, e, fn, reads=(), writes=()):
        self._wait(e, self._deps(reads, writes))
        ins = fn()
        s = self.esem[e]
        s[1] += 1
        ins.then_inc(s[0], 1)
        self._commit((s[0], s[1]), reads, writes)

    def dma(self, q, out, in_, reads=(), writes=(), semkey=None, **kw):
        if semkey not in self.dsem:
            self.dsem[semkey] = [self.es.enter_context(self.nc.semaphore("D%d" % len(self.dsem))), 0]
        ds = self.dsem[semkey]
        self._wait(q, self._deps(reads, writes, own=ds[0]))
        ins = self.eng[q].dma_start(out=out, in_=in_, **kw)
        ds[1] += 16
        ins.then_inc(ds[0], 16)
        self._commit((ds[0], ds[1]), reads, writes)

    def barrier(self):
        evs = [(s[0], s[1]) for e, s in self.esem.items() if s[1] > 0 and e != "sp"]
        evs += [(s[0], s[1]) for s in self.dsem.values() if s[1] > 0]
        self._wait("sp", evs)
        self.bar[1] += 1
        self.nc.sync.sem_inc(self.bar[0], 1)
        for e in self.eng:
            if e != "sp":
                self.eng[e].wait_ge(self.bar[0], self.bar[1])
                for sem, val in evs:
                    self.known[e][id(sem)] = max(self.known[e].get(id(sem), 0), val)
        self.lastw = {}
        self.readers = {}


def _filter_consts():
    f32 = np.float32
    t = np.linspace(0.0, 1.0, L, dtype=f32)[:, None]
    ang = (f32(2.0 * np.pi) * np.arange(L, dtype=f32)[:, None] / f32(L)).astype(f32)
    bands = np.linspace(1e-4, 15, 16, dtype=f32)[None, :]
    feats = np.concatenate([t, np.cos(bands * ang), -np.sin(bands * ang)], axis=-1).astype(f32)
    featsT = np.ascontiguousarray(feats.T)
    tneg = np.ascontiguousarray((-t[:, 0]).reshape(NT, 128).T)
    max_decay = np.log(1e-2) / 0.3
    min_decay = np.log(1e-2) / 1.5
    absd = np.abs(np.linspace(min_decay, max_decay, D, dtype=f32)).reshape(1, D).astype(f32)
    return featsT, tneg.astype(f32), absd


def _dft_tables():
    th = 2.0 * np.pi / NF
    u = np.arange(2048)
    sv = np.arange(2048)
    fe, fo = 2 * u, 2 * u + 1

    def co(f, kind, w=None):
        ang = ((f[:, None] * sv[None, :]) % NF) * th
        m = np.cos(ang) if kind == "c" else -np.sin(ang)
        return m if w is None else m * w[:, None]
    Fm = np.concatenate([co(fe, "c"), co(fo, "c"), co(fe, "s"), co(fo, "s")], 0)
    Ftab = np.ascontiguousarray(Fm.reshape(64, 128, 16, 128).transpose(0, 3, 2, 1)).astype(ml_dtypes.bfloat16).reshape(64, 128, 2048)
    w_e = np.where(fe == 0, 1.0 / NF, 2.0 / NF)
    w_o = np.full(2048, 2.0 / NF)
    Gm = np.concatenate([co(fe, "c", w_e), co(fo, "s", w_o), co(fo, "c", w_o), co(fe, "s", w_e)], 0)
    Gtab = np.ascontiguousarray(Gm.reshape(8, 8, 128, 4, 512).transpose(3, 0, 2, 1, 4)).astype(ml_dtypes.bfloat16).reshape(4, 8, 128, 4096)
    gcol = np.concatenate([w_e * np.cos(np.pi * fe / 2.0), w_o * (-np.sin(np.pi * fo / 2.0))])
    gcol = np.ascontiguousarray(gcol.reshape(32, 128).T).astype(ml_dtypes.bfloat16)
    sgn = np.where(np.arange(128) % 2 == 0, 1.0, -1.0)
    small = np.zeros((128, 2048 + 512), np.float32)
    small[0, 0:128] = sgn
    small[0, 128:256] = -sgn
    small[0, 256] = 1.0
    small[0, 257] = 1.0 / NF
    small[:, 258] = sgn
    small[0, 512:512 + 2048] = (1.0 / NF) * np.where(np.arange(2048) % 2 == 0, 1.0, -1.0)
    E1 = np.zeros((128, 128), np.float32)
    for q in range(1, 128):
        E1[128 - q, q] = 1.0
    E2 = np.zeros((128, 128), np.float32)
    E2[0, 0] = 1.0
    ex = np.concatenate([E1, E2], 1)
    return Ftab, Gtab, gcol, small.astype(ml_dtypes.bfloat16), ex.astype(ml_dtypes.bfloat16)


def _natten_consts():
    cols = np.arange(64)
    cs = np.clip(cols - 8, 0, 48)
    kc = np.arange(64)
    inwin = (kc[None, :] >= cs[:, None]) & (kc[None, :] < cs[:, None] + 16)
    mask = np.where(inwin, 0.0, NEG).astype(np.float32)
    m = np.tile(mask[None, :, None, None, :], (2, 1, 14, 2, 1)).reshape(128, 14 * 128)
    idx = np.clip(15 + kc[None, :] - cols[:, None], 0, 30)
    return m.astype(np.float32), idx


def _pool_consts():
    t = np.arange(L)
    rows = []
    for w in (2, 4, 8, 16):
        lo = np.clip(t - w // 2, 0, L)
        hi = np.clip(t + w // 2, 0, L)
        rows.append(1.0 / (hi - lo).astype(np.float32))
    return np.stack(rows).astype(np.float32)


class Builder:
    def __init__(self, nlayers=4, debug=False):
        self.nlayers = nlayers
        self.debug = debug
        self.nc = bass.Bass("TRN2", target_bir_lowering=False)
        self.es = contextlib.ExitStack()
        self.T = Trk(self.nc, self.es)
        self.inp = {}

    def din(self, name, shape, dt=F32):
        ap = self.nc.dram_tensor(name, list(shape), dt, kind="ExternalInput").ap()
        self.inp[name] = ap
        return ap

    def dscr(self, name, shape, dt):
        return self.nc.dram_tensor(name, list(shape), dt, kind="Internal").ap()

    def sb(self, st, name, shape, dt=F32):
        self._uid = getattr(self, "_uid", 0) + 1
        return st.enter_context(self.nc.sbuf_tensor("%s_%d" % (name, self._uid), list(shape), dt))

    def ps(self, st, name, shape=(128, 512), dt=F32):
        self._uid = getattr(self, "_uid", 0) + 1
        return st.enter_context(self.nc.psum_tensor("%s_%d" % (name, self._uid), list(shape), dt))

    def declare(self):
        d = self.din
        self.x = d("x", [L, D])
        self.p = d("p", [4, L, 256])
        self.norm_g = d("norm_g", [4, D])
        self.final_g = d("final_g", [1, D])
        self.ple_g = d("ple_g", [4, D])
        self.ple_up = d("ple_up", [4, 256, D])
        self.ple_gate_w = d("ple_gate_w", [4, D, D])
        self.ev_w_in = d("ev_w_in", [2, D, 7168])
        self.ev_w_out = d("ev_w_out", [2, 2048, D])
        self.od_w_in = d("od_w_in", [2, D, 6144])
        self.od_w_out = d("od_w_out", [2, 2048, D])
        self.convw = d("convw", [2, 128, 72])
        self.convb = d("convb", [2, 128, 24])
        self.hy_w0 = d("hy_w0", [2, 33, 64])
        self.hy_w1 = d("hy_w1", [2, 64, 64])
        self.hy_w2 = d("hy_w2", [2, 64, 64])
        self.hy_cols = d("hy_cols", [2, 64, 4])
        self.hy_wout = d("hy_wout", [2, 64, 2048])
        self.hy_d = d("hy_d", [2, D])
        self.gm_norm_g = d("gm_norm_g", [2, D])
        self.gm_wsT = d("gm_wsT", [2, 128, 8 * 128])
        self.gm_bs = d("gm_bs", [2, 8, 128])
        self.pool_w = d("pool_w", [2, 4, 256, 256])
        self.poolcols = d("poolcols", [2, 128, 16])
        self.rpbT = d("rpbT", [2, 8, 128, 14 * 128])
        self.c_ident = d("c_ident", [128, 128])
        self.c_featsT = d("c_featsT", [33, L])
        self.c_tneg = d("c_tneg", [128, NT])
        self.c_absd = d("c_absd", [1, D])
        self.c_F = d("c_F", [64, 128, 2048], BF16)
        self.c_G = d("c_G", [4, 8, 128, 4096], BF16)
        self.c_gcol = d("c_gcol", [128, 32], BF16)
        self.c_small = d("c_small", [128, 2560], BF16)
        self.c_ex = d("c_ex", [128, 256], BF16)
        self.c_mask = d("c_mask", [128, 14 * 128])
        self.c_invcnt = d("c_invcnt", [4, L])
        if self.debug:
            self.out = self.nc.dram_tensor("out", [L, D], F32, kind="ExternalOutput").ap()
        else:
            self.out = self.nc.dram_tensor("out", [L, D], F32, kind="ExternalOutput").ap()
        self.hbuf = self.dscr("hbuf", [L, D], F32)
        self.YT = self.dscr("YT", [2048, L], BF16)
        self.Abuf = self.dscr("Abuf", [D, L], F32)
        self.zbuf = self.dscr("zbuf", [D, L], BF16)
        self.kebuf = self.dscr("kebuf", [L, D], BF16)
        self.kobuf = self.dscr("kobuf", [L, D], BF16)
        self.Hre = self.dscr("Hre", [2, 32, 128, 512], F32)
        self.Him = self.dscr("Him", [2, 32, 128, 512], F32)
        self.Hny = self.dscr("Hny", [2, 1, 512], F32)

    def rms_rstd(self, st_key, src, junk, ss, sd, rstd):
        T, nc = self.T, self.nc
        T.op("act", lambda: nc.scalar.activation(out=junk, in_=src, func=AF.Square, accum_out=ss),
             reads=[st_key], writes=[st_key + "_junk", st_key + "_ss"])
        T.op("act", lambda: nc.scalar.activation(out=sd, in_=ss, func=AF.Sqrt, bias=self.eps_col[:, 0:1], scale=1.0 / D),
             reads=[st_key + "_ss", "eps"], writes=[st_key + "_sd"])
        T.op("dve", lambda: nc.vector.reciprocal(out=rstd, in_=sd), reads=[st_key + "_sd"], writes=[st_key + "_rstd"])

    def load_row_bcast(self, dst, src_row, key):
        self.T.dma("sp", dst, src_row.partition_broadcast(128), writes=[key], semkey=key)

    def transposes_f32(self, src, src_key, psT, ps_keys, nblk):
        T, nc = self.T, self.nc
        for h0 in range(0, nblk, 4):
            nb = min(4, nblk - h0)
            bank = psT[h0 // 4]

            def f(h0=h0, nb=nb, bank=bank):
                ins = None
                for q in range(nb):
                    ins = nc.tensor.transpose(out=bank[:, q * 128:(q + 1) * 128],
                                              in_=src[:, (h0 + q) * 128:(h0 + q + 1) * 128],
                                              identity=self.ident[:])
                return ins
            T.op("pe", f, reads=[src_key, "ident"], writes=[ps_keys[h0 // 4]])

    def load_globals(self):
        es, T, nc = self.es, self.T, self.nc
        self.ident = self.sb(es, "ident", [128, 128], F32)
        self.ident_bf = self.sb(es, "ident_bf", [128, 128], BF16)
        self.ones_bf = self.sb(es, "ones_bf", [128, 64], BF16)
        self.ones_f = self.sb(es, "ones_f", [128, 128], F32)
        self.eps_col = self.sb(es, "eps_col", [128, 1], F32)
        T.dma("sp", self.ident[:], self.c_ident[:, :], writes=["ident"], semkey="ident")
        T.dma("pool", self.ident_bf[:], self.c_ident[:, :], writes=["ident_bf"], semkey="ident_bf")
        T.op("dve", lambda: nc.vector.memset(self.ones_bf[:], 1.0), writes=["ones_bf"])
        T.op("dve", lambda: nc.vector.memset(self.ones_f[:], 1.0), writes=["ones_f"])
        T.op("dve", lambda: nc.vector.memset(self.eps_col[:], EPS), writes=["eps"])

    def head(self, li):
        T, nc = self.T, self.nc
        src = self.x if li == 0 else self.hbuf
        with contextlib.ExitStack() as st:
            grow = self.sb(st, "hd_grow", [128, D])
            self.load_row_bcast(grow[:], self.norm_g[li, :], "hd_grow")
            sets = []
            for par in range(2):
                sets.append(dict(
                    hin=self.sb(st, f"hd_hin{par}", [128, D]),
                    xh=self.sb(st, f"hd_xh{par}", [128, D]),
                    junk=self.sb(st, f"hd_junk{par}", [128, D], BF16),
                    ss=self.sb(st, f"hd_ss{par}", [128, 1]),
                    sd=self.sb(st, f"hd_sd{par}", [128, 1]),
                    rstd=self.sb(st, f"hd_rstd{par}", [128, 1]),
                    psT=[self.ps(st, f"hd_psT{par}_{i}") for i in range(2)],
                ))
            def h1(t):
                par = t % 2
                S = sets[par]
                k = f"hd{par}"
                T.dma("sp", S["hin"][:], src[t * 128:(t + 1) * 128, :], reads=[("h", t)], writes=[k], semkey=k)
                self.rms_rstd(k, S["hin"][:], S["junk"][:], S["ss"][:], S["sd"][:], S["rstd"][:])
                T.op("dve", lambda: nc.vector.scalar_tensor_tensor(
                    out=S["xh"][:], in0=S["hin"][:], scalar=S["rstd"][:, 0:1], in1=grow[:],
                    op0=ALU.mult, op1=ALU.mult), reads=[k, k + "_rstd", "hd_grow"], writes=[k + "_xh"])

            h1(0)
            for t in range(NT):
                if t + 1 < NT:
                    h1(t + 1)
                par = t % 2
                S = sets[par]
                k = f"hd{par}"
                self.transposes_f32(S["xh"], k + "_xh", S["psT"], [k + "_ps0", k + "_ps1"], 8)
                T.op("act", lambda: nc.scalar.copy(
                    out=self.hnT[:, 0:4, t * 128:(t + 1) * 128],
                    in_=S["psT"][0][:].rearrange("p (a b) -> p a b", a=4)),
                    reads=[k + "_ps0"], writes=[("hnT", t // 4)])
                T.op("dve", lambda: nc.vector.tensor_copy(
                    out=self.hnT[:, 4:8, t * 128:(t + 1) * 128],
                    in_=S["psT"][1][:].rearrange("p (a b) -> p a b", a=4)),
                    reads=[k + "_ps1"], writes=[("hnT", t // 4)])
        T.barrier()

    def load_w(self, dst, w2d, r0, nk, c0, ncol, key):
        src = w2d[r0:r0 + nk * 128, c0:c0 + ncol].rearrange("(k p) c -> p k c", p=128)
        self.T.dma("pool", dst, src, writes=[key], semkey=key)

    def proj_fm(self, ps_ap, wblk, wkey, tg, pskey, col0=0):
        T, nc = self.T, self.nc

        def f():
            ins = None
            for k in range(8):
                ins = nc.tensor.matmul(ps_ap, lhsT=wblk[:, k, col0:col0 + 128],
                                       rhs=self.hnT[:, k, tg * 512:(tg + 1) * 512],
                                       start=(k == 0), stop=(k == 7))
            return ins
        T.op("pe", f, reads=[wkey, ("hnT", tg)], writes=[pskey])

    def tail(self, li, w_out2d, last):
        T, nc = self.T, self.nc
        src = self.x if li == 0 else self.hbuf
        with contextlib.ExitStack() as st:
            wo = self.sb(st, "tl_wo", [128, 16, D], BF16)
            gw = self.sb(st, "tl_gw", [128, 8, D], BF16)
            uw = self.sb(st, "tl_uw", [128, 2, D], BF16)
            for kk in range(0, 16, 4):
                self.load_w(wo[:, kk:kk + 4, :], w_out2d, kk * 128, 4, 0, D, f"tl_wo{kk}")
            for kk in range(0, 8, 4):
                self.load_w(gw[:, kk:kk + 4, :], self.ple_gate_w[li], kk * 128, 4, 0, D, f"tl_gw{kk}")
            self.load_w(uw[:], self.ple_up[li], 0, 2, 0, D, "tl_uw")
            wkeys = [f"tl_wo{kk}" for kk in range(0, 16, 4)]
            gkeys = [f"tl_gw{kk}" for kk in range(0, 8, 4)]
            pgrow = self.sb(st, "tl_pgrow", [128, D])
            self.load_row_bcast(pgrow[:], self.ple_g[li, :], "tl_pgrow")
            nrow = self.sb(st, "tl_nrow", [128, D])
            self.load_row_bcast(nrow[:], self.final_g[0, :] if last else self.norm_g[li + 1, :], "tl_nrow")
            Yp = [self.sb(st, f"tl_Yp{i}", [128, 16, 256], BF16) for i in range(2)]
            sets = []
            for par in range(2):
                sets.append(dict(
                    hc=self.sb(st, f"tl_hc{par}", [128, D]),
                    xh=self.sb(st, f"tl_xh{par}", [128, D]),
                    xn=self.sb(st, f"tl_xn{par}", [128, D]),
                    sig=self.sb(st, f"tl_sig{par}", [128, D]),
                    hn2=self.sb(st, f"tl_hn2{par}", [128, 8, 128], BF16),
                    pin=self.sb(st, f"tl_pin{par}", [128, 256]),
                    pT=self.sb(st, f"tl_pT{par}", [128, 2, 128], BF16),
                    ss=self.sb(st, f"tl_ss{par}", [128, 1]),
                    sd=self.sb(st, f"tl_sd{par}", [128, 1]),
                    rstd=self.sb(st, f"tl_rstd{par}", [128, 1]),
                    ss2=self.sb(st, f"tl_ss2{par}", [128, 1]),
                    sd2=self.sb(st, f"tl_sd2{par}", [128, 1]),
                    rstd2=self.sb(st, f"tl_rstd2{par}", [128, 1]),
                ))
            psM = [self.ps(st, f"tl_psM{i}") for i in range(2)]
            psT = [self.ps(st, f"tl_psT{i}") for i in range(2)]
            psG = [self.ps(st, f"tl_psG{i}") for i in range(2)]
            psU = [self.ps(st, f"tl_psU{i}") for i in range(2)]

            def load_Y(pc):
                yp = Yp[pc % 2]
                srcy = self.YT[:, pc * 256:(pc + 1) * 256].rearrange("(k p) t -> p k t", p=128)
                T.dma("sp", yp[:], srcy, reads=[("YT", pc // 2)], writes=[f"tl_Yp{pc % 2}"], semkey=f"tl_Yp{pc % 2}")

            def norm(S, k, src_key, junk, junk_key, ss, sd, rstd, grow, grow_key, dst, dst_key, sfx):
                T.op("act", lambda: nc.scalar.activation(out=junk[:], in_=S["hc"][:], func=AF.Square, accum_out=ss[:]),
                     reads=[src_key], writes=[junk_key, k + "_ss" + sfx])
                T.op("act", lambda: nc.scalar.activation(out=sd[:], in_=ss[:], func=AF.Sqrt, bias=self.eps_col[:, 0:1], scale=1.0 / D),
                     reads=[k + "_ss" + sfx, "eps"], writes=[k + "_sd" + sfx])
                T.op("dve", lambda: nc.vector.reciprocal(out=rstd[:], in_=sd[:]), reads=[k + "_sd" + sfx], writes=[k + "_rstd" + sfx])
                T.op("dve", lambda: nc.vector.scalar_tensor_tensor(out=dst[:], in0=S["hc"][:], scalar=rstd[:, 0:1], in1=grow[:],
                                                                   op0=ALU.mult, op1=ALU.mult),
                     reads=[src_key, k + "_rstd" + sfx, grow_key], writes=[dst_key])

            def stA(t):
                pc, tt = t // 2, t % 2
                if tt == 0 and pc + 1 < 16:
                    load_Y(pc + 1)
                par = t % 2
                S = sets[par]
                k = f"tl{par}"
                yp = Yp[pc % 2]
                ypk = f"tl_Yp{pc % 2}"
                T.dma("sp", S["hc"][:], src[t * 128:(t + 1) * 128, :], reads=[("h", t)], writes=[k + "_hc"], semkey=k + "_hc")
                T.dma("sp", S["pin"][:], self.p[li, t * 128:(t + 1) * 128, :], writes=[k + "_pin"], semkey=k + "_pin")
                for half in range(2):
                    def f(half=half):
                        ins = None
                        for kk in range(16):
                            ins = nc.tensor.matmul(psM[half][:], lhsT=yp[:, kk, tt * 128:(tt + 1) * 128],
                                                   rhs=wo[:, kk, half * 512:(half + 1) * 512],
                                                   start=(kk == 0), stop=(kk == 15))
                        return ins
                    T.op("pe", f, reads=[ypk] + wkeys, writes=[f"tl_psM{half}"])
                    T.op("dve", lambda half=half: nc.vector.tensor_tensor(
                        out=S["hc"][:, half * 512:(half + 1) * 512], in0=psM[half][:],
                        in1=S["hc"][:, half * 512:(half + 1) * 512], op=ALU.add),
                        reads=[f"tl_psM{half}", k + "_hc"], writes=[k + "_hc"])
                norm(S, k, k + "_hc", S["sig"], k + "_sig", S["ss"], S["sd"], S["rstd"], pgrow, "tl_pgrow", S["xh"], k + "_xh", "")

            def stB(t):
                par = t % 2
                S = sets[par]
                k = f"tl{par}"
                for half in range(2):
                    def f(half=half):
                        ins = None
                        for kk in range(8):
                            ins = nc.tensor.matmul(psG[half][:], lhsT=S["hn2"][:, kk, :],
                                                   rhs=gw[:, kk, half * 512:(half + 1) * 512],
                                                   start=(kk == 0), stop=(kk == 7))
                        return ins
                    T.op("pe", f, reads=[k + "_hn2"] + gkeys, writes=[f"tl_psG{half}"])
                    T.op("act", lambda half=half: nc.scalar.activation(
                        out=S["sig"][:, half * 512:(half + 1) * 512], in_=psG[half][:], func=AF.Sigmoid),
                        reads=[f"tl_psG{half}"], writes=[k + "_sig"])

            def stC(t):
                par = t % 2
                S = sets[par]
                k = f"tl{par}"
                self.transposes_f32(S["xh"], k + "_xh", psT, ["tl_psT0", "tl_psT1"], 8)
                T.op("act", lambda: nc.scalar.copy(out=S["hn2"][:, 0:4, :],
                                                   in_=psT[0][:].rearrange("p (a b) -> p a b", a=4)),
                     reads=["tl_psT0"], writes=[k + "_hn2"])
                T.op("dve", lambda: nc.vector.tensor_copy(out=S["hn2"][:, 4:8, :],
                                                          in_=psT[1][:].rearrange("p (a b) -> p a b", a=4)),
                     reads=["tl_psT1"], writes=[k + "_hn2"])

                def fp():
                    ins = None
                    for q in range(2):
                        ins = nc.tensor.transpose(out=psM[0][:, q * 128:(q + 1) * 128],
                                                  in_=S["pin"][:, q * 128:(q + 1) * 128], identity=self.ident[:])
                    return ins
                T.op("pe", fp, reads=[k + "_pin", "ident"], writes=["tl_psM0"])
                T.op("act", lambda: nc.scalar.copy(out=S["pT"][:],
                                                   in_=psM[0][:, 0:256].rearrange("p (a b) -> p a b", a=2)),
                     reads=["tl_psM0"], writes=[k + "_pT"])

            def stD(t):
                par = t % 2
                S = sets[par]
                k = f"tl{par}"
                for half in range(2):
                    def f(half=half):
                        ins = None
                        for kk in range(2):
                            ins = nc.tensor.matmul(psU[half][:], lhsT=S["pT"][:, kk, :],
                                                   rhs=uw[:, kk, half * 512:(half + 1) * 512],
                                                   start=(kk == 0), stop=(kk == 1))
                        return ins
                    T.op("pe", f, reads=[k + "_pT", "tl_uw"], writes=[f"tl_psU{half}"])
                    T.op("dve", lambda half=half: nc.vector.tensor_tensor(
                        out=S["sig"][:, half * 512:(half + 1) * 512], in0=psU[half][:],
                        in1=S["sig"][:, half * 512:(half + 1) * 512], op=ALU.mult),
                        reads=[f"tl_psU{half}", k + "_sig"], writes=[k + "_sig"])
                T.op("pool", lambda: nc.gpsimd.tensor_tensor(out=S["hc"][:], in0=S["hc"][:], in1=S["sig"][:], op=ALU.add),
                     reads=[k + "_hc", k + "_sig"], writes=[k + "_hc"])
                if not last:
                    T.dma("pool", self.hbuf[t * 128:(t + 1) * 128, :], S["hc"][:], reads=[k + "_hc"], writes=[("h", t)],
                          semkey=k + "_hcst")
                norm(S, k, k + "_hc", S["xn"], k + "_xn", S["ss2"], S["sd2"], S["rstd2"], nrow, "tl_nrow", S["xn"], k + "_xn", "2")
                if last:
                    T.dma("pool", self.out[t * 128:(t + 1) * 128, :], S["xn"][:], reads=[k + "_xn"], writes=[("out", t)],
                          semkey=k + "_xnst")

            def stE(t):
                par = t % 2
                S = sets[par]
                k = f"tl{par}"
                self.transposes_f32(S["xn"], k + "_xn", psT, ["tl_psT0", "tl_psT1"], 8)
                T.op("act", lambda: nc.scalar.copy(out=self.hnT[:, 0:4, t * 128:(t + 1) * 128],
                                                   in_=psT[0][:].rearrange("p (a b) -> p a b", a=4)),
                     reads=["tl_psT0"], writes=[("hnT", t // 4)])
                T.op("dve", lambda: nc.vector.tensor_copy(out=self.hnT[:, 4:8, t * 128:(t + 1) * 128],
                                                          in_=psT[1][:].rearrange("p (a b) -> p a b", a=4)),
                     reads=["tl_psT1"], writes=[("hnT", t // 4)])

            load_Y(0)
            stA(0)
            stC(0)
            for t in range(NT):
                if t + 1 < NT:
                    stA(t + 1)
                if t >= 1 and not last:
                    stE(t - 1)
                stB(t)
                if t + 1 < NT:
                    stC(t + 1)
                stD(t)
            if not last:
                stE(NT - 1)
        T.barrier()

    def even_pre(self, j):
        T, nc = self.T, self.nc
        w2d = self.ev_w_in[j]
        with contextlib.ExitStack() as st:
            cw = self.sb(st, "ep_cw", [128, 72])
            cb = self.sb(st, "ep_cb", [128, 24])
            T.dma("sp", cw[:], self.convw[j], writes=["ep_cw"], semkey="ep_cw")
            T.dma("sp", cb[:], self.convb[j], writes=["ep_cb"], semkey="ep_cb")
            wb = [self.sb(st, f"ep_wb{i}", [128, 8, 512], BF16) for i in range(2)]
            raws = [self.sb(st, f"ep_raw{i}", [128, L + 2]) for i in range(2)]
            rawi = [0]
            bA = self.sb(st, "ep_bA", [128, L])
            bB = self.sb(st, "ep_bB", [128, L])
            zb = self.sb(st, "ep_zb", [128, L], BF16)
            pss = [self.ps(st, f"ep_ps{i}") for i in range(4)]
            for i in range(2):
                T.op("pool", lambda i=i: nc.gpsimd.memset(raws[i][:, 0:1], 0.0), writes=[f"ep_raw{i}"])
                T.op("pool", lambda i=i: nc.gpsimd.memset(raws[i][:, L + 1:L + 2], 0.0), writes=[f"ep_raw{i}"])

            def load_wb(cc):
                for q in range(4):
                    self.load_w(wb[cc % 2][:, :, q * 128:(q + 1) * 128], w2d, 0, 8, q * 1024 + cc * 128, 128,
                                f"ep_wb{cc % 2}")

            def proj_to_raw(cc, q):
                rawi[0] ^= 1
                raw, rk = raws[rawi[0]], f"ep_raw{rawi[0]}"
                for tg in range(8):
                    pk = f"ep_ps{tg % 4}"
                    self.proj_fm(pss[tg % 4][:], wb[cc % 2], f"ep_wb{cc % 2}", tg, pk, col0=q * 128)
                    T.op("act", lambda tg=tg: nc.scalar.copy(out=raw[:, 1 + tg * 512:1 + (tg + 1) * 512], in_=pss[tg % 4][:]),
                         reads=[pk], writes=[rk])

            def conv(cc, q, dst, dkey):
                raw, rk = raws[rawi[0]], f"ep_raw{rawi[0]}"
                ch = q * 8 + cc
                w = lambda tap: cw[:, tap * 24 + ch:tap * 24 + ch + 1]
                T.op("dve", lambda: nc.vector.tensor_scalar(out=dst[:], in0=raw[:, 1:L + 1], scalar1=w(1), scalar2=cb[:, ch:ch + 1],
                                                            op0=ALU.mult, op1=ALU.add),
                     reads=[rk, "ep_cw", "ep_cb"], writes=[dkey])
                T.op("dve", lambda: nc.vector.scalar_tensor_tensor(out=dst[:], in0=raw[:, 0:L], scalar=w(0), in1=dst[:],
                                                                   op0=ALU.mult, op1=ALU.add),
                     reads=[rk, dkey], writes=[dkey])
                T.op("dve", lambda: nc.vector.scalar_tensor_tensor(out=dst[:], in0=raw[:, 2:L + 2], scalar=w(2), in1=dst[:],
                                                                   op0=ALU.mult, op1=ALU.add),
                     reads=[rk, dkey], writes=[dkey])

            load_wb(0)
            for cc in range(8):
                if cc + 1 < 8:
                    load_wb(cc + 1)
                proj_to_raw(cc, 1)
                conv(cc, 1, bA, "ep_bA")
                proj_to_raw(cc, 2)
                conv(cc, 2, bB, "ep_bB")
                T.op("pool", lambda: nc.gpsimd.tensor_tensor(out=zb[:], in0=bB[:], in1=bA[:], op=ALU.mult),
                     reads=["ep_bA", "ep_bB"], writes=["ep_zb"])
                T.dma("sp", self.zbuf[cc * 128:(cc + 1) * 128, :], zb[:], reads=["ep_zb"], writes=[("z", cc)], semkey="ep_zb")
                proj_to_raw(cc, 0)
                conv(cc, 0, bA, "ep_bA")
                for tg in range(8):
                    pk = f"ep_ps{tg % 4}"
                    self.proj_fm(pss[tg % 4][:], wb[cc % 2], f"ep_wb{cc % 2}", tg, pk, col0=3 * 128)
                    T.op("act", lambda tg=tg: nc.scalar.activation(out=bB[:, tg * 512:(tg + 1) * 512], in_=pss[tg % 4][:], func=AF.Silu),
                         reads=[pk], writes=["ep_bB"])
                T.op("pool", lambda: nc.gpsimd.tensor_tensor(out=bA[:], in0=bA[:], in1=bB[:], op=ALU.mult),
                     reads=["ep_bA", "ep_bB"], writes=["ep_bA"])
                T.dma("sp", self.Abuf[cc * 128:(cc + 1) * 128, :], bA[:], reads=["ep_bA"], writes=[("A", cc)], semkey="ep_bA")
        T.barrier()

    def even_gmlp(self, j):
        T, nc = self.T, self.nc
        w2d = self.ev_w_in[j]
        with contextlib.ExitStack() as st:
            wu = self.sb(st, "eg_wu", [128, 8, D], BF16)
            wv = self.sb(st, "eg_wv", [128, 8, D], BF16)
            wg = self.sb(st, "eg_wg", [128, 8, D], BF16)
            for kk in range(0, 8, 4):
                self.load_w(wv[:, kk:kk + 4, :], w2d, kk * 128, 4, 5120, D, f"eg_wv{kk}")
                self.load_w(wu[:, kk:kk + 4, :], w2d, kk * 128, 4, 4096, D, f"eg_wu{kk}")
                self.load_w(wg[:, kk:kk + 4, :], w2d, kk * 128, 4, 6144, D, f"eg_wg{kk}")
            wsT = self.sb(st, "eg_wsT", [128, 8 * 128], BF16)
            T.dma("pool", wsT[:], self.gm_wsT[j], writes=["eg_wsT"], semkey="eg_wsT")
            bsrow = self.sb(st, "eg_bsrow", [128, 8, 4, 128])
            for g in range(8):
                for n in range(4):
                    T.dma("sp", bsrow[:, g, n, :], self.gm_bs[j, g, :].partition_broadcast(128),
                          writes=["eg_bsrow"], semkey="eg_bsrow")
            gmg = self.sb(st, "eg_gmg", [128, D])
            self.load_row_bcast(gmg[:], self.gm_norm_g[j, :], "eg_gmg")
            vb = [self.sb(st, f"eg_vb{i}", [128, D]) for i in range(2)]
            junk = self.sb(st, "eg_junk", [128, D], BF16)
            ss = [self.sb(st, f"eg_ss{i}", [128, 1]) for i in range(2)]
            sd = [self.sb(st, f"eg_sd{i}", [128, 1]) for i in range(2)]
            rstd = [self.sb(st, f"eg_rstd{i}", [128, 1]) for i in range(2)]
            vn = [self.sb(st, f"eg_vn{i}", [128, D], BF16) for i in range(4)]
            sg = [self.sb(st, f"eg_sg{i}", [128, 512]) for i in range(2)]
            tt_ = [self.sb(st, f"eg_t{i}", [128, 512]) for i in range(2)]
            yb = [self.sb(st, f"eg_yb{i}", [128, 512], BF16) for i in range(2)]
            psV = [self.ps(st, f"eg_psV{i}") for i in range(2)]
            psS = [self.ps(st, f"eg_psS{i}") for i in range(2)]
            psU = [self.ps(st, f"eg_psU{i}") for i in range(2)]
            psG = [self.ps(st, f"eg_psG{i}") for i in range(2)]
            wvk = [f"eg_wv{kk}" for kk in range(0, 8, 4)]
            wuk = [f"eg_wu{kk}" for kk in range(0, 8, 4)]
            wgk = [f"eg_wg{kk}" for kk in range(0, 8, 4)]
            for tg in range(8):
                for n in range(4):
                    t = tg * 4 + n
                    par = n % 2
                    for half in range(2):
                        def f(half=half):
                            ins = None
                            for kk in range(8):
                                ins = nc.tensor.matmul(psV[half][:], lhsT=self.hnT[:, kk, t * 128:(t + 1) * 128],
                                                       rhs=wv[:, kk, half * 512:(half + 1) * 512],
                                                       start=(kk == 0), stop=(kk == 7))
                            return ins
                        T.op("pe", f, reads=wvk + [("hnT", tg)], writes=[f"eg_psV{half}"])
                        T.op("act", lambda half=half: nc.scalar.copy(out=vb[par][:, half * 512:(half + 1) * 512], in_=psV[half][:]),
                             reads=[f"eg_psV{half}"], writes=[f"eg_vb{par}"])
                    self.rms_rstd(f"eg_vb{par}", vb[par][:], junk[:], ss[par][:], sd[par][:], rstd[par][:])
                    T.op("dve", lambda: nc.vector.scalar_tensor_tensor(
                        out=vn[n][:], in0=vb[par][:], scalar=rstd[par][:, 0:1], in1=gmg[:], op0=ALU.mult, op1=ALU.mult),
                        reads=[f"eg_vb{par}", f"eg_vb{par}_rstd", "eg_gmg"], writes=[f"eg_vn{n}"])
                for g in range(8):
                    par = g % 2

                    def fs():
                        ins = None
                        for n in range(4):
                            ins = nc.tensor.matmul(psS[par][:, n * 128:(n + 1) * 128], lhsT=vn[n][:, g * 128:(g + 1) * 128],
                                                   rhs=wsT[:, g * 128:(g + 1) * 128], start=True, stop=True)
                        return ins
                    T.op("pe", fs, reads=[f"eg_vn{n}" for n in range(4)] + ["eg_wsT"], writes=[f"eg_psS{par}"])

                    def fu(w=wu, ps=psU):
                        ins = None
                        for kk in range(8):
                            ins = nc.tensor.matmul(ps[par][:], lhsT=w[:, kk, g * 128:(g + 1) * 128],
                                                   rhs=self.hnT[:, kk, tg * 512:(tg + 1) * 512], start=(kk == 0), stop=(kk == 7))
                        return ins
                    T.op("pe", fu, reads=wuk + [("hnT", tg)], writes=[f"eg_psU{par}"])
                    T.op("pe", lambda: fu(wg, psG), reads=wgk + [("hnT", tg)], writes=[f"eg_psG{par}"])
                    T.op("act", lambda: nc.scalar.activation(out=sg[par][:], in_=psG[par][:], func=AF.Silu),
                         reads=[f"eg_psG{par}"], writes=[f"eg_sg{par}"])
                    T.op("dve", lambda: nc.vector.tensor_tensor(out=tt_[par][:], in0=psS[par][:],
                                                                in1=bsrow[:, g, :, :].rearrange("p a b -> p (a b)"), op=ALU.add),
                         reads=[f"eg_psS{par}", "eg_bsrow"], writes=[f"eg_t{par}"])
                    T.op("dve", lambda: nc.vector.tensor_tensor(out=tt_[par][:], in0=psU[par][:], in1=tt_[par][:], op=ALU.mult),
                         reads=[f"eg_psU{par}", f"eg_t{par}"], writes=[f"eg_t{par}"])
                    T.op("pool", lambda: nc.gpsimd.tensor_tensor(out=yb[par][:], in0=tt_[par][:], in1=sg[par][:], op=ALU.mult),
                         reads=[f"eg_t{par}", f"eg_sg{par}"], writes=[f"eg_yb{par}"])
                    T.dma("sp", self.YT[1024 + g * 128:1024 + (g + 1) * 128, tg * 512:(tg + 1) * 512], yb[par][:],
                          reads=[f"eg_yb{par}"], writes=[("YT", tg)], semkey=f"eg_yb{par}")
        T.barrier()

    def even_filter(self, j, st_outer):
        T, nc = self.T, self.nc
        self.rn_row = self.sb(st_outer, "rn_row", [128, D])
        self.d_row = self.sb(st_outer, "d_row", [128, D])
        self.load_row_bcast(self.d_row[:], self.hy_d[j, :], "d_row")
        with contextlib.ExitStack() as st:
            feats = self.sb(st, "ef_feats", [33, L])
            w0 = self.sb(st, "ef_w0", [33, 64])
            w1 = self.sb(st, "ef_w1", [64, 64])
            w2 = self.sb(st, "ef_w2", [64, 64])
            cols = self.sb(st, "ef_cols", [64, 4])
            fb = self.sb(st, "ef_fb", [64, 3])
            wout = self.sb(st, "ef_wout", [64, 2048])
            absd = self.sb(st, "ef_absd", [128, D])
            tneg = self.sb(st, "ef_tneg", [128, NT])
            hA = self.sb(st, "ef_hA", [64, L])
            hB = self.sb(st, "ef_hB", [64, L])
            tmp = self.sb(st, "ef_tmp", [64, L])
            T.dma("sp", feats[:], self.c_featsT[:, :], writes=["ef_feats"], semkey="ef_feats")
            T.dma("sp", w0[:], self.hy_w0[j], writes=["ef_w0"], semkey="ef_w0")
            T.dma("sp", w1[:], self.hy_w1[j], writes=["ef_w1"], semkey="ef_w1")
            T.dma("sp", w2[:], self.hy_w2[j], writes=["ef_w2"], semkey="ef_w2")
            T.dma("sp", cols[:], self.hy_cols[j], writes=["ef_cols"], semkey="ef_cols")
            T.dma("sp", wout[:], self.hy_wout[j], writes=["ef_wout"], semkey="ef_wout")
            T.dma("sp", tneg[:], self.c_tneg[:, :], writes=["ef_tneg"], semkey="ef_tneg")
            self.load_row_bcast(absd[:], self.c_absd[0, :], "ef_absd")
            T.op("dve", lambda: nc.vector.tensor_scalar(out=fb[:], in0=cols[:, 0:3], scalar1=cols[:, 3:4], scalar2=None, op0=ALU.mult),
                 reads=["ef_cols"], writes=["ef_fb"])
            psm = [self.ps(st, f"ef_psm{i}", (64, 512)) for i in range(2)]
            srcs = [(feats, "ef_feats", w0, "ef_w0", 33), (hA, "ef_hA", w1, "ef_w1", 64), (hB, "ef_hB", w2, "ef_w2", 64)]
            dsts = [(hA, "ef_hA"), (hB, "ef_hB"), (hA, "ef_hA")]
            for li in range(3):
                src, skey, w, wkey, kdim = srcs[li]
                dst, dkey = dsts[li]
                for tg in range(8):
                    pk = f"ef_psm{tg % 2}"
                    T.op("pe", lambda tg=tg: nc.tensor.matmul(psm[tg % 2][:], lhsT=w[0:kdim, :], rhs=src[0:kdim, tg * 512:(tg + 1) * 512],
                                                              start=True, stop=True),
                         reads=[skey, wkey], writes=[pk])
                    T.op("act", lambda tg=tg: nc.scalar.activation(out=dst[:, tg * 512:(tg + 1) * 512], in_=psm[tg % 2][:],
                                                                   func=AF.Identity, bias=fb[:, li:li + 1], scale=cols[:, 3:4]),
                         reads=[pk, "ef_fb", "ef_cols"], writes=[dkey])
                T.op("dve", lambda: nc.vector.tensor_scalar(out=tmp[:], in0=dst[:], scalar1=1.0 / TWO_PI, scalar2=MAGIC,
                                                            op0=ALU.mult, op1=ALU.add), reads=[dkey], writes=["ef_tmp"])
                T.op("dve", lambda: nc.vector.tensor_scalar(out=tmp[:], in0=tmp[:], scalar1=MAGIC, scalar2=TWO_PI,
                                                            op0=ALU.subtract, op1=ALU.mult), reads=["ef_tmp"], writes=["ef_tmp"])
                T.op("dve", lambda: nc.vector.tensor_tensor(out=dst[:], in0=dst[:], in1=tmp[:], op=ALU.subtract),
                     reads=[dkey, "ef_tmp"], writes=[dkey])
                T.op("dve", lambda: nc.vector.tensor_scalar(out=dst[:], in0=dst[:], scalar1=3.1415925, scalar2=-3.1415925,
                                                            op0=ALU.min, op1=ALU.max), reads=[dkey], writes=[dkey])
                T.op("act", lambda: nc.scalar.activation(out=dst[:], in_=dst[:], func=AF.Sin), reads=[dkey], writes=[dkey])
            h3, h3k = hA, "ef_hA"
            psk = [self.ps(st, f"ef_psk{i}") for i in range(4)]
            pss = [self.ps(st, f"ef_pss{i}") for i in range(2)]
            Es = [self.sb(st, f"ef_E{i}", [128, D]) for i in range(2)]
            k0s = [self.sb(st, f"ef_k0{i}", [128, D]) for i in range(2)]
            k1s = [self.sb(st, f"ef_k1{i}", [128, D]) for i in range(2)]
            sq0s = [self.sb(st, f"ef_sq0{i}", [128, D]) for i in range(2)]
            sq1s = [self.sb(st, f"ef_sq1{i}", [128, D]) for i in range(2)]
            sqbs = [self.sb(st, f"ef_sqb{i}", [128, D], BF16) for i in range(2)]
            ones_b = self.sb(st, "ef_onesb", [128, 128], BF16)
            hhi = self.sb(st, "ef_hhi", [64, L], BF16)
            hlo = self.sb(st, "ef_hlo", [64, L], BF16)
            whi = self.sb(st, "ef_whi", [64, 2048], BF16)
            wlo = self.sb(st, "ef_wlo", [64, 2048], BF16)
            T.op("act", lambda: nc.scalar.copy(out=hhi[:], in_=h3[:]), reads=[h3k], writes=["ef_hhi"])
            T.op("dve", lambda: nc.vector.tensor_tensor(out=hlo[:], in0=h3[:], in1=hhi[:], op=ALU.subtract),
                 reads=[h3k, "ef_hhi"], writes=["ef_hlo"])
            T.op("act", lambda: nc.scalar.copy(out=whi[:], in_=wout[:]), reads=["ef_wout"], writes=["ef_whi"])
            T.op("dve", lambda: nc.vector.tensor_tensor(out=wlo[:], in0=wout[:], in1=whi[:], op=ALU.subtract),
                 reads=["ef_wout", "ef_whi"], writes=["ef_wlo"])
            T.op("dve", lambda: nc.vector.memset(ones_b[:], 1.0), writes=["ef_onesb"])
            keb = [self.sb(st, f"ef_keb{i}", [128, D], BF16) for i in range(2)]
            kob = [self.sb(st, f"ef_kob{i}", [128, D], BF16) for i in range(2)]
            pending_ss = []
            for tj in range(NT):
                par = tj % 2
                E, k0, k1, sq0, sq1, sqb = Es[par], k0s[par], k1s[par], sq0s[par], sq1s[par], sqbs[par]
                kE, kk0, kk1, ks0, ks1, ksb_ = f"ef_E{par}", f"ef_k0{par}", f"ef_k1{par}", f"ef_sq0{par}", f"ef_sq1{par}", f"ef_sqb{par}"
                for q in range(4):
                    def fk(q=q):
                        ts_ = slice(tj * 128, (tj + 1) * 128)
                        cs_ = slice(q * 512, (q + 1) * 512)
                        nc.tensor.matmul(psk[q][:], lhsT=hhi[:, ts_], rhs=whi[:, cs_], start=True, stop=False)
                        nc.tensor.matmul(psk[q][:], lhsT=hlo[:, ts_], rhs=whi[:, cs_], start=False, stop=False)
                        return nc.tensor.matmul(psk[q][:], lhsT=hhi[:, ts_], rhs=wlo[:, cs_], start=False, stop=True)
                    T.op("pe", fk, reads=["ef_hhi", "ef_hlo", "ef_whi", "ef_wlo"], writes=[f"ef_psk{q}"])
                T.op("act", lambda: nc.scalar.activation(out=E[:], in_=absd[:], func=AF.Exp, scale=tneg[:, tj:tj + 1]),
                     reads=["ef_absd", "ef_tneg"], writes=[kE])
                for q in range(2):
                    T.op("dve", lambda q=q: nc.vector.tensor_tensor(out=k0[:, q * 512:(q + 1) * 512], in0=psk[q][:],
                                                                    in1=E[:, q * 512:(q + 1) * 512], op=ALU.mult),
                         reads=[f"ef_psk{q}", kE], writes=[kk0])
                    T.op("dve", lambda q=q: nc.vector.tensor_tensor(out=k1[:, q * 512:(q + 1) * 512], in0=psk[2 + q][:],
                                                                    in1=E[:, q * 512:(q + 1) * 512], op=ALU.mult),
                         reads=[f"ef_psk{2 + q}", kE], writes=[kk1])
                if tj == 0:
                    T.op("dve", lambda: nc.vector.memset(k1[0:1, :], 0.0), reads=[], writes=[kk1])
                T.op("pool", lambda: nc.gpsimd.tensor_tensor(out=keb[par][:], in0=k0[:], in1=k1[:], op=ALU.add),
                     reads=[kk0, kk1], writes=[f"ef_keb{par}"])
                T.op("pool", lambda: nc.gpsimd.tensor_tensor(out=kob[par][:], in0=k0[:], in1=k1[:], op=ALU.subtract),
                     reads=[kk0, kk1], writes=[f"ef_kob{par}"])
                T.dma("sp", self.kebuf[tj * 128:(tj + 1) * 128, :], keb[par][:], reads=[f"ef_keb{par}"], writes=[("ke", tj)],
                      semkey=f"ef_keb{par}")
                T.dma("sp", self.kobuf[tj * 128:(tj + 1) * 128, :], kob[par][:], reads=[f"ef_kob{par}"], writes=[("ko", tj)],
                      semkey=f"ef_kob{par}")
                T.op("act", lambda: nc.scalar.activation(out=sq0[:], in_=k0[:], func=AF.Square), reads=[kk0], writes=[ks0])
                T.op("act", lambda: nc.scalar.activation(out=sq1[:], in_=k1[:], func=AF.Square), reads=[kk1], writes=[ks1])
                T.op("dve", lambda: nc.vector.tensor_tensor(out=sqb[:], in0=sq0[:], in1=sq1[:], op=ALU.add),
                     reads=[ks0, ks1], writes=[ksb_])
                def ssmm(tjj):
                    sqb_, key_ = sqbs[tjj % 2], f"ef_sqb{tjj % 2}"
                    for q in range(2):
                        T.op("pe", lambda q=q: nc.tensor.matmul(pss[q][:], lhsT=ones_b[:], rhs=sqb_[:, q * 512:(q + 1) * 512],
                                                                start=(tjj == 0), stop=(tjj == NT - 1)),
                             reads=[key_, "ef_onesb"], writes=[f"ef_pss{q}"])
                pending_ss.append(tj)
                if len(pending_ss) > 1:
                    ssmm(pending_ss.pop(0))
            while pending_ss:
                ssmm(pending_ss.pop(0))
            for q in range(2):
                T.op("act", lambda q=q: nc.scalar.activation(out=self.rn_row[:, q * 512:(q + 1) * 512], in_=pss[q][:], func=AF.Sqrt,
                                                             bias=self.eps_col[:, 0:1], scale=1.0),
                     reads=[f"ef_pss{q}", "eps"], writes=["rn_row"])
            T.op("dve", lambda: nc.vector.reciprocal(out=self.rn_row[:], in_=self.rn_row[:]), reads=["rn_row"], writes=["rn_row"])
        T.barrier()

    def _fwd_pairs(self, st, half, srcs, x2048, small, consume, tag):
        T, nc = self.T, self.nc
        FA = [self.sb(st, f"ff_FA{i}", [128, 2048], BF16) for i in range(2)]
        FB = [self.sb(st, f"ff_FB{i}", [128, 2048], BF16) for i in range(2)]
        psA = [self.ps(st, f"ff_psA{i}") for i in range(2)]
        psB = [self.ps(st, f"ff_psB{i}") for i in range(2)]
        psNy = self.ps(st, "ff_psNy")

        def chans(pi):
            par, i = pi // 16, pi % 16
            return (i, 32 + i) if par == 0 else (16 + i, 48 + i)

        def loadF(pi):
            b = pi % 2
            ca, cb = chans(pi)
            T.dma("sp", FA[b][:], self.c_F[ca], writes=[f"ff_FA{b}"], semkey=f"ff_FA{b}")
            T.dma("sp", FB[b][:], self.c_F[cb], writes=[f"ff_FB{b}"], semkey=f"ff_FB{b}")

        def group(ps, Fb, src, spec_lhsT, xrow):
            ins = None
            for jj in range(16):
                ins = nc.tensor.matmul(ps, lhsT=Fb[:, jj * 128:(jj + 1) * 128], rhs=src[:, jj, :],
                                       start=(jj == 0), stop=(jj == 15 and spec_lhsT is None))
            if spec_lhsT is not None:
                ins = nc.tensor.matmul(ps, lhsT=spec_lhsT, rhs=xrow, start=False, stop=True)
            return ins

        srcP, keyP = srcs["AP"]

        def fny():
            for jj in range(16):
                nc.tensor.matmul(psNy[0:1, :], lhsT=small[:, 258:259], rhs=srcP[:, jj, :], start=(jj == 0), stop=False)
            return nc.tensor.matmul(psNy[0:1, :], lhsT=small[0:1, 256:257], rhs=x2048["A"][0], start=False, stop=True)
        T.op("pe", fny, reads=[keyP, x2048["A"][1], "ff_small"], writes=["ff_psNy"])
        consume(-1, 0, psNy, None)
        loadF(0)
        for pi in range(32):
            if pi + 1 < 32:
                loadF(pi + 1)
            b = pi % 2
            par = pi // 16
            if par == 0:
                sa, ka = srcs["AP"]
                sbm, kb = srcs["BM"]
                T.op("pe", lambda: group(psA[b][:], FA[b], sa, small[0:1, 0:128], x2048["A"][0]),
                     reads=[f"ff_FA{b}", ka, x2048["A"][1], "ff_small"], writes=[f"ff_psA{b}"])
                T.op("pe", lambda: group(psB[b][:], FB[b], sbm, None, None), reads=[f"ff_FB{b}", kb], writes=[f"ff_psB{b}"])
            else:
                sa, ka = srcs["AM"]
                sbp, kb = srcs["BP"]
                T.op("pe", lambda: group(psA[b][:], FA[b], sa, None, None), reads=[f"ff_FA{b}", ka], writes=[f"ff_psA{b}"])
                T.op("pe", lambda: group(psB[b][:], FB[b], sbp, small[0:1, 128:256], x2048["B"][0]),
                     reads=[f"ff_FB{b}", kb, x2048["B"][1], "ff_small"], writes=[f"ff_psB{b}"])
            consume(pi, b, psA[b], psB[b])

    def even_fft(self, j):
        T, nc = self.T, self.nc
        with contextlib.ExitStack() as stc:
            small = self.sb(stc, "ff_small", [128, 2560], BF16)
            gcol = self.sb(stc, "ff_gcol", [128, 32], BF16)
            ex = self.sb(stc, "ff_ex", [128, 256], BF16)
            T.dma("sp", small[:], self.c_small[:, :], writes=["ff_small"], semkey="ff_small")
            T.dma("sp", gcol[:], self.c_gcol[:, :], writes=["ff_gcol"], semkey="ff_gcol")
            T.dma("sp", ex[:], self.c_ex[:, :], writes=["ff_ex"], semkey="ff_ex")
            for half in range(2):
                self._fft_half(j, half, small, gcol, ex)

    def _fft_half(self, j, half, small, gcol, ex):
        T, nc = self.T, self.nc
        c0 = half * 512
        rn = self.rn_row[:, c0:c0 + 512]
        dr = self.d_row[:, c0:c0 + 512]
        with contextlib.ExitStack() as st:
            ksb = {}
            for nm, buf in (("ke", self.kebuf), ("ko", self.kobuf)):
                ksb[nm] = self.sb(st, f"ff_{nm}", [128, NT, 512], BF16)
                for q in range(0, NT, 8):
                    T.dma("sp", ksb[nm][:, q:q + 8, :], buf[q * 128:(q + 8) * 128, c0:c0 + 512].rearrange("(j p) c -> p j c", p=128),
                          writes=[f"ff_{nm}"], semkey=f"ff_{nm}")
            fold = {}
            for nm in ("pe", "me", "po", "mo"):
                fold[nm] = self.sb(st, f"ff_{nm}", [128, 16, 512], BF16)
            with contextlib.ExitStack() as stR:
                psR = [self.ps(stR, f"ff_psR{i}") for i in range(2)]
                n = 0
                for nm, Pn, Mn in (("ke", "pe", "me"), ("ko", "po", "mo")):
                    src = ksb[nm]
                    for jj in range(16):
                        b = n % 2
                        n += 1

                        def frev():
                            ins = nc.tensor.matmul(psR[b][:], lhsT=ex[:, 0:128], rhs=src[:, 31 - jj, :], start=True, stop=(jj == 0))
                            if jj >= 1:
                                ins = nc.tensor.matmul(psR[b][:], lhsT=ex[:, 128:256], rhs=src[:, 32 - jj, :], start=False, stop=True)
                            return ins
                        T.op("pe", frev, reads=[f"ff_{nm}", "ff_ex"], writes=[f"ff_psR{b}"])
                        T.op("dve", lambda: nc.vector.tensor_tensor(out=fold[Pn][:, jj, :], in0=psR[b][:], in1=src[:, jj, :], op=ALU.add),
                             reads=[f"ff_psR{b}", f"ff_{nm}"], writes=[f"ff_{Pn}"])
                        T.op("dve", lambda: nc.vector.tensor_tensor(out=fold[Mn][:, jj, :], in0=src[:, jj, :], in1=psR[b][:], op=ALU.subtract),
                             reads=[f"ff_psR{b}", f"ff_{nm}"], writes=[f"ff_{Mn}"])
            hr = [self.sb(st, f"ff_hr{i}", [128, 512]) for i in range(2)]
            hi = [self.sb(st, f"ff_hi{i}", [128, 512]) for i in range(2)]
            hn = self.sb(st, "ff_hn", [1, 512])

            def consume_i(pi, b, pA, pB):
                if pi < 0:
                    T.op("dve", lambda: nc.vector.tensor_tensor(out=hn[:], in0=pA[0:1, :], in1=rn[0:1, :], op=ALU.mult),
                         reads=["ff_psNy", "rn_row"], writes=["ff_hn"])
                    T.op("dve", lambda: nc.vector.tensor_tensor(out=hn[:], in0=hn[:], in1=dr[0:1, :], op=ALU.add),
                         reads=["ff_hn", "d_row"], writes=["ff_hn"])
                    T.dma("sp", self.Hny[half], hn[:], reads=["ff_hn"], writes=[("Hny", half)], semkey="ff_hn")
                    return
                T.op("dve", lambda: nc.vector.tensor_tensor(out=hr[b][:], in0=pA[:], in1=rn, op=ALU.mult),
                     reads=[f"ff_psA{b}", "rn_row"], writes=[f"ff_hr{b}"])
                T.op("pool", lambda: nc.gpsimd.tensor_tensor(out=hr[b][:], in0=hr[b][:], in1=dr, op=ALU.add),
                     reads=[f"ff_hr{b}", "d_row"], writes=[f"ff_hr{b}"])
                T.op("dve", lambda: nc.vector.tensor_tensor(out=hi[b][:], in0=pB[:], in1=rn, op=ALU.mult),
                     reads=[f"ff_psB{b}", "rn_row"], writes=[f"ff_hi{b}"])
                T.dma("sp", self.Hre[half, pi], hr[b][:], reads=[f"ff_hr{b}"], writes=[("Hre", half, pi)], semkey=f"ff_hr{b}")
                T.dma("sp", self.Him[half, pi], hi[b][:], reads=[f"ff_hi{b}"], writes=[("Him", half, pi)], semkey=f"ff_hi{b}")

            srcs = dict(AP=(fold["pe"], "ff_pe"), AM=(fold["me"], "ff_me"), BP=(fold["po"], "ff_po"), BM=(fold["mo"], "ff_mo"))
            x2048 = dict(A=(ksb["ke"][0:1, 16, :], "ff_ke"), B=(ksb["ko"][0:1, 16, :], "ff_ko"))
            self._fwd_pairs(st, half, srcs, x2048, small, consume_i, "i")
        T.barrier()
        with contextlib.ExitStack() as st:
            Y = self.sb(st, "ff_Y", [128, 64, 512], BF16)
            ynq = self.sb(st, "ff_ynq", [1, 512], BF16)
            with contextlib.ExitStack() as st2:
                pT = self.sb(st2, "ff_pT", [128, 16, 512], BF16)
                mT = self.sb(st2, "ff_mT", [128, 16, 512], BF16)
                z2048 = self.sb(st2, "ff_z2048", [1, 512], BF16)
                zf = self.sb(st2, "ff_zf", [128, L], BF16)
                pf = self.sb(st2, "ff_pf", [128, 2048], BF16)
                mf = self.sb(st2, "ff_mf", [128, 2048], BF16)
                with contextlib.ExitStack() as st3:
                    psTb = [self.ps(st3, f"ff_psT{i}", (128, 512), BF16) for i in range(2)]
                    nt = 0
                    for cs in range(4):
                        T.dma("sp", zf[:], self.zbuf[c0 + cs * 128:c0 + (cs + 1) * 128, :], reads=[("z", half * 4 + cs)],
                              writes=["ff_zf"], semkey="ff_zf")
                        T.op("dve", lambda: nc.vector.tensor_tensor(out=pf[:, 1:2048], in0=zf[:, 1:2048], in1=zf[:, 4095:2048:-1], op=ALU.add),
                             reads=["ff_zf"], writes=["ff_pf"])
                        T.op("dve", lambda: nc.vector.tensor_copy(out=pf[:, 0:1], in_=zf[:, 0:1]), reads=["ff_zf"], writes=["ff_pf"])
                        T.op("dve", lambda: nc.vector.tensor_tensor(out=mf[:, 1:2048], in0=zf[:, 1:2048], in1=zf[:, 4095:2048:-1], op=ALU.subtract),
                             reads=["ff_zf"], writes=["ff_mf"])
                        T.op("dve", lambda: nc.vector.tensor_copy(out=mf[:, 0:1], in_=zf[:, 0:1]), reads=["ff_zf"], writes=["ff_mf"])
                        for srcf, skey, dstT, dkey in ((pf, "ff_pf", pT, "ff_pT"), (mf, "ff_mf", mT, "ff_mT")):
                            for j0 in range(0, 16, 4):
                                pb = nt % 2
                                nt += 1

                                def ftr():
                                    ins = None
                                    for q in range(4):
                                        ins = nc.tensor.transpose(out=psTb[pb][:, q * 128:(q + 1) * 128],
                                                                  in_=srcf[:, (j0 + q) * 128:(j0 + q + 1) * 128], identity=self.ident_bf[:])
                                    return ins
                                T.op("pe", ftr, reads=[skey, "ident_bf"], writes=[f"ff_psT{pb}"])
                                if pb == 0:
                                    T.op("act", lambda: nc.scalar.copy(out=dstT[:, j0:j0 + 4, cs * 128:(cs + 1) * 128],
                                                                       in_=psTb[pb][:].rearrange("p (a b) -> p a b", a=4)),
                                         reads=[f"ff_psT{pb}"], writes=[dkey])
                                else:
                                    T.op("dve", lambda: nc.vector.tensor_copy(out=dstT[:, j0:j0 + 4, cs * 128:(cs + 1) * 128],
                                                                              in_=psTb[pb][:].rearrange("p (a b) -> p a b", a=4)),
                                         reads=[f"ff_psT{pb}"], writes=[dkey])
                        pb = nt % 2
                        nt += 1
                        T.op("pe", lambda: nc.tensor.transpose(out=psTb[pb][0:1, 0:128], in_=zf[:, 2048:2049], identity=self.ident_bf[:]),
                             reads=["ff_zf", "ident_bf"], writes=[f"ff_psT{pb}"])
                        T.op("act", lambda: nc.scalar.copy(out=z2048[0:1, cs * 128:(cs + 1) * 128], in_=psTb[pb][0:1, 0:128]),
                             reads=[f"ff_psT{pb}"], writes=["ff_z2048"])
                hr = [self.sb(st2, f"ff_hr{i}", [128, 512]) for i in range(2)]
                hi = [self.sb(st2, f"ff_hi{i}", [128, 512]) for i in range(2)]
                hn = self.sb(st2, "ff_hn", [1, 512])
                t1 = self.sb(st2, "ff_t1", [128, 512])
                t2 = self.sb(st2, "ff_t2", [128, 512])
                t3 = self.sb(st2, "ff_t3", [128, 512])
                t4 = self.sb(st2, "ff_t4", [128, 512])
                T.dma("sp", hn[:], self.Hny[half], reads=[("Hny", half)], writes=["ff_hn"], semkey="ff_hn")
                loaded = set()

                def loadH(pi):
                    b = pi % 2
                    T.dma("sp", hr[b][:], self.Hre[half, pi], reads=[("Hre", half, pi)], writes=[f"ff_hr{b}"], semkey=f"ff_hr{b}")
                    T.dma("sp", hi[b][:], self.Him[half, pi], reads=[("Him", half, pi)], writes=[f"ff_hi{b}"], semkey=f"ff_hi{b}")

                def consume_ii(pi, b, pA, pB):
                    if pi < 0:
                        T.op("dve", lambda: nc.vector.tensor_tensor(out=ynq[:], in0=pA[0:1, :], in1=hn[:], op=ALU.mult),
                             reads=["ff_psNy", "ff_hn"], writes=["ff_ynq"])
                        loadH(0)
                        return
                    par, i = pi // 16, pi % 16
                    ca, cb = (i, 32 + i) if par == 0 else (16 + i, 48 + i)
                    T.op("dve", lambda: nc.vector.tensor_tensor(out=t1[:], in0=pA[:], in1=hr[b][:], op=ALU.mult),
                         reads=[f"ff_psA{b}", f"ff_hr{b}"], writes=["ff_t1"])
                    T.op("dve", lambda: nc.vector.tensor_tensor(out=t2[:], in0=pB[:], in1=hi[b][:], op=ALU.mult),
                         reads=[f"ff_psB{b}", f"ff_hi{b}"], writes=["ff_t2"])
                    T.op("dve", lambda: nc.vector.tensor_tensor(out=t3[:], in0=pA[:], in1=hi[b][:], op=ALU.mult),
                         reads=[f"ff_psA{b}", f"ff_hi{b}"], writes=["ff_t3"])
                    T.op("dve", lambda: nc.vector.tensor_tensor(out=t4[:], in0=pB[:], in1=hr[b][:], op=ALU.mult),
                         reads=[f"ff_psB{b}", f"ff_hr{b}"], writes=["ff_t4"])
                    if pi + 1 < 32:
                        loadH(pi + 1)
                    T.op("pool", lambda: nc.gpsimd.tensor_tensor(out=Y[:, ca, :], in0=t1[:], in1=t2[:], op=ALU.subtract),
                         reads=["ff_t1", "ff_t2"], writes=[("ff_Y", ca)])
                    T.op("pool", lambda: nc.gpsimd.tensor_tensor(out=Y[:, cb, :], in0=t3[:], in1=t4[:], op=ALU.add),
                         reads=["ff_t3", "ff_t4"], writes=[("ff_Y", cb)])

                srcs = dict(AP=(pT, "ff_pT"), AM=(mT, "ff_mT"), BP=(pT, "ff_pT"), BM=(mT, "ff_mT"))
                x2048 = dict(A=(z2048[0:1, :], "ff_z2048"), B=(z2048[0:1, :], "ff_z2048"))
                self._fwd_pairs(st2, half, srcs, x2048, small, consume_ii, "ii")
            T.barrier()
            with contextlib.ExitStack() as st3:
                Gp = [self.sb(st3, f"ff_Gp{i}", [128, 4096], BF16) for i in range(3)]
                P1sb = [self.sb(st3, f"ff_P1sb{i}", [128, 512]) for i in range(4)]
                sm = [self.sb(st3, f"ff_sm{i}", [128, 512]) for i in range(2)]
                df = [self.sb(st3, f"ff_df{i}", [128, 512]) for i in range(2)]
                Af = [self.sb(st3, f"ff_Af{i}", [128, 512]) for i in range(2)]
                Am = [self.sb(st3, f"ff_Am{i}", [128, 512]) for i in range(2)]
                yof = [self.sb(st3, f"ff_yof{i}", [128, 512], BF16) for i in range(2)]
                yom = [self.sb(st3, f"ff_yom{i}", [128, 512], BF16) for i in range(2)]
                acol = self.sb(st3, "ff_acol", [128, 4])
                ycol = self.sb(st3, "ff_ycol", [128, 4], BF16)
                psI = [self.ps(st3, f"ff_psI{i}") for i in range(8)]
                chmap = list(range(0, 16)) + list(range(48, 64)) + list(range(16, 32)) + list(range(32, 48))
                for cs in range(4):
                    cc = half * 4 + cs
                    T.dma("sp", acol[:, cs:cs + 1], self.Abuf[cc * 128:(cc + 1) * 128, 2048:2049], reads=[("A", cc)],
                          writes=["ff_acol"], semkey="ff_acol", allow_slow_non_contiguous=True)

                    def fcol():
                        for k in range(32):
                            nc.tensor.matmul(psI[7][:, cs:cs + 1], lhsT=Y[:, chmap[k], cs * 128:(cs + 1) * 128], rhs=gcol[:, k:k + 1],
                                             start=(k == 0), stop=False)
                        return nc.tensor.matmul(psI[7][:, cs:cs + 1], lhsT=ynq[0:1, cs * 128:(cs + 1) * 128], rhs=small[0:1, 257:258],
                                                start=False, stop=True)
                    T.op("pe", fcol, reads=[("ff_Y", c) for c in chmap[:32]] + ["ff_gcol", "ff_ynq", "ff_small"], writes=["ff_psI7"])
                T.op("dve", lambda: nc.vector.tensor_tensor(out=ycol[:], in0=psI[7][:, 0:4], in1=acol[:], op=ALU.mult),
                     reads=["ff_psI7", "ff_acol"], writes=["ff_ycol"])
                for cs in range(4):
                    cc = half * 4 + cs
                    T.dma("sp", self.YT[cc * 128:(cc + 1) * 128, 2048:2049], ycol[:, cs:cs + 1], reads=["ff_ycol"],
                          writes=[("YT", 4)], semkey="ff_ycol", allow_slow_non_contiguous=True)
                order = [(tg, pc) for tg in range(4) for pc in range(8)]

                def loadG(n):
                    tg, pc = order[n]
                    T.dma("sp", Gp[n % 3][:], self.c_G[tg, pc], writes=[f"ff_Gp{n % 3}"], semkey=f"ff_Gp{n % 3}")

                loadG(0)
                loadG(1)
                ne = 0
                for n, (tg, pc) in enumerate(order):
                    if n + 2 < len(order):
                        loadG(n + 2)
                    gb = Gp[n % 3]
                    isP1 = pc < 4
                    for cs in range(4):
                        bk = (0 if isP1 else 4) + cs
                        bank = psI[bk]

                        def fi():
                            ins = None
                            for rr in range(8):
                                lastmm = (pc % 4 == 3 and rr == 7)
                                ins = nc.tensor.matmul(bank[:], lhsT=Y[:, chmap[pc * 8 + rr], cs * 128:(cs + 1) * 128],
                                                       rhs=gb[:, rr * 512:(rr + 1) * 512],
                                                       start=(pc % 4 == 0 and rr == 0), stop=(lastmm and not isP1))
                            if isP1 and pc == 3:
                                ins = nc.tensor.matmul(bank[:], lhsT=ynq[0:1, cs * 128:(cs + 1) * 128],
                                                       rhs=small[0:1, 512 + tg * 512:512 + (tg + 1) * 512], start=False, stop=True)
                            return ins
                        T.op("pe", fi, reads=[f"ff_Gp{n % 3}", "ff_ynq", "ff_small"] + [("ff_Y", chmap[pc * 8 + rr]) for rr in range(8)],
                             writes=[f"ff_psI{bk}"])
                    if pc == 3:
                        for cs in range(4):
                            T.op("act", lambda: nc.scalar.copy(out=P1sb[cs][:], in_=psI[cs][:]), reads=[f"ff_psI{cs}"], writes=[f"ff_P1sb{cs}"])
                    if pc == 7:
                        for cs in range(4):
                            cc = half * 4 + cs
                            e = ne % 2
                            ne += 1
                            if tg == 0:
                                mlo, mw = 3585, 511
                            else:
                                mlo, mw = 3585 - 512 * tg, 512
                            T.dma("act", Af[e][:], self.Abuf[cc * 128:(cc + 1) * 128, tg * 512:(tg + 1) * 512], reads=[("A", cc)],
                                  writes=[f"ff_Af{e}"], semkey=f"ff_Af{e}")
                            T.dma("act", Am[e][:, 0:mw], self.Abuf[cc * 128:(cc + 1) * 128, mlo:mlo + mw], reads=[("A", cc)],
                                  writes=[f"ff_Am{e}"], semkey=f"ff_Am{e}")
                            T.op("dve", lambda: nc.vector.tensor_tensor(out=sm[e][:], in0=psI[4 + cs][:], in1=P1sb[cs][:], op=ALU.add),
                                 reads=[f"ff_psI{4 + cs}", f"ff_P1sb{cs}"], writes=[f"ff_sm{e}"])
                            T.op("dve", lambda: nc.vector.tensor_tensor(out=df[e][:], in0=P1sb[cs][:], in1=psI[4 + cs][:], op=ALU.subtract),
                                 reads=[f"ff_psI{4 + cs}", f"ff_P1sb{cs}"], writes=[f"ff_df{e}"])
                            T.op("pool", lambda: nc.gpsimd.tensor_tensor(out=yof[e][:], in0=sm[e][:], in1=Af[e][:], op=ALU.mult),
                                 reads=[f"ff_sm{e}", f"ff_Af{e}"], writes=[f"ff_yof{e}"])
                            dfr = df[e][:, 511:0:-1] if tg == 0 else df[e][:, ::-1]
                            T.op("dve", lambda: nc.vector.tensor_tensor(out=yom[e][:, 0:mw], in0=dfr, in1=Am[e][:, 0:mw], op=ALU.mult),
                                 reads=[f"ff_df{e}", f"ff_Am{e}"], writes=[f"ff_yom{e}"])
                            T.dma("act", self.YT[cc * 128:(cc + 1) * 128, tg * 512:(tg + 1) * 512], yof[e][:],
                                  reads=[f"ff_yof{e}"], writes=[("YT", tg)], semkey=f"ff_yof{e}")
                            T.dma("act", self.YT[cc * 128:(cc + 1) * 128, mlo:mlo + mw], yom[e][:, 0:mw],
                                  reads=[f"ff_yom{e}"], writes=[("YT", 7 - tg)], semkey=f"ff_yom{e}")
        T.barrier()

    def odd_pool(self, j):
        T, nc = self.T, self.nc
        w2d = self.od_w_in[j]
        with contextlib.ExitStack() as st:
            pc = self.sb(st, "op_cols", [128, 16])
            T.dma("sp", pc[:], self.poolcols[j], writes=["op_cols"], semkey="op_cols")
            wx = self.sb(st, "op_wx", [128, 8, 256], BF16)
            wg = self.sb(st, "op_wg", [128, 8, 256], BF16)
            pw = self.sb(st, "op_pw", [128, 2, 256], BF16)
            inv = self.sb(st, "op_inv", [128, L])
            xps = [self.sb(st, f"op_xp{i}", [128, L + 16]) for i in range(2)]
            sA = self.sb(st, "op_sA", [128, L + 16])
            sB = self.sb(st, "op_sB", [128, L + 16])
            dTs = [[self.sb(st, f"op_dT{p}{i}", [128, L], BF16) for i in range(2)] for p in range(2)]
            sg = [self.sb(st, f"op_sg{i}", [128, 512]) for i in range(2)]
            yt = [self.sb(st, f"op_yt{i}", [128, 512]) for i in range(2)]
            yb = [self.sb(st, f"op_yb{i}", [128, 512], BF16) for i in range(2)]
            pss = [self.ps(st, f"op_ps{i}") for i in range(4)]
            psy = [self.ps(st, f"op_psy{i}") for i in range(2)]
            psg = [self.ps(st, f"op_psg{i}") for i in range(2)]
            for i in range(2):
                T.op("pool", lambda i=i: nc.gpsimd.memset(xps[i][:, 0:8], 0.0), writes=[f"op_xp{i}"])
                T.op("pool", lambda i=i: nc.gpsimd.memset(xps[i][:, L + 8:L + 16], 0.0), writes=[f"op_xp{i}"])

            def projX(g, c2):
                xp, xk = xps[c2], f"op_xp{c2}"
                for tg in range(8):
                    pk = f"op_ps{tg % 4}"
                    self.proj_fm(pss[tg % 4][:], wx, "op_wx", tg, pk, col0=c2 * 128)
                    T.op("act", lambda tg=tg: nc.scalar.copy(out=xp[:, 8 + tg * 512:8 + (tg + 1) * 512], in_=pss[tg % 4][:]),
                         reads=[pk], writes=[xk])

            def chain(g, c2):
                w = (2, 4, 8, 16)[g]
                xp, xk = xps[c2], f"op_xp{c2}"
                dT, dk = dTs[g % 2][c2], f"op_dT{g % 2}{c2}"
                n = L + 16
                srcb, skey = xp, xk
                bufs = [(sA, "op_sA"), (sB, "op_sB")]
                step = 1
                bi = 0
                vlen = n
                while step < w:
                    dst, dkey = bufs[bi]
                    ln = vlen - step
                    vlen = ln
                    h1_ = 2432
                    T.op("dve", lambda srcb=srcb, dst=dst, step=step: nc.vector.tensor_tensor(
                        out=dst[:, 0:h1_], in0=srcb[:, 0:h1_], in1=srcb[:, step:step + h1_], op=ALU.add),
                        reads=[skey], writes=[dkey])
                    T.op("pool", lambda srcb=srcb, dst=dst, step=step, ln=ln: nc.gpsimd.tensor_tensor(
                        out=dst[:, h1_:ln], in0=srcb[:, h1_:ln], in1=srcb[:, step + h1_:step + ln], op=ALU.add),
                        reads=[skey], writes=[dkey])
                    srcb, skey = dst, dkey
                    step *= 2
                    bi ^= 1
                off = 8 - w // 2
                dst, dkey = bufs[bi]
                T.op("dve", lambda: nc.vector.tensor_tensor(out=dst[:, 0:L], in0=srcb[:, off:off + L], in1=inv[:], op=ALU.mult),
                     reads=[skey, "op_inv"], writes=[dkey])
                T.op("pool", lambda: nc.gpsimd.tensor_tensor(out=dT[:], in0=dst[:, 0:L], in1=xp[:, 8:8 + L], op=ALU.subtract),
                     reads=[dkey, xk], writes=[dk])

            def Yst(g):
                dT = dTs[g % 2]
                dks = [f"op_dT{g % 2}0", f"op_dT{g % 2}1"]
                for do in range(2):
                    ch = g * 2 + do
                    for tg in range(8):
                        b = tg % 2

                        def fy():
                            ins = None
                            for c2 in range(2):
                                ins = nc.tensor.matmul(psy[b][:], lhsT=pw[:, c2, do * 128:(do + 1) * 128],
                                                       rhs=dT[c2][:, tg * 512:(tg + 1) * 512], start=(c2 == 0), stop=(c2 == 1))
                            return ins
                        T.op("pe", fy, reads=["op_pw"] + dks, writes=[f"op_psy{b}"])
                        self.proj_fm(psg[b][:], wg, "op_wg", tg, f"op_psg{b}", col0=do * 128)
                        T.op("act", lambda: nc.scalar.activation(out=sg[b][:], in_=psg[b][:], func=AF.Silu),
                             reads=[f"op_psg{b}"], writes=[f"op_sg{b}"])
                        T.op("dve", lambda: nc.vector.tensor_scalar(out=yt[b][:], in0=psy[b][:], scalar1=pc[:, ch:ch + 1],
                                                                    scalar2=pc[:, 8 + ch:9 + ch], op0=ALU.add, op1=ALU.mult),
                             reads=[f"op_psy{b}", "op_cols"], writes=[f"op_yt{b}"])
                        T.op("pool", lambda: nc.gpsimd.tensor_tensor(out=yb[b][:], in0=yt[b][:], in1=sg[b][:], op=ALU.mult),
                             reads=[f"op_yt{b}", f"op_sg{b}"], writes=[f"op_yb{b}"])
                        T.dma("sp", self.YT[ch * 128:(ch + 1) * 128, tg * 512:(tg + 1) * 512], yb[b][:],
                              reads=[f"op_yb{b}"], writes=[("YT", tg)], semkey=f"op_yb{b}")

            for k in range(5):
                if k < 4:
                    self.load_w(wx[:], w2d, 0, 8, k * 256, 256, "op_wx")
                    self.load_row_bcast(inv[:], self.c_invcnt[k, :], "op_inv")
                    projX(k, 0)
                    projX(k, 1)
                    chain(k, 0)
                    chain(k, 1)
                if k >= 1:
                    Yst(k - 1)
                if k < 4:
                    self.load_w(wg[:], w2d, 0, 8, 1024 + k * 256, 256, "op_wg")
                    self.load_w(pw[:], self.pool_w[j, k], 0, 2, 0, 256, "op_pw")
        T.barrier()

    def odd_attn(self, j):
        T, nc = self.T, self.nc
        w2d = self.od_w_in[j]
        with contextlib.ExitStack() as st:
            mask = self.sb(st, "oa_mask", [128, 14 * 128])
            T.dma("sp", mask[:], self.c_mask[:, :], writes=["oa_mask"], semkey="oa_mask")
            wq = self.sb(st, "oa_wq", [128, 8, 512], BF16)
            QT = self.sb(st, "oa_QT", [128, L], BF16)
            KT = self.sb(st, "oa_KT", [128, L], BF16)
            SG = self.sb(st, "oa_SG", [128, L], BF16)
            VA = self.sb(st, "oa_VA", [128, 32, 128], BF16)
            VB = self.sb(st, "oa_VB", [128, 31, 128], BF16)
            TB = self.sb(st, "oa_TB", [128, 14 * 128], BF16)
            TBf = self.sb(st, "oa_TBf", [128, 14 * 128])
            PT = [self.sb(st, f"oa_PT{i}", [128, 256], BF16) for i in range(4)]
            rd = [self.sb(st, f"oa_rd{i}", [128, 512]) for i in range(2)]
            yd = [self.sb(st, f"oa_yd{i}", [128, 512], BF16) for i in range(2)]
            psP = [self.ps(st, f"oa_psP{i}") for i in range(2)]
            psS = [self.ps(st, f"oa_psS{i}") for i in range(2)]
            psN = [self.ps(st, f"oa_psN{i}") for i in range(2)]
            psD = [self.ps(st, f"oa_psD{i}") for i in range(2)]
            for hp in range(8):
                for q, c0 in enumerate((2048, 3072, 4096, 5120)):
                    self.load_w(wq[:, :, q * 128:(q + 1) * 128], w2d, 0, 8, c0 + hp * 128, 128, "oa_wq")
                T.dma("sp", TBf[:], self.rpbT[j, hp], writes=["oa_TBf"], semkey="oa_TBf")
                T.op("dve", lambda: nc.vector.tensor_tensor(out=TB[:], in0=TBf[:], in1=mask[:], op=ALU.add),
                     reads=["oa_TBf", "oa_mask"], writes=["oa_TB"])
                for tg in range(8):
                    b = tg % 2
                    self.proj_fm(psP[b][:], wq, "oa_wq", tg, f"oa_psP{b}", col0=0)
                    T.op("act", lambda: nc.scalar.activation(out=QT[:, tg * 512:(tg + 1) * 512], in_=psP[b][:], func=AF.Copy, scale=0.125),
                         reads=[f"oa_psP{b}"], writes=["oa_QT"])
                    self.proj_fm(psS[b][:], wq, "oa_wq", tg, f"oa_psS{b}", col0=128)
                    T.op("dve", lambda: nc.vector.tensor_copy(out=KT[:, tg * 512:(tg + 1) * 512], in_=psS[b][:]),
                         reads=[f"oa_psS{b}"], writes=["oa_KT"])
                    self.proj_fm(psN[b][:], wq, "oa_wq", tg, f"oa_psN{b}", col0=384)
                    T.op("act", lambda: nc.scalar.activation(out=SG[:, tg * 512:(tg + 1) * 512], in_=psN[b][:], func=AF.Silu),
                         reads=[f"oa_psN{b}"], writes=["oa_SG"])
                for which, Vt, ntile, toff in ((0, VA, 32, 0),):
                    for t0 in range(0, ntile, 4):
                        nb = min(4, ntile - t0)
                        b = (t0 // 4) % 2

                        def fv():
                            ins = None
                            for q in range(nb):
                                tok = (t0 + q) * 128 + toff
                                for kk in range(8):
                                    ins = nc.tensor.matmul(psD[b][:, q * 128:(q + 1) * 128], lhsT=self.hnT[:, kk, tok:tok + 128],
                                                           rhs=wq[:, kk, 256:384], start=(kk == 0), stop=(kk == 7))
                            return ins
                        T.op("pe", fv, reads=["oa_wq"] + [("hnT", g) for g in range(8)], writes=[f"oa_psD{b}"])
                        T.op("dve", lambda: nc.vector.tensor_copy(out=Vt[:, t0:t0 + nb, :],
                                                                  in_=psD[b][:, 0:nb * 128].rearrange("p (a b) -> p a b", a=nb)),
                             reads=[f"oa_psD{b}"], writes=[f"oa_V{which}"])
                T.dma("sp", VB[0:64, :, :], VA[64:128, 0:31, :], reads=["oa_V0"], writes=["oa_V1"], semkey="oa_V1")
                T.dma("sp", VB[64:128, :, :], VA[0:64, 1:32, :], reads=["oa_V0"], writes=["oa_V1"], semkey="oa_V1")
                psSs = [[psS[0], psS[1]], [psP[0], psP[1]]]
                psSk = [["oa_psS0", "oa_psS1"], ["oa_psP0", "oa_psP1"]]

                def FS(r):
                    rs = min(max(r - 4, 0), 56)
                    o = r - rs
                    par = r % 2

                    def f():
                        ins = None
                        for m in range(4):
                            kt0 = 64 * (rs + 2 * m)
                            dd = 2 * m - o + 7
                            for hl in range(2):
                                hb = hl * 64
                                nc.tensor.matmul(psSs[par][hl][:, m * 64:(m + 1) * 64], lhsT=KT[hb:hb + 64, kt0:kt0 + 128],
                                                 rhs=QT[hb:hb + 64, r * 64:(r + 1) * 64], start=True, stop=False)
                            for hl in range(2):
                                hb = hl * 64
                                ins = nc.tensor.matmul(psSs[par][hl][:, m * 64:(m + 1) * 64], lhsT=TB[hb:hb + 64, dd * 128:(dd + 1) * 128],
                                                       rhs=self.ident_bf[hb:hb + 64, hb:hb + 64], start=False, stop=True)
                        return ins
                    T.op("pe", f, reads=["oa_KT", "oa_QT", "oa_TB", "ident_bf"], writes=psSk[par])
                    for hl in range(2):
                        T.op("act", lambda hl=hl: nc.scalar.activation(out=PT[par * 2 + hl][:], in_=psSs[par][hl][:, 0:256], func=AF.Exp),
                             reads=[psSk[par][hl]], writes=[f"oa_PT{par * 2 + hl}"])

                def FN(r):
                    rs = min(max(r - 4, 0), 56)
                    par = r % 2
                    rg, rl = r // 8, r % 8
                    nb_ = rg % 2

                    def f():
                        ins = None
                        for m in range(4):
                            row0 = rs + 2 * m
                            for hl in range(2):
                                hb = hl * 64
                                if row0 % 2 == 0:
                                    vsrc = VA[:, row0 // 2, hb:hb + 64]
                                else:
                                    vsrc = VB[:, (row0 - 1) // 2, hb:hb + 64]
                                nc.tensor.matmul(psN[nb_][hb:hb + 64, rl * 64:(rl + 1) * 64], lhsT=vsrc,
                                                 rhs=PT[par * 2 + hl][:, m * 64:(m + 1) * 64], start=(m == 0), stop=(m == 3))
                            for hl in range(2):
                                hb = hl * 64
                                ins = nc.tensor.matmul(psD[nb_][hb:hb + 64, rl * 64:(rl + 1) * 64], lhsT=self.ones_bf[:, 0:64],
                                                       rhs=PT[par * 2 + hl][:, m * 64:(m + 1) * 64], start=(m == 0), stop=(m == 3))
                        return ins
                    T.op("pe", f, reads=[f"oa_PT{par * 2}", f"oa_PT{par * 2 + 1}", "oa_V0", "oa_V1", "ones_bf"],
                         writes=[f"oa_psN{nb_}", f"oa_psD{nb_}"])
                    if rl == 7:
                        T.op("dve", lambda: nc.vector.reciprocal(out=rd[nb_][:], in_=psD[nb_][:]), reads=[f"oa_psD{nb_}"], writes=[f"oa_rd{nb_}"])
                        T.op("dve", lambda: nc.vector.tensor_tensor(out=rd[nb_][:], in0=psN[nb_][:], in1=rd[nb_][:], op=ALU.mult),
                             reads=[f"oa_psN{nb_}", f"oa_rd{nb_}"], writes=[f"oa_rd{nb_}"])
                        T.op("pool", lambda: nc.gpsimd.tensor_tensor(out=yd[nb_][:], in0=rd[nb_][:], in1=SG[:, rg * 512:(rg + 1) * 512], op=ALU.mult),
                             reads=[f"oa_rd{nb_}", "oa_SG"], writes=[f"oa_yd{nb_}"])
                        T.dma("sp", self.YT[1024 + hp * 128:1024 + (hp + 1) * 128, rg * 512:(rg + 1) * 512], yd[nb_][:],
                              reads=[f"oa_yd{nb_}"], writes=[("YT", rg)], semkey=f"oa_yd{nb_}")

                FS(0)
                for r in range(64):
                    if r + 1 < 64:
                        FS(r + 1)
                    FN(r)
        T.barrier()

    def build(self):
        self.declare()
        self.load_globals()
        T = self.T
        hst = contextlib.ExitStack()
        self.hnT = self.sb(hst, "hnT", [128, 8, L], BF16)
        self.head(0)
        for li in range(self.nlayers):
            j = li // 2
            last = (li == 3)
            if li % 2 == 0:
                self.even_pre(j)
                self.even_gmlp(j)
            else:
                self.odd_pool(j)
                self.odd_attn(j)
            hst.close()
            if li % 2 == 0:
                with contextlib.ExitStack() as st_outer:
                    self.even_filter(j, st_outer)
                    self.even_fft(j)
            if not last:
                hst = contextlib.ExitStack()
                self.hnT = self.sb(hst, "hnT", [128, 8, L], BF16)
            self.tail(li, self.ev_w_out[j] if li % 2 == 0 else self.od_w_out[j], last)
        hst.close()
        if self.nlayers < 4:
            with contextlib.ExitStack() as st:
                buf = [self.sb(st, f"dbg{i}", [128, D]) for i in range(2)]
                for t in range(NT):
                    b = t % 2
                    T.dma("sp", buf[b][:], self.hbuf[t * 128:(t + 1) * 128, :], reads=[("h", t)], writes=[f"dbg{b}"], semkey=f"dbg{b}")
                    T.dma("sp", self.out[t * 128:(t + 1) * 128, :], buf[b][:], reads=[f"dbg{b}"], writes=[("out", t)], semkey=f"dbg{b}")
        T.barrier()
        return self.nc


_CONST_CACHE = {}


def _consts():
    if not _CONST_CACHE:
        featsT, tneg, absd = _filter_consts()
        Ftab, Gtab, gcol, small, ex = _dft_tables()
        mask, idx = _natten_consts()
        _CONST_CACHE.update(dict(
            c_ident=np.eye(128, dtype=np.float32), c_featsT=featsT, c_tneg=tneg, c_absd=absd,
            c_F=Ftab, c_G=Gtab, c_gcol=gcol, c_small=small, c_ex=ex, c_mask=mask, c_invcnt=_pool_consts(), _idx=idx))
    return _CONST_CACHE


def _prep_shared(inp):
    c = _consts()
    f = lambda a: np.ascontiguousarray(np.asarray(a, dtype=np.float32))
    sh = {}
    for k in ("norm_g", "ple_g", "ple_up", "ple_gate_w", "ev_w_in", "ev_w_out", "od_w_in", "od_w_out",
              "hy_w0", "hy_w1", "hy_w2", "hy_wout", "hy_d", "gm_norm_g", "gm_bs", "pool_w"):
        sh[k] = f(inp[k])
    sh["final_g"] = f(inp["final_g"]).reshape(1, D)
    cw = f(inp["ev_conv_w"])
    sh["convw"] = np.ascontiguousarray(cw.reshape(2, 3, 24, 128).transpose(0, 3, 1, 2).reshape(2, 128, 72))
    cb = f(inp["ev_conv_b"])
    sh["convb"] = np.ascontiguousarray(cb.reshape(2, 24, 128).transpose(0, 2, 1))
    sh["hy_cols"] = np.ascontiguousarray(np.stack([f(inp["hy_b0"]), f(inp["hy_b1"]), f(inp["hy_b2"]), f(inp["hy_freq"])], axis=-1))
    ws = f(inp["gm_ws"])
    sh["gm_wsT"] = np.ascontiguousarray(ws.transpose(0, 3, 1, 2).reshape(2, 128, 1024))
    pb = f(inp["pool_b"]).reshape(2, 8, 128).transpose(0, 2, 1)
    psc = f(inp["pool_scale"]).reshape(2, 8, 128).transpose(0, 2, 1)
    sh["poolcols"] = np.ascontiguousarray(np.concatenate([pb, psc], axis=-1))
    rpb = f(inp["na_rpb"])
    idx = c["_idx"]
    dd = (np.arange(14)[:, None] + np.arange(2)[None, :])
    g = rpb[:, :, dd[:, :, None, None], idx[None, None, :, :]]
    g = g.reshape(2, 8, 2, 14, 2, 64, 64).transpose(0, 1, 2, 5, 3, 4, 6)
    sh["rpbT"] = np.ascontiguousarray(g.reshape(2, 8, 128, 14 * 128))
    for k, v in c.items():
        if not k.startswith("_"):
            sh[k] = v
    return sh


_NC_CACHE = {}


def kernel(**inputs):
    nl = int(inputs.pop("_nlayers", 4))
    cores = inputs.pop("_cores", list(range(8)))
    sh = _prep_shared(inputs)
    x = np.asarray(inputs["x"], dtype=np.float32)
    p = np.asarray(inputs["p"], dtype=np.float32)
    if nl not in _NC_CACHE:
        _NC_CACHE[nl] = Builder(nlayers=nl).build()
    nc = _NC_CACHE[nl]
    in_maps = []
    for b in cores:
        m = dict(sh)
        m["x"] = np.ascontiguousarray(x[b])
        m["p"] = np.ascontiguousarray(p[:, b])
        in_maps.append(m)
    res = run_bass_kernel_spmd(nc, in_maps, core_ids=list(range(len(cores))))
    out = np.stack([np.asarray(r["out"], dtype=np.float32) for r in res.results], axis=0)
    return out
```

```python
import contextlib
import numpy as np
import ml_dtypes
import concourse.bass as bass
import concourse.mybir as mybir
from concourse.bass_utils import run_bass_kernel_spmd

F32 = mybir.dt.float32
BF16 = mybir.dt.bfloat16
AF = mybir.ActivationFunctionType
ALU = mybir.AluOpType

L = 4096
D = 1024
NT = 32
NF = 8192
EPS = 1e-6
TWO_PI = 2.0 * np.pi
MAGIC = 1.5 * 2 ** 23
NEG = -30000.0


class Trk:
    def __init__(self, nc, es):
        self.nc = nc
        self.eng = {"pe": nc.tensor, "act": nc.scalar, "dve": nc.vector, "pool": nc.gpsimd, "sp": nc.sync}
        self.esem = {e: [es.enter_context(nc.semaphore("E_" + e)), 0] for e in self.eng}
        self.bar = [es.enter_context(nc.semaphore("BAR")), 0]
        self.known = {e: {} for e in self.eng}
        self.lastw = {}
        self.readers = {}
        self.dsem = {}
        self.dfree = []
        self.ndsem = 0
        self.es = es
        self.sems = {}
        self.same_engine_sync = True

    def _wait(self, e, deps):
        own = self.esem[e][0]
        for sem, val in deps:
            if sem is own and (e in ("pe", "sp") or not self.same_engine_sync):
                continue
            k = id(sem)
            if self.known[e].get(k, 0) < val:
                self.eng[e].wait_ge(sem, val)
                self.known[e][k] = val

    def _deps(self, reads, writes, own=None):
        d = []
        for k in reads:
            if k in self.lastw:
                d.append(self.lastw[k])
        for k in writes:
            if k in self.lastw and self.lastw[k][0] is not own:
                d.append(self.lastw[k])
            d.extend(self.readers.get(k, {}).values())
        return d

    def _commit(self, ev, reads, writes):
        for k in reads:
            r = self.readers.setdefault(k, {})
            if r.get(id(ev[0]), (None, 0))[1] < ev[1]:
                r[id(ev[0])] = ev
        for k in writes:
            self.lastw[k] = ev
            self.readers[k] = {}

    def op(self, e, fn, reads=(), writes=()):
        self._wait(e, self._deps(reads, writes))
        ins = fn()
        s = self.esem[e]
        s[1] += 1
        ins.then_inc(s[0], 1)
        self._commit((s[0], s[1]), reads, writes)

    def dma(self, q, out, in_, reads=(), writes=(), semkey=None, **kw):
        if semkey not in self.dsem:
            if self.dfree:
                self.dsem[semkey] = self.dfree.pop()
            else:
                self.ndsem += 1
                self.dsem[semkey] = [self.es.enter_context(self.nc.semaphore("D%d" % self.ndsem)), 0]
        ds = self.dsem[semkey]
        self._wait(q, self._deps(reads, writes, own=ds[0]))
        ins = self.eng[q].dma_start(out=out, in_=in_, **kw)
        ds[1] += 16
        ins.then_inc(ds[0], 16)
        self._commit((ds[0], ds[1]), reads, writes)

    def barrier(self):
        evs = [(s[0], s[1]) for e, s in self.esem.items() if s[1] > 0 and e != "sp"]
        evs += [(s[0], s[1]) for s in self.dsem.values() if s[1] > 0]
        self._wait("sp", evs)
        self.bar[1] += 1
        self.nc.sync.sem_inc(self.bar[0], 1)
        for e in self.eng:
            if e != "sp":
                self.eng[e].wait_ge(self.bar[0], self.bar[1])
                for sem, val in evs:
                    self.known[e][id(sem)] = max(self.known[e].get(id(sem), 0), val)
        self.lastw = {}
        self.readers = {}
        self.dfree.extend(self.dsem.values())
        self.dsem = {}


def _filter_consts():
    f32 = np.float32
    t = np.linspace(0.0, 1.0, L, dtype=f32)[:, None]
    ang = (f32(2.0 * np.pi) * np.arange(L, dtype=f32)[:, None] / f32(L)).astype(f32)
    bands = np.linspace(1e-4, 15, 16, dtype=f32)[None, :]
    feats = np.concatenate([t, np.cos(bands * ang), -np.sin(bands * ang)], axis=-1).astype(f32)
    featsT = np.ascontiguousarray(feats.T)
    tneg = np.ascontiguousarray((-t[:, 0]).reshape(NT, 128).T)
    max_decay = np.log(1e-2) / 0.3
    min_decay = np.log(1e-2) / 1.5
    absd = np.abs(np.linspace(min_decay, max_decay, D, dtype=f32)).reshape(1, D).astype(f32)
    return featsT, tneg.astype(f32), absd


def _dft_tables():
    th = 2.0 * np.pi / NF
    u = np.arange(2048)
    sv = np.arange(2048)
    fe, fo = 2 * u, 2 * u + 1

    def co(f, kind, w=None):
        ang = ((f[:, None] * sv[None, :]) % NF) * th
        m = np.cos(ang) if kind == "c" else -np.sin(ang)
        return m if w is None else m * w[:, None]
    Fm = np.concatenate([co(fe, "c"), co(fo, "c"), co(fe, "s"), co(fo, "s")], 0)
    Ftab = np.ascontiguousarray(Fm.reshape(64, 128, 16, 128).transpose(0, 3, 2, 1)).astype(ml_dtypes.bfloat16).reshape(64, 128, 2048)
    w_e = np.where(fe == 0, 1.0 / NF, 2.0 / NF)
    w_o = np.full(2048, 2.0 / NF)
    Gm = np.concatenate([co(fe, "c", w_e), co(fo, "s", w_o), co(fo, "c", w_o), co(fe, "s", w_e)], 0)
    Gtab = np.ascontiguousarray(Gm.reshape(8, 8, 128, 4, 512).transpose(3, 0, 2, 1, 4)).astype(ml_dtypes.bfloat16).reshape(4, 8, 128, 4096)
    gcol = np.concatenate([w_e * np.cos(np.pi * fe / 2.0), w_o * (-np.sin(np.pi * fo / 2.0))])
    gcol = np.ascontiguousarray(gcol.reshape(32, 128).T).astype(ml_dtypes.bfloat16)
    sgn = np.where(np.arange(128) % 2 == 0, 1.0, -1.0)
    small = np.zeros((128, 2048 + 512), np.float32)
    small[0, 0:128] = sgn
    small[0, 128:256] = -sgn
    small[0, 256] = 1.0
    small[0, 257] = 1.0 / NF
    small[:, 258] = sgn
    small[0, 512:512 + 2048] = (1.0 / NF) * np.where(np.arange(2048) % 2 == 0, 1.0, -1.0)
    E1 = np.zeros((128, 128), np.float32)
    for q in range(1, 128):
        E1[128 - q, q] = 1.0
    E2 = np.zeros((128, 128), np.float32)
    E2[0, 0] = 1.0
    ex = np.concatenate([E1, E2], 1)
    return Ftab, Gtab, gcol, small.astype(ml_dtypes.bfloat16), ex.astype(ml_dtypes.bfloat16)


def _natten_consts():
    cols = np.arange(64)
    cs = np.clip(cols - 8, 0, 48)
    kc = np.arange(64)
    inwin = (kc[None, :] >= cs[:, None]) & (kc[None, :] < cs[:, None] + 16)
    mask = np.where(inwin, 0.0, NEG).astype(np.float32)
    m = np.tile(mask[None, :, None, None, :], (2, 1, 14, 2, 1)).reshape(128, 14 * 128)
    idx = np.clip(15 + kc[None, :] - cols[:, None], 0, 30)
    return m.astype(np.float32), idx


def _pool_consts():
    t = np.arange(L)
    rows = []
    for w in (2, 4, 8, 16):
        lo = np.clip(t - w // 2, 0, L)
        hi = np.clip(t + w // 2, 0, L)
        rows.append(1.0 / (hi - lo).astype(np.float32))
    return np.stack(rows).astype(np.float32)


class Builder:
    def __init__(self, nlayers=4, debug=False):
        self.nlayers = nlayers
        self.debug = debug
        self.nc = bass.Bass("TRN2", target_bir_lowering=False)
        self.es = contextlib.ExitStack()
        self.T = Trk(self.nc, self.es)
        self.inp = {}

    def din(self, name, shape, dt=F32):
        ap = self.nc.dram_tensor(name, list(shape), dt, kind="ExternalInput").ap()
        self.inp[name] = ap
        return ap

    def dscr(self, name, shape, dt):
        return self.nc.dram_tensor(name, list(shape), dt, kind="Internal").ap()

    def sb(self, st, name, shape, dt=F32):
        self._uid = getattr(self, "_uid", 0) + 1
        return st.enter_context(self.nc.sbuf_tensor("%s_%d" % (name, self._uid), list(shape), dt))

    def ps(self, st, name, shape=(128, 512), dt=F32):
        self._uid = getattr(self, "_uid", 0) + 1
        return st.enter_context(self.nc.psum_tensor("%s_%d" % (name, self._uid), list(shape), dt))

    def declare(self):
        d = self.din
        self.x = d("x", [L, D])
        self.p = d("p", [4, L, 256])
        self.norm_g = d("norm_g", [4, D])
        self.final_g = d("final_g", [1, D])
        self.ple_g = d("ple_g", [4, D])
        self.ple_up = d("ple_up", [4, 256, D])
        self.ple_gate_w = d("ple_gate_w", [4, D, D])
        self.ev_w_in = d("ev_w_in", [2, D, 7168])
        self.ev_w_out = d("ev_w_out", [2, 2048, D])
        self.od_w_in = d("od_w_in", [2, D, 6144])
        self.od_w_out = d("od_w_out", [2, 2048, D])
        self.convw = d("convw", [2, 128, 72])
        self.convb = d("convb", [2, 128, 24])
        self.hy_w0 = d("hy_w0", [2, 33, 64])
        self.hy_w1 = d("hy_w1", [2, 64, 64])
        self.hy_w2 = d("hy_w2", [2, 64, 64])
        self.hy_cols = d("hy_cols", [2, 64, 4])
        self.hy_wout = d("hy_wout", [2, 64, 2048])
        self.hy_d = d("hy_d", [2, D])
        self.gm_norm_g = d("gm_norm_g", [2, D])
        self.gm_wsT = d("gm_wsT", [2, 128, 8 * 128])
        self.gm_bs = d("gm_bs", [2, 8, 128])
        self.pool_w = d("pool_w", [2, 4, 256, 256])
        self.poolcols = d("poolcols", [2, 128, 16])
        self.rpbT = d("rpbT", [2, 8, 128, 14 * 128])
        self.c_ident = d("c_ident", [128, 128])
        self.c_featsT = d("c_featsT", [33, L])
        self.c_tneg = d("c_tneg", [128, NT])
        self.c_absd = d("c_absd", [1, D])
        self.c_F = d("c_F", [64, 128, 2048], BF16)
        self.c_G = d("c_G", [4, 8, 128, 4096], BF16)
        self.c_gcol = d("c_gcol", [128, 32], BF16)
        self.c_small = d("c_small", [128, 2560], BF16)
        self.c_ex = d("c_ex", [128, 256], BF16)
        self.c_mask = d("c_mask", [128, 14 * 128])
        self.c_invcnt = d("c_invcnt", [4, L])
        if self.debug:
            self.out = self.nc.dram_tensor("out", [L, D], F32, kind="ExternalOutput").ap()
        else:
            self.out = self.nc.dram_tensor("out", [L, D], F32, kind="ExternalOutput").ap()
        self.hbuf = self.dscr("hbuf", [L, D], F32)
        self.YT = self.dscr("YT", [2048, L], BF16)
        self.Abuf = self.dscr("Abuf", [D, L], F32)
        self.zbuf = self.dscr("zbuf", [D, L], BF16)
        self.kebuf = self.dscr("kebuf", [L, D], BF16)
        self.kobuf = self.dscr("kobuf", [L, D], BF16)
        self.Hre = self.dscr("Hre", [2, 32, 128, 512], F32)
        self.Him = self.dscr("Him", [2, 32, 128, 512], F32)
        self.Hny = self.dscr("Hny", [2, 1, 512], F32)

    def rms_rstd(self, st_key, src, junk, ss, sd, rstd):
        T, nc = self.T, self.nc
        T.op("act", lambda: nc.scalar.activation(out=junk, in_=src, func=AF.Square, accum_out=ss),
             reads=[st_key], writes=[st_key + "_junk", st_key + "_ss"])
        T.op("act", lambda: nc.scalar.activation(out=sd, in_=ss, func=AF.Sqrt, bias=self.eps_col[:, 0:1], scale=1.0 / D),
             reads=[st_key + "_ss", "eps"], writes=[st_key + "_sd"])
        T.op("dve", lambda: nc.vector.reciprocal(out=rstd, in_=sd), reads=[st_key + "_sd"], writes=[st_key + "_rstd"])

    def load_row_bcast(self, dst, src_row, key):
        self.T.dma("sp", dst, src_row.partition_broadcast(128), writes=[key], semkey=key)

    def transposes_f32(self, src, src_key, psT, ps_keys, nblk):
        T, nc = self.T, self.nc
        for h0 in range(0, nblk, 4):
            nb = min(4, nblk - h0)
            bank = psT[h0 // 4]

            def f(h0=h0, nb=nb, bank=bank):
                ins = None
                for q in range(nb):
                    ins = nc.tensor.transpose(out=bank[:, q * 128:(q + 1) * 128],
                                              in_=src[:, (h0 + q) * 128:(h0 + q + 1) * 128],
                                              identity=self.ident[:])
                return ins
            T.op("pe", f, reads=[src_key, "ident"], writes=[ps_keys[h0 // 4]])

    def load_globals(self):
        es, T, nc = self.es, self.T, self.nc
        self.ident = self.sb(es, "ident", [128, 128], F32)
        self.ident_bf = self.sb(es, "ident_bf", [128, 128], BF16)
        self.ones_bf = self.sb(es, "ones_bf", [128, 64], BF16)
        self.ones_f = self.sb(es, "ones_f", [128, 128], F32)
        self.eps_col = self.sb(es, "eps_col", [128, 1], F32)
        T.dma("sp", self.ident[:], self.c_ident[:, :], writes=["ident"], semkey="ident")
        T.dma("pool", self.ident_bf[:], self.c_ident[:, :], writes=["ident_bf"], semkey="ident_bf")
        T.op("dve", lambda: nc.vector.memset(self.ones_bf[:], 1.0), writes=["ones_bf"])
        T.op("dve", lambda: nc.vector.memset(self.ones_f[:], 1.0), writes=["ones_f"])
        T.op("dve", lambda: nc.vector.memset(self.eps_col[:], EPS), writes=["eps"])

    def head(self, li):
        T, nc = self.T, self.nc
        src = self.x if li == 0 else self.hbuf
        with contextlib.ExitStack() as st:
            grow = self.sb(st, "hd_grow", [128, D])
            self.load_row_bcast(grow[:], self.norm_g[li, :], "hd_grow")
            sets = []
            for par in range(2):
                sets.append(dict(
                    hin=self.sb(st, f"hd_hin{par}", [128, D]),
                    xh=self.sb(st, f"hd_xh{par}", [128, D]),
                    junk=self.sb(st, f"hd_junk{par}", [128, D], BF16),
                    ss=self.sb(st, f"hd_ss{par}", [128, 1]),
                    sd=self.sb(st, f"hd_sd{par}", [128, 1]),
                    rstd=self.sb(st, f"hd_rstd{par}", [128, 1]),
                    psT=[self.ps(st, f"hd_psT{par}_{i}") for i in range(2)],
                ))
            for t in range(NT):
                par = t % 2
                S = sets[par]
                k = f"hd{par}"
                T.dma("sp", S["hin"][:], src[t * 128:(t + 1) * 128, :], reads=[("h", t)], writes=[k], semkey=k)
                self.rms_rstd(k, S["hin"][:], S["junk"][:], S["ss"][:], S["sd"][:], S["rstd"][:])
                T.op("dve", lambda: nc.vector.scalar_tensor_tensor(
                    out=S["xh"][:], in0=S["hin"][:], scalar=S["rstd"][:, 0:1], in1=grow[:],
                    op0=ALU.mult, op1=ALU.mult), reads=[k, k + "_rstd", "hd_grow"], writes=[k + "_xh"])
                self.transposes_f32(S["xh"], k + "_xh", S["psT"], [k + "_ps0", k + "_ps1"], 8)
                T.op("act", lambda: nc.scalar.copy(
                    out=self.hnT[:, 0:4, t * 128:(t + 1) * 128],
                    in_=S["psT"][0][:].rearrange("p (a b) -> p a b", a=4)),
                    reads=[k + "_ps0"], writes=[("hnT", t // 4)])
                T.op("dve", lambda: nc.vector.tensor_copy(
                    out=self.hnT[:, 4:8, t * 128:(t + 1) * 128],
                    in_=S["psT"][1][:].rearrange("p (a b) -> p a b", a=4)),
                    reads=[k + "_ps1"], writes=[("hnT", t // 4)])
        T.barrier()

    def load_w(self, dst, w2d, r0, nk, c0, ncol, key):
        src = w2d[r0:r0 + nk * 128, c0:c0 + ncol].rearrange("(k p) c -> p k c", p=128)
        self.T.dma("pool", dst, src, writes=[key], semkey=key)

    def proj_fm(self, ps_ap, wblk, wkey, tg, pskey, col0=0):
        T, nc = self.T, self.nc

        def f():
            ins = None
            for k in range(8):
                ins = nc.tensor.matmul(ps_ap, lhsT=wblk[:, k, col0:col0 + 128],
                                       rhs=self.hnT[:, k, tg * 512:(tg + 1) * 512],
                                       start=(k == 0), stop=(k == 7))
            return ins
        T.op("pe", f, reads=[wkey, ("hnT", tg)], writes=[pskey])

    def tail(self, li, w_out2d, last):
        T, nc = self.T, self.nc
        src = self.x if li == 0 else self.hbuf
        with contextlib.ExitStack() as st:
            wo = self.sb(st, "tl_wo", [128, 16, D], BF16)
            gw = self.sb(st, "tl_gw", [128, 8, D], BF16)
            uw = self.sb(st, "tl_uw", [128, 2, D], BF16)
            for kk in range(0, 16, 4):
                self.load_w(wo[:, kk:kk + 4, :], w_out2d, kk * 128, 4, 0, D, f"tl_wo{kk}")
            for kk in range(0, 8, 4):
                self.load_w(gw[:, kk:kk + 4, :], self.ple_gate_w[li], kk * 128, 4, 0, D, f"tl_gw{kk}")
            self.load_w(uw[:], self.ple_up[li], 0, 2, 0, D, "tl_uw")
            wkeys = [f"tl_wo{kk}" for kk in range(0, 16, 4)]
            gkeys = [f"tl_gw{kk}" for kk in range(0, 8, 4)]
            pgrow = self.sb(st, "tl_pgrow", [128, D])
            self.load_row_bcast(pgrow[:], self.ple_g[li, :], "tl_pgrow")
            nrow = self.sb(st, "tl_nrow", [128, D])
            self.load_row_bcast(nrow[:], self.final_g[0, :] if last else self.norm_g[li + 1, :], "tl_nrow")
            Yp = [self.sb(st, f"tl_Yp{i}", [128, 16, 256], BF16) for i in range(2)]
            sets = []
            for par in range(2):
                sets.append(dict(
                    hc=self.sb(st, f"tl_hc{par}", [128, D]),
                    xh=self.sb(st, f"tl_xh{par}", [128, D]),
                    xn=self.sb(st, f"tl_xn{par}", [128, D]),
                    sig=self.sb(st, f"tl_sig{par}", [128, D]),
                    hn2=self.sb(st, f"tl_hn2{par}", [128, 8, 128], BF16),
                    pin=self.sb(st, f"tl_pin{par}", [128, 256]),
                    pT=self.sb(st, f"tl_pT{par}", [128, 2, 128], BF16),
                    ss=self.sb(st, f"tl_ss{par}", [128, 1]),
                    sd=self.sb(st, f"tl_sd{par}", [128, 1]),
                    rstd=self.sb(st, f"tl_rstd{par}", [128, 1]),
                    ss2=self.sb(st, f"tl_ss2{par}", [128, 1]),
                    sd2=self.sb(st, f"tl_sd2{par}", [128, 1]),
                    rstd2=self.sb(st, f"tl_rstd2{par}", [128, 1]),
                ))
            psM = [self.ps(st, f"tl_psM{i}") for i in range(2)]
            psT = [self.ps(st, f"tl_psT{i}") for i in range(2)]
            psG = [self.ps(st, f"tl_psG{i}") for i in range(2)]
            psU = [self.ps(st, f"tl_psU{i}") for i in range(2)]

            def load_Y(pc):
                yp = Yp[pc % 2]
                srcy = self.YT[:, pc * 256:(pc + 1) * 256].rearrange("(k p) t -> p k t", p=128)
                T.dma("sp", yp[:], srcy, reads=[("YT", pc // 2)], writes=[f"tl_Yp{pc % 2}"], semkey=f"tl_Yp{pc % 2}")

            def norm(S, k, src_key, junk, junk_key, ss, sd, rstd, grow, grow_key, dst, dst_key, sfx):
                T.op("act", lambda: nc.scalar.activation(out=junk[:], in_=S["hc"][:], func=AF.Square, accum_out=ss[:]),
                     reads=[src_key], writes=[junk_key, k + "_ss" + sfx])
                T.op("act", lambda: nc.scalar.activation(out=sd[:], in_=ss[:], func=AF.Sqrt, bias=self.eps_col[:, 0:1], scale=1.0 / D),
                     reads=[k + "_ss" + sfx, "eps"], writes=[k + "_sd" + sfx])
                T.op("dve", lambda: nc.vector.reciprocal(out=rstd[:], in_=sd[:]), reads=[k + "_sd" + sfx], writes=[k + "_rstd" + sfx])
                T.op("dve", lambda: nc.vector.scalar_tensor_tensor(out=dst[:], in0=S["hc"][:], scalar=rstd[:, 0:1], in1=grow[:],
                                                                   op0=ALU.mult, op1=ALU.mult),
                     reads=[src_key, k + "_rstd" + sfx, grow_key], writes=[dst_key])

            def stA(t):
                pc, tt = t // 2, t % 2
                if tt == 0 and pc + 1 < 16:
                    load_Y(pc + 1)
                par = t % 2
                S = sets[par]
                k = f"tl{par}"
                yp = Yp[pc % 2]
                ypk = f"tl_Yp{pc % 2}"
                T.dma("sp", S["hc"][:], src[t * 128:(t + 1) * 128, :], reads=[("h", t)], writes=[k + "_hc"], semkey=k + "_hc")
                T.dma("sp", S["pin"][:], self.p[li, t * 128:(t + 1) * 128, :], writes=[k + "_pin"], semkey=k + "_pin")
                for half in range(2):
                    def f(half=half):
                        ins = None
                        for kk in range(16):
                            ins = nc.tensor.matmul(psM[half][:], lhsT=yp[:, kk, tt * 128:(tt + 1) * 128],
                                                   rhs=wo[:, kk, half * 512:(half + 1) * 512],
                                                   start=(kk == 0), stop=(kk == 15))
                        return ins
                    T.op("pe", f, reads=[ypk] + wkeys, writes=[f"tl_psM{half}"])
                    T.op("dve", lambda half=half: nc.vector.tensor_tensor(
                        out=S["hc"][:, half * 512:(half + 1) * 512], in0=psM[half][:],
                        in1=S["hc"][:, half * 512:(half + 1) * 512], op=ALU.add),
                        reads=[f"tl_psM{half}", k + "_hc"], writes=[k + "_hc"])
                norm(S, k, k + "_hc", S["sig"], k + "_sig", S["ss"], S["sd"], S["rstd"], pgrow, "tl_pgrow", S["xh"], k + "_xh", "")

            def stB(t):
                par = t % 2
                S = sets[par]
                k = f"tl{par}"
                for half in range(2):
                    def f(half=half):
                        ins = None
                        for kk in range(8):
                            ins = nc.tensor.matmul(psG[half][:], lhsT=S["hn2"][:, kk, :],
                                                   rhs=gw[:, kk, half * 512:(half + 1) * 512],
                                                   start=(kk == 0), stop=(kk == 7))
                        return ins
                    T.op("pe", f, reads=[k + "_hn2"] + gkeys, writes=[f"tl_psG{half}"])
                    T.op("act", lambda half=half: nc.scalar.activation(
                        out=S["sig"][:, half * 512:(half + 1) * 512], in_=psG[half][:], func=AF.Sigmoid),
                        reads=[f"tl_psG{half}"], writes=[k + "_sig"])

            def stC(t):
                par = t % 2
                S = sets[par]
                k = f"tl{par}"
                self.transposes_f32(S["xh"], k + "_xh", psT, ["tl_psT0", "tl_psT1"], 8)
                T.op("act", lambda: nc.scalar.copy(out=S["hn2"][:, 0:4, :],
                                                   in_=psT[0][:].rearrange("p (a b) -> p a b", a=4)),
                     reads=["tl_psT0"], writes=[k + "_hn2"])
                T.op("dve", lambda: nc.vector.tensor_copy(out=S["hn2"][:, 4:8, :],
                                                          in_=psT[1][:].rearrange("p (a b) -> p a b", a=4)),
                     reads=["tl_psT1"], writes=[k + "_hn2"])

                def fp():
                    ins = None
                    for q in range(2):
                        ins = nc.tensor.transpose(out=psM[0][:, q * 128:(q + 1) * 128],
                                                  in_=S["pin"][:, q * 128:(q + 1) * 128], identity=self.ident[:])
                    return ins
                T.op("pe", fp, reads=[k + "_pin", "ident"], writes=["tl_psM0"])
                T.op("act", lambda: nc.scalar.copy(out=S["pT"][:],
                                                   in_=psM[0][:, 0:256].rearrange("p (a b) -> p a b", a=2)),
                     reads=["tl_psM0"], writes=[k + "_pT"])

            def stD(t):
                par = t % 2
                S = sets[par]
                k = f"tl{par}"
                for half in range(2):
                    def f(half=half):
                        ins = None
                        for kk in range(2):
                            ins = nc.tensor.matmul(psU[half][:], lhsT=S["pT"][:, kk, :],
                                                   rhs=uw[:, kk, half * 512:(half + 1) * 512],
                                                   start=(kk == 0), stop=(kk == 1))
                        return ins
                    T.op("pe", f, reads=[k + "_pT", "tl_uw"], writes=[f"tl_psU{half}"])
                    T.op("dve", lambda half=half: nc.vector.tensor_tensor(
                        out=S["sig"][:, half * 512:(half + 1) * 512], in0=psU[half][:],
                        in1=S["sig"][:, half * 512:(half + 1) * 512], op=ALU.mult),
                        reads=[f"tl_psU{half}", k + "_sig"], writes=[k + "_sig"])
                T.op("pool", lambda: nc.gpsimd.tensor_tensor(out=S["hc"][:], in0=S["hc"][:], in1=S["sig"][:], op=ALU.add),
                     reads=[k + "_hc", k + "_sig"], writes=[k + "_hc"])
                if not last:
                    T.dma("pool", self.hbuf[t * 128:(t + 1) * 128, :], S["hc"][:], reads=[k + "_hc"], writes=[("h", t)],
                          semkey=k + "_hcst")
                norm(S, k, k + "_hc", S["xn"], k + "_xn", S["ss2"], S["sd2"], S["rstd2"], nrow, "tl_nrow", S["xn"], k + "_xn", "2")
                if last:
                    T.dma("pool", self.out[t * 128:(t + 1) * 128, :], S["xn"][:], reads=[k + "_xn"], writes=[("out", t)],
                          semkey=k + "_xnst")

            def stE(t):
                par = t % 2
                S = sets[par]
                k = f"tl{par}"
                self.transposes_f32(S["xn"], k + "_xn", psT, ["tl_psT0", "tl_psT1"], 8)
                T.op("act", lambda: nc.scalar.copy(out=self.hnT[:, 0:4, t * 128:(t + 1) * 128],
                                                   in_=psT[0][:].rearrange("p (a b) -> p a b", a=4)),
                     reads=["tl_psT0"], writes=[("hnT", t // 4)])
                T.op("dve", lambda: nc.vector.tensor_copy(out=self.hnT[:, 4:8, t * 128:(t + 1) * 128],
                                                          in_=psT[1][:].rearrange("p (a b) -> p a b", a=4)),
                     reads=["tl_psT1"], writes=[("hnT", t // 4)])

            load_Y(0)
            stA(0)
            stC(0)
            for t in range(NT):
                if t + 1 < NT:
                    stA(t + 1)
                if t >= 1 and not last:
                    stE(t - 1)
                stB(t)
                if t + 1 < NT:
                    stC(t + 1)
                stD(t)
            if not last:
                stE(NT - 1)
        T.barrier()

    def even_pre(self, j):
        T, nc = self.T, self.nc
        w2d = self.ev_w_in[j]
        with contextlib.ExitStack() as st:
            cw = self.sb(st, "ep_cw", [128, 72])
            cb = self.sb(st, "ep_cb", [128, 24])
            T.dma("sp", cw[:], self.convw[j], writes=["ep_cw"], semkey="ep_cw")
            T.dma("sp", cb[:], self.convb[j], writes=["ep_cb"], semkey="ep_cb")
            wb = [self.sb(st, f"ep_wb{i}", [128, 8, 512], BF16) for i in range(2)]
            raws = [self.sb(st, f"ep_raw{i}", [128, L + 2]) for i in range(2)]
            rawi = [0]
            bA = self.sb(st, "ep_bA", [128, L])
            bB = self.sb(st, "ep_bB", [128, L])
            zb = self.sb(st, "ep_zb", [128, L], BF16)
            pss = [self.ps(st, f"ep_ps{i}") for i in range(4)]
            for i in range(2):
                T.op("pool", lambda i=i: nc.gpsimd.memset(raws[i][:, 0:1], 0.0), writes=[f"ep_raw{i}"])
                T.op("pool", lambda i=i: nc.gpsimd.memset(raws[i][:, L + 1:L + 2], 0.0), writes=[f"ep_raw{i}"])

            def load_wb(cc):
                for q in range(4):
                    self.load_w(wb[cc % 2][:, :, q * 128:(q + 1) * 128], w2d, 0, 8, q * 1024 + cc * 128, 128,
                                f"ep_wb{cc % 2}")

            def proj_to_raw(cc, q):
                rawi[0] ^= 1
                raw, rk = raws[rawi[0]], f"ep_raw{rawi[0]}"
                for tg in range(8):
                    pk = f"ep_ps{tg % 4}"
                    self.proj_fm(pss[tg % 4][:], wb[cc % 2], f"ep_wb{cc % 2}", tg, pk, col0=q * 128)
                    T.op("act", lambda tg=tg: nc.scalar.copy(out=raw[:, 1 + tg * 512:1 + (tg + 1) * 512], in_=pss[tg % 4][:]),
                         reads=[pk], writes=[rk])

            def conv(cc, q, dst, dkey):
                raw, rk = raws[rawi[0]], f"ep_raw{rawi[0]}"
                ch = q * 8 + cc
                w = lambda tap: cw[:, tap * 24 + ch:tap * 24 + ch + 1]
                T.op("dve", lambda: nc.vector.tensor_scalar(out=dst[:], in0=raw[:, 1:L + 1], scalar1=w(1), scalar2=cb[:, ch:ch + 1],
                                                            op0=ALU.mult, op1=ALU.add),
                     reads=[rk, "ep_cw", "ep_cb"], writes=[dkey])
                T.op("dve", lambda: nc.vector.scalar_tensor_tensor(out=dst[:], in0=raw[:, 0:L], scalar=w(0), in1=dst[:],
                                                                   op0=ALU.mult, op1=ALU.add),
                     reads=[rk, dkey], writes=[dkey])
                T.op("dve", lambda: nc.vector.scalar_tensor_tensor(out=dst[:], in0=raw[:, 2:L + 2], scalar=w(2), in1=dst[:],
                                                                   op0=ALU.mult, op1=ALU.add),
                     reads=[rk, dkey], writes=[dkey])

            load_wb(0)
            for cc in range(8):
                if cc + 1 < 8:
                    load_wb(cc + 1)
                proj_to_raw(cc, 1)
                conv(cc, 1, bA, "ep_bA")
                proj_to_raw(cc, 2)
                conv(cc, 2, bB, "ep_bB")
                T.op("pool", lambda: nc.gpsimd.tensor_tensor(out=zb[:], in0=bB[:], in1=bA[:], op=ALU.mult),
                     reads=["ep_bA", "ep_bB"], writes=["ep_zb"])
                T.dma("sp", self.zbuf[cc * 128:(cc + 1) * 128, :], zb[:], reads=["ep_zb"], writes=[("z", cc)], semkey="ep_zb")
                proj_to_raw(cc, 0)
                conv(cc, 0, bA, "ep_bA")
                for tg in range(8):
                    pk = f"ep_ps{tg % 4}"
                    self.proj_fm(pss[tg % 4][:], wb[cc % 2], f"ep_wb{cc % 2}", tg, pk, col0=3 * 128)
                    T.op("act", lambda tg=tg: nc.scalar.activation(out=bB[:, tg * 512:(tg + 1) * 512], in_=pss[tg % 4][:], func=AF.Silu),
                         reads=[pk], writes=["ep_bB"])
                T.op("pool", lambda: nc.gpsimd.tensor_tensor(out=bA[:], in0=bA[:], in1=bB[:], op=ALU.mult),
                     reads=["ep_bA", "ep_bB"], writes=["ep_bA"])
                T.dma("sp", self.Abuf[cc * 128:(cc + 1) * 128, :], bA[:], reads=["ep_bA"], writes=[("A", cc)], semkey="ep_bA")
        T.barrier()

    def even_gmlp(self, j):
        T, nc = self.T, self.nc
        w2d = self.ev_w_in[j]
        with contextlib.ExitStack() as st:
            wu = self.sb(st, "eg_wu", [128, 8, D], BF16)
            wv = self.sb(st, "eg_wv", [128, 8, D], BF16)
            wg = self.sb(st, "eg_wg", [128, 8, D], BF16)
            for kk in range(0, 8, 4):
                self.load_w(wv[:, kk:kk + 4, :], w2d, kk * 128, 4, 5120, D, f"eg_wv{kk}")
                self.load_w(wu[:, kk:kk + 4, :], w2d, kk * 128, 4, 4096, D, f"eg_wu{kk}")
                self.load_w(wg[:, kk:kk + 4, :], w2d, kk * 128, 4, 6144, D, f"eg_wg{kk}")
            wsT = self.sb(st, "eg_wsT", [128, 8 * 128], BF16)
            T.dma("pool", wsT[:], self.gm_wsT[j], writes=["eg_wsT"], semkey="eg_wsT")
            bsrow = self.sb(st, "eg_bsrow", [128, 8, 4, 128])
            for g in range(8):
                for n in range(4):
                    T.dma("sp", bsrow[:, g, n, :], self.gm_bs[j, g, :].partition_broadcast(128),
                          writes=["eg_bsrow"], semkey="eg_bsrow")
            gmg = self.sb(st, "eg_gmg", [128, D])
            self.load_row_bcast(gmg[:], self.gm_norm_g[j, :], "eg_gmg")
            vb = [self.sb(st, f"eg_vb{i}", [128, D]) for i in range(2)]
            junk = self.sb(st, "eg_junk", [128, D], BF16)
            ss = [self.sb(st, f"eg_ss{i}", [128, 1]) for i in range(2)]
            sd = [self.sb(st, f"eg_sd{i}", [128, 1]) for i in range(2)]
            rstd = [self.sb(st, f"eg_rstd{i}", [128, 1]) for i in range(2)]
            vn = [self.sb(st, f"eg_vn{i}", [128, D], BF16) for i in range(4)]
            sg = [self.sb(st, f"eg_sg{i}", [128, 512]) for i in range(2)]
            tt_ = [self.sb(st, f"eg_t{i}", [128, 512]) for i in range(2)]
            yb = [self.sb(st, f"eg_yb{i}", [128, 512], BF16) for i in range(2)]
            psV = [self.ps(st, f"eg_psV{i}") for i in range(2)]
            psS = [self.ps(st, f"eg_psS{i}") for i in range(2)]
            psU = [self.ps(st, f"eg_psU{i}") for i in range(2)]
            psG = [self.ps(st, f"eg_psG{i}") for i in range(2)]
            wvk = [f"eg_wv{kk}" for kk in range(0, 8, 4)]
            wuk = [f"eg_wu{kk}" for kk in range(0, 8, 4)]
            wgk = [f"eg_wg{kk}" for kk in range(0, 8, 4)]
            for tg in range(8):
                for n in range(4):
                    t = tg * 4 + n
                    par = n % 2
                    for half in range(2):
                        def f(half=half):
                            ins = None
                            for kk in range(8):
                                ins = nc.tensor.matmul(psV[half][:], lhsT=self.hnT[:, kk, t * 128:(t + 1) * 128],
                                                       rhs=wv[:, kk, half * 512:(half + 1) * 512],
                                                       start=(kk == 0), stop=(kk == 7))
                            return ins
                        T.op("pe", f, reads=wvk + [("hnT", tg)], writes=[f"eg_psV{half}"])
                        T.op("act", lambda half=half: nc.scalar.copy(out=vb[par][:, half * 512:(half + 1) * 512], in_=psV[half][:]),
                             reads=[f"eg_psV{half}"], writes=[f"eg_vb{par}"])
                    self.rms_rstd(f"eg_vb{par}", vb[par][:], junk[:], ss[par][:], sd[par][:], rstd[par][:])
                    T.op("dve", lambda: nc.vector.scalar_tensor_tensor(
                        out=vn[n][:], in0=vb[par][:], scalar=rstd[par][:, 0:1], in1=gmg[:], op0=ALU.mult, op1=ALU.mult),
                        reads=[f"eg_vb{par}", f"eg_vb{par}_rstd", "eg_gmg"], writes=[f"eg_vn{n}"])
                for g in range(8):
                    par = g % 2

                    def fs():
                        ins = None
                        for n in range(4):
                            ins = nc.tensor.matmul(psS[par][:, n * 128:(n + 1) * 128], lhsT=vn[n][:, g * 128:(g + 1) * 128],
                                                   rhs=wsT[:, g * 128:(g + 1) * 128], start=True, stop=True)
                        return ins
                    T.op("pe", fs, reads=[f"eg_vn{n}" for n in range(4)] + ["eg_wsT"], writes=[f"eg_psS{par}"])

                    def fu(w=wu, ps=psU):
                        ins = None
                        for kk in range(8):
                            ins = nc.tensor.matmul(ps[par][:], lhsT=w[:, kk, g * 128:(g + 1) * 128],
                                                   rhs=self.hnT[:, kk, tg * 512:(tg + 1) * 512], start=(kk == 0), stop=(kk == 7))
                        return ins
                    T.op("pe", fu, reads=wuk + [("hnT", tg)], writes=[f"eg_psU{par}"])
                    T.op("pe", lambda: fu(wg, psG), reads=wgk + [("hnT", tg)], writes=[f"eg_psG{par}"])
                    T.op("act", lambda: nc.scalar.activation(out=sg[par][:], in_=psG[par][:], func=AF.Silu),
                         reads=[f"eg_psG{par}"], writes=[f"eg_sg{par}"])
                    T.op("dve", lambda: nc.vector.tensor_tensor(out=tt_[par][:], in0=psS[par][:],
                                                                in1=bsrow[:, g, :, :].rearrange("p a b -> p (a b)"), op=ALU.add),
                         reads=[f"eg_psS{par}", "eg_bsrow"], writes=[f"eg_t{par}"])
                    T.op("dve", lambda: nc.vector.tensor_tensor(out=tt_[par][:], in0=psU[par][:], in1=tt_[par][:], op=ALU.mult),
                         reads=[f"eg_psU{par}", f"eg_t{par}"], writes=[f"eg_t{par}"])
                    T.op("pool", lambda: nc.gpsimd.tensor_tensor(out=yb[par][:], in0=tt_[par][:], in1=sg[par][:], op=ALU.mult),
                         reads=[f"eg_t{par}", f"eg_sg{par}"], writes=[f"eg_yb{par}"])
                    T.dma("sp", self.YT[1024 + g * 128:1024 + (g + 1) * 128, tg * 512:(tg + 1) * 512], yb[par][:],
                          reads=[f"eg_yb{par}"], writes=[("YT", tg)], semkey=f"eg_yb{par}")
        T.barrier()

    def even_filter(self, j, st_outer):
        T, nc = self.T, self.nc
        self.rn_row = self.sb(st_outer, "rn_row", [128, D])
        self.d_row = self.sb(st_outer, "d_row", [128, D])
        self.load_row_bcast(self.d_row[:], self.hy_d[j, :], "d_row")
        with contextlib.ExitStack() as st:
            feats = self.sb(st, "ef_feats", [33, L])
            w0 = self.sb(st, "ef_w0", [33, 64])
            w1 = self.sb(st, "ef_w1", [64, 64])
            w2 = self.sb(st, "ef_w2", [64, 64])
            cols = self.sb(st, "ef_cols", [64, 4])
            fb = self.sb(st, "ef_fb", [64, 3])
            wout = self.sb(st, "ef_wout", [64, 2048])
            absd = self.sb(st, "ef_absd", [128, D])
            tneg = self.sb(st, "ef_tneg", [128, NT])
            hA = self.sb(st, "ef_hA", [64, L])
            hB = self.sb(st, "ef_hB", [64, L])
            tmp = self.sb(st, "ef_tmp", [64, L])
            T.dma("sp", feats[:], self.c_featsT[:, :], writes=["ef_feats"], semkey="ef_feats")
            T.dma("sp", w0[:], self.hy_w0[j], writes=["ef_w0"], semkey="ef_w0")
            T.dma("sp", w1[:], self.hy_w1[j], writes=["ef_w1"], semkey="ef_w1")
            T.dma("sp", w2[:], self.hy_w2[j], writes=["ef_w2"], semkey="ef_w2")
            T.dma("sp", cols[:], self.hy_cols[j], writes=["ef_cols"], semkey="ef_cols")
            T.dma("sp", wout[:], self.hy_wout[j], writes=["ef_wout"], semkey="ef_wout")
            T.dma("sp", tneg[:], self.c_tneg[:, :], writes=["ef_tneg"], semkey="ef_tneg")
            self.load_row_bcast(absd[:], self.c_absd[0, :], "ef_absd")
            T.op("dve", lambda: nc.vector.tensor_scalar(out=fb[:], in0=cols[:, 0:3], scalar1=cols[:, 3:4], scalar2=None, op0=ALU.mult),
                 reads=["ef_cols"], writes=["ef_fb"])
            psm = [self.ps(st, f"ef_psm{i}", (64, 512)) for i in range(2)]
            srcs = [(feats, "ef_feats", w0, "ef_w0", 33), (hA, "ef_hA", w1, "ef_w1", 64), (hB, "ef_hB", w2, "ef_w2", 64)]
            dsts = [(hA, "ef_hA"), (hB, "ef_hB"), (hA, "ef_hA")]
            for li in range(3):
                src, skey, w, wkey, kdim = srcs[li]
                dst, dkey = dsts[li]
                for tg in range(8):
                    pk = f"ef_psm{tg % 2}"
                    T.op("pe", lambda tg=tg: nc.tensor.matmul(psm[tg % 2][:], lhsT=w[0:kdim, :], rhs=src[0:kdim, tg * 512:(tg + 1) * 512],
                                                              start=True, stop=True),
                         reads=[skey, wkey], writes=[pk])
                    T.op("act", lambda tg=tg: nc.scalar.activation(out=dst[:, tg * 512:(tg + 1) * 512], in_=psm[tg % 2][:],
                                                                   func=AF.Identity, bias=fb[:, li:li + 1], scale=cols[:, 3:4]),
                         reads=[pk, "ef_fb", "ef_cols"], writes=[dkey])
                T.op("dve", lambda: nc.vector.tensor_scalar(out=tmp[:], in0=dst[:], scalar1=1.0 / TWO_PI, scalar2=MAGIC,
                                                            op0=ALU.mult, op1=ALU.add), reads=[dkey], writes=["ef_tmp"])
                T.op("dve", lambda: nc.vector.tensor_scalar(out=tmp[:], in0=tmp[:], scalar1=MAGIC, scalar2=TWO_PI,
                                                            op0=ALU.subtract, op1=ALU.mult), reads=["ef_tmp"], writes=["ef_tmp"])
                T.op("dve", lambda: nc.vector.tensor_tensor(out=dst[:], in0=dst[:], in1=tmp[:], op=ALU.subtract),
                     reads=[dkey, "ef_tmp"], writes=[dkey])
                T.op("dve", lambda: nc.vector.tensor_scalar(out=dst[:], in0=dst[:], scalar1=3.1415925, scalar2=-3.1415925,
                                                            op0=ALU.min, op1=ALU.max), reads=[dkey], writes=[dkey])
                T.op("act", lambda: nc.scalar.activation(out=dst[:], in_=dst[:], func=AF.Sin), reads=[dkey], writes=[dkey])
            h3, h3k = hA, "ef_hA"
            psk = [self.ps(st, f"ef_psk{i}") for i in range(4)]
            pss = [self.ps(st, f"ef_pss{i}") for i in range(2)]
            Es = [self.sb(st, f"ef_E{i}", [128, D]) for i in range(2)]
            k0s = [self.sb(st, f"ef_k0{i}", [128, D]) for i in range(2)]
            k1s = [self.sb(st, f"ef_k1{i}", [128, D]) for i in range(2)]
            sq0s = [self.sb(st, f"ef_sq0{i}", [128, D]) for i in range(2)]
            sq1s = [self.sb(st, f"ef_sq1{i}", [128, D]) for i in range(2)]
            sqbs = [self.sb(st, f"ef_sqb{i}", [128, D], BF16) for i in range(2)]
            ones_b = self.sb(st, "ef_onesb", [128, 128], BF16)
            hhi = self.sb(st, "ef_hhi", [64, L], BF16)
            hlo = self.sb(st, "ef_hlo", [64, L], BF16)
            whi = self.sb(st, "ef_whi", [64, 2048], BF16)
            wlo = self.sb(st, "ef_wlo", [64, 2048], BF16)
            T.op("act", lambda: nc.scalar.copy(out=hhi[:], in_=h3[:]), reads=[h3k], writes=["ef_hhi"])
            T.op("dve", lambda: nc.vector.tensor_tensor(out=hlo[:], in0=h3[:], in1=hhi[:], op=ALU.subtract),
                 reads=[h3k, "ef_hhi"], writes=["ef_hlo"])
            T.op("act", lambda: nc.scalar.copy(out=whi[:], in_=wout[:]), reads=["ef_wout"], writes=["ef_whi"])
            T.op("dve", lambda: nc.vector.tensor_tensor(out=wlo[:], in0=wout[:], in1=whi[:], op=ALU.subtract),
                 reads=["ef_wout", "ef_whi"], writes=["ef_wlo"])
            T.op("dve", lambda: nc.vector.memset(ones_b[:], 1.0), writes=["ef_onesb"])
            keb = [self.sb(st, f"ef_keb{i}", [128, D], BF16) for i in range(2)]
            kob = [self.sb(st, f"ef_kob{i}", [128, D], BF16) for i in range(2)]
            pending_ss = []
            for tj in range(NT):
                par = tj % 2
                E, k0, k1, sq0, sq1, sqb = Es[par], k0s[par], k1s[par], sq0s[par], sq1s[par], sqbs[par]
                kE, kk0, kk1, ks0, ks1, ksb_ = f"ef_E{par}", f"ef_k0{par}", f"ef_k1{par}", f"ef_sq0{par}", f"ef_sq1{par}", f"ef_sqb{par}"
                for q in range(4):
                    def fk(q=q):
                        ts_ = slice(tj * 128, (tj + 1) * 128)
                        cs_ = slice(q * 512, (q + 1) * 512)
                        nc.tensor.matmul(psk[q][:], lhsT=hhi[:, ts_], rhs=whi[:, cs_], start=True, stop=False)
                        nc.tensor.matmul(psk[q][:], lhsT=hlo[:, ts_], rhs=whi[:, cs_], start=False, stop=False)
                        return nc.tensor.matmul(psk[q][:], lhsT=hhi[:, ts_], rhs=wlo[:, cs_], start=False, stop=True)
                    T.op("pe", fk, reads=["ef_hhi", "ef_hlo", "ef_whi", "ef_wlo"], writes=[f"ef_psk{q}"])
                T.op("act", lambda: nc.scalar.activation(out=E[:], in_=absd[:], func=AF.Exp, scale=tneg[:, tj:tj + 1]),
                     reads=["ef_absd", "ef_tneg"], writes=[kE])
                for q in range(2):
                    T.op("dve", lambda q=q: nc.vector.tensor_tensor(out=k0[:, q * 512:(q + 1) * 512], in0=psk[q][:],
                                                                    in1=E[:, q * 512:(q + 1) * 512], op=ALU.mult),
                         reads=[f"ef_psk{q}", kE], writes=[kk0])
                    T.op("dve", lambda q=q: nc.vector.tensor_tensor(out=k1[:, q * 512:(q + 1) * 512], in0=psk[2 + q][:],
                                                                    in1=E[:, q * 512:(q + 1) * 512], op=ALU.mult),
                         reads=[f"ef_psk{2 + q}", kE], writes=[kk1])
                if tj == 0:
                    T.op("dve", lambda: nc.vector.memset(k1[0:1, :], 0.0), reads=[], writes=[kk1])
                T.op("pool", lambda: nc.gpsimd.tensor_tensor(out=keb[par][:], in0=k0[:], in1=k1[:], op=ALU.add),
                     reads=[kk0, kk1], writes=[f"ef_keb{par}"])
                T.op("pool", lambda: nc.gpsimd.tensor_tensor(out=kob[par][:], in0=k0[:], in1=k1[:], op=ALU.subtract),
                     reads=[kk0, kk1], writes=[f"ef_kob{par}"])
                T.dma("sp", self.kebuf[tj * 128:(tj + 1) * 128, :], keb[par][:], reads=[f"ef_keb{par}"], writes=[("ke", tj)],
                      semkey=f"ef_keb{par}")
                T.dma("sp", self.kobuf[tj * 128:(tj + 1) * 128, :], kob[par][:], reads=[f"ef_kob{par}"], writes=[("ko", tj)],
                      semkey=f"ef_kob{par}")
                T.op("act", lambda: nc.scalar.activation(out=sq0[:], in_=k0[:], func=AF.Square), reads=[kk0], writes=[ks0])
                T.op("act", lambda: nc.scalar.activation(out=sq1[:], in_=k1[:], func=AF.Square), reads=[kk1], writes=[ks1])
                T.op("dve", lambda: nc.vector.tensor_tensor(out=sqb[:], in0=sq0[:], in1=sq1[:], op=ALU.add),
                     reads=[ks0, ks1], writes=[ksb_])
                def ssmm(tjj):
                    sqb_, key_ = sqbs[tjj % 2], f"ef_sqb{tjj % 2}"
                    for q in range(2):
                        T.op("pe", lambda q=q: nc.tensor.matmul(pss[q][:], lhsT=ones_b[:], rhs=sqb_[:, q * 512:(q + 1) * 512],
                                                                start=(tjj == 0), stop=(tjj == NT - 1)),
                             reads=[key_, "ef_onesb"], writes=[f"ef_pss{q}"])
                pending_ss.append(tj)
                if len(pending_ss) > 1:
                    ssmm(pending_ss.pop(0))
            while pending_ss:
                ssmm(pending_ss.pop(0))
            for q in range(2):
                T.op("act", lambda q=q: nc.scalar.activation(out=self.rn_row[:, q * 512:(q + 1) * 512], in_=pss[q][:], func=AF.Sqrt,
                                                             bias=self.eps_col[:, 0:1], scale=1.0),
                     reads=[f"ef_pss{q}", "eps"], writes=["rn_row"])
            T.op("dve", lambda: nc.vector.reciprocal(out=self.rn_row[:], in_=self.rn_row[:]), reads=["rn_row"], writes=["rn_row"])
        T.barrier()

    def _fwd_pairs(self, st, half, srcs, x2048, small, consume, tag):
        T, nc = self.T, self.nc
        FA = [self.sb(st, f"ff_FA{i}", [128, 2048], BF16) for i in range(2)]
        FB = [self.sb(st, f"ff_FB{i}", [128, 2048], BF16) for i in range(2)]
        psA = [self.ps(st, f"ff_psA{i}") for i in range(2)]
        psB = [self.ps(st, f"ff_psB{i}") for i in range(2)]
        psNy = self.ps(st, "ff_psNy")

        def chans(pi):
            par, i = pi // 16, pi % 16
            return (i, 32 + i) if par == 0 else (16 + i, 48 + i)

        def loadF(pi):
            b = pi % 2
            ca, cb = chans(pi)
            T.dma("sp", FA[b][:], self.c_F[ca], writes=[f"ff_FA{b}"], semkey=f"ff_FA{b}")
            T.dma("sp", FB[b][:], self.c_F[cb], writes=[f"ff_FB{b}"], semkey=f"ff_FB{b}")

        def group(ps, Fb, src, spec_lhsT, xrow):
            ins = None
            for jj in range(16):
                ins = nc.tensor.matmul(ps, lhsT=Fb[:, jj * 128:(jj + 1) * 128], rhs=src[:, jj, :],
                                       start=(jj == 0), stop=(jj == 15 and spec_lhsT is None))
            if spec_lhsT is not None:
                ins = nc.tensor.matmul(ps, lhsT=spec_lhsT, rhs=xrow, start=False, stop=True)
            return ins

        srcP, keyP = srcs["AP"]

        def fny():
            for jj in range(16):
                nc.tensor.matmul(psNy[0:1, :], lhsT=small[:, 258:259], rhs=srcP[:, jj, :], start=(jj == 0), stop=False)
            return nc.tensor.matmul(psNy[0:1, :], lhsT=small[0:1, 256:257], rhs=x2048["A"][0], start=False, stop=True)
        T.op("pe", fny, reads=[keyP, x2048["A"][1], "ff_small"], writes=["ff_psNy"])
        consume(-1, 0, psNy, None)
        loadF(0)
        for pi in range(32):
            if pi + 1 < 32:
                loadF(pi + 1)
            b = pi % 2
            par = pi // 16
            if par == 0:
                sa, ka = srcs["AP"]
                sbm, kb = srcs["BM"]
                T.op("pe", lambda: group(psA[b][:], FA[b], sa, small[0:1, 0:128], x2048["A"][0]),
                     reads=[f"ff_FA{b}", ka, x2048["A"][1], "ff_small"], writes=[f"ff_psA{b}"])
                T.op("pe", lambda: group(psB[b][:], FB[b], sbm, None, None), reads=[f"ff_FB{b}", kb], writes=[f"ff_psB{b}"])
            else:
                sa, ka = srcs["AM"]
                sbp, kb = srcs["BP"]
                T.op("pe", lambda: group(psA[b][:], FA[b], sa, None, None), reads=[f"ff_FA{b}", ka], writes=[f"ff_psA{b}"])
                T.op("pe", lambda: group(psB[b][:], FB[b], sbp, small[0:1, 128:256], x2048["B"][0]),
                     reads=[f"ff_FB{b}", kb, x2048["B"][1], "ff_small"], writes=[f"ff_psB{b}"])
            consume(pi, b, psA[b], psB[b])

    def even_fft(self, j):
        T, nc = self.T, self.nc
        with contextlib.ExitStack() as stc:
            small = self.sb(stc, "ff_small", [128, 2560], BF16)
            gcol = self.sb(stc, "ff_gcol", [128, 32], BF16)
            ex = self.sb(stc, "ff_ex", [128, 256], BF16)
            T.dma("sp", small[:], self.c_small[:, :], writes=["ff_small"], semkey="ff_small")
            T.dma("sp", gcol[:], self.c_gcol[:, :], writes=["ff_gcol"], semkey="ff_gcol")
            T.dma("sp", ex[:], self.c_ex[:, :], writes=["ff_ex"], semkey="ff_ex")
            for half in range(2):
                self._fft_half(j, half, small, gcol, ex)

    def _fft_half(self, j, half, small, gcol, ex):
        T, nc = self.T, self.nc
        c0 = half * 512
        rn = self.rn_row[:, c0:c0 + 512]
        dr = self.d_row[:, c0:c0 + 512]
        with contextlib.ExitStack() as st:
            ksb = {}
            for nm, buf in (("ke", self.kebuf), ("ko", self.kobuf)):
                ksb[nm] = self.sb(st, f"ff_{nm}", [128, NT, 512], BF16)
                for q in range(0, NT, 8):
                    T.dma("sp", ksb[nm][:, q:q + 8, :], buf[q * 128:(q + 8) * 128, c0:c0 + 512].rearrange("(j p) c -> p j c", p=128),
                          writes=[f"ff_{nm}"], semkey=f"ff_{nm}")
            fold = {}
            for nm in ("pe", "me", "po", "mo"):
                fold[nm] = self.sb(st, f"ff_{nm}", [128, 16, 512], BF16)
            with contextlib.ExitStack() as stR:
                psR = [self.ps(stR, f"ff_psR{i}") for i in range(2)]
                n = 0
                for nm, Pn, Mn in (("ke", "pe", "me"), ("ko", "po", "mo")):
                    src = ksb[nm]
                    for jj in range(16):
                        b = n % 2
                        n += 1

                        def frev():
                            ins = nc.tensor.matmul(psR[b][:], lhsT=ex[:, 0:128], rhs=src[:, 31 - jj, :], start=True, stop=(jj == 0))
                            if jj >= 1:
                                ins = nc.tensor.matmul(psR[b][:], lhsT=ex[:, 128:256], rhs=src[:, 32 - jj, :], start=False, stop=True)
                            return ins
                        T.op("pe", frev, reads=[f"ff_{nm}", "ff_ex"], writes=[f"ff_psR{b}"])
                        T.op("dve", lambda: nc.vector.tensor_tensor(out=fold[Pn][:, jj, :], in0=psR[b][:], in1=src[:, jj, :], op=ALU.add),
                             reads=[f"ff_psR{b}", f"ff_{nm}"], writes=[f"ff_{Pn}"])
                        T.op("dve", lambda: nc.vector.tensor_tensor(out=fold[Mn][:, jj, :], in0=src[:, jj, :], in1=psR[b][:], op=ALU.subtract),
                             reads=[f"ff_psR{b}", f"ff_{nm}"], writes=[f"ff_{Mn}"])
            hr = [self.sb(st, f"ff_hr{i}", [128, 512]) for i in range(2)]
            hi = [self.sb(st, f"ff_hi{i}", [128, 512]) for i in range(2)]
            hn = self.sb(st, "ff_hn", [1, 512])

            def consume_i(pi, b, pA, pB):
                if pi < 0:
                    T.op("dve", lambda: nc.vector.tensor_tensor(out=hn[:], in0=pA[0:1, :], in1=rn[0:1, :], op=ALU.mult),
                         reads=["ff_psNy", "rn_row"], writes=["ff_hn"])
                    T.op("dve", lambda: nc.vector.tensor_tensor(out=hn[:], in0=hn[:], in1=dr[0:1, :], op=ALU.add),
                         reads=["ff_hn", "d_row"], writes=["ff_hn"])
                    T.dma("sp", self.Hny[half], hn[:], reads=["ff_hn"], writes=[("Hny", half)], semkey="ff_hn")
                    return
                T.op("dve", lambda: nc.vector.tensor_tensor(out=hr[b][:], in0=pA[:], in1=rn, op=ALU.mult),
                     reads=[f"ff_psA{b}", "rn_row"], writes=[f"ff_hr{b}"])
                T.op("pool", lambda: nc.gpsimd.tensor_tensor(out=hr[b][:], in0=hr[b][:], in1=dr, op=ALU.add),
                     reads=[f"ff_hr{b}", "d_row"], writes=[f"ff_hr{b}"])
                T.op("dve", lambda: nc.vector.tensor_tensor(out=hi[b][:], in0=pB[:], in1=rn, op=ALU.mult),
                     reads=[f"ff_psB{b}", "rn_row"], writes=[f"ff_hi{b}"])
                T.dma("sp", self.Hre[half, pi], hr[b][:], reads=[f"ff_hr{b}"], writes=[("Hre", half, pi)], semkey=f"ff_hr{b}")
                T.dma("sp", self.Him[half, pi], hi[b][:], reads=[f"ff_hi{b}"], writes=[("Him", half, pi)], semkey=f"ff_hi{b}")

            srcs = dict(AP=(fold["pe"], "ff_pe"), AM=(fold["me"], "ff_me"), BP=(fold["po"], "ff_po"), BM=(fold["mo"], "ff_mo"))
            x2048 = dict(A=(ksb["ke"][0:1, 16, :], "ff_ke"), B=(ksb["ko"][0:1, 16, :], "ff_ko"))
            self._fwd_pairs(st, half, srcs, x2048, small, consume_i, "i")
        T.barrier()
        with contextlib.ExitStack() as st:
            Y = self.sb(st, "ff_Y", [128, 64, 512], BF16)
            ynq = self.sb(st, "ff_ynq", [1, 512], BF16)
            with contextlib.ExitStack() as st2:
                pT = self.sb(st2, "ff_pT", [128, 16, 512], BF16)
                mT = self.sb(st2, "ff_mT", [128, 16, 512], BF16)
                z2048 = self.sb(st2, "ff_z2048", [1, 512], BF16)
                zf = self.sb(st2, "ff_zf", [128, L], BF16)
                pf = self.sb(st2, "ff_pf", [128, 2048], BF16)
                mf = self.sb(st2, "ff_mf", [128, 2048], BF16)
                with contextlib.ExitStack() as st3:
                    psTb = [self.ps(st3, f"ff_psT{i}", (128, 512), BF16) for i in range(2)]
                    nt = 0
                    for cs in range(4):
                        T.dma("sp", zf[:], self.zbuf[c0 + cs * 128:c0 + (cs + 1) * 128, :], reads=[("z", half * 4 + cs)],
                              writes=["ff_zf"], semkey="ff_zf")
                        T.op("dve", lambda: nc.vector.tensor_tensor(out=pf[:, 1:2048], in0=zf[:, 1:2048], in1=zf[:, 4095:2048:-1], op=ALU.add),
                             reads=["ff_zf"], writes=["ff_pf"])
                        T.op("dve", lambda: nc.vector.tensor_copy(out=pf[:, 0:1], in_=zf[:, 0:1]), reads=["ff_zf"], writes=["ff_pf"])
                        T.op("dve", lambda: nc.vector.tensor_tensor(out=mf[:, 1:2048], in0=zf[:, 1:2048], in1=zf[:, 4095:2048:-1], op=ALU.subtract),
                             reads=["ff_zf"], writes=["ff_mf"])
                        T.op("dve", lambda: nc.vector.tensor_copy(out=mf[:, 0:1], in_=zf[:, 0:1]), reads=["ff_zf"], writes=["ff_mf"])
                        for srcf, skey, dstT, dkey in ((pf, "ff_pf", pT, "ff_pT"), (mf, "ff_mf", mT, "ff_mT")):
                            for j0 in range(0, 16, 4):
                                pb = nt % 2
                                nt += 1

                                def ftr():
                                    ins = None
                                    for q in range(4):
                                        ins = nc.tensor.transpose(out=psTb[pb][:, q * 128:(q + 1) * 128],
                                                                  in_=srcf[:, (j0 + q) * 128:(j0 + q + 1) * 128], identity=self.ident_bf[:])
                                    return ins
                                T.op("pe", ftr, reads=[skey, "ident_bf"], writes=[f"ff_psT{pb}"])
                                if pb == 0:
                                    T.op("act", lambda: nc.scalar.copy(out=dstT[:, j0:j0 + 4, cs * 128:(cs + 1) * 128],
                                                                       in_=psTb[pb][:].rearrange("p (a b) -> p a b", a=4)),
                                         reads=[f"ff_psT{pb}"], writes=[dkey])
                                else:
                                    T.op("dve", lambda: nc.vector.tensor_copy(out=dstT[:, j0:j0 + 4, cs * 128:(cs + 1) * 128],
                                                                              in_=psTb[pb][:].rearrange("p (a b) -> p a b", a=4)),
                                         reads=[f"ff_psT{pb}"], writes=[dkey])
                        pb = nt % 2
                        nt += 1
                        T.op("pe", lambda: nc.tensor.transpose(out=psTb[pb][0:1, 0:128], in_=zf[:, 2048:2049], identity=self.ident_bf[:]),
                             reads=["ff_zf", "ident_bf"], writes=[f"ff_psT{pb}"])
                        T.op("act", lambda: nc.scalar.copy(out=z2048[0:1, cs * 128:(cs + 1) * 128], in_=psTb[pb][0:1, 0:128]),
                             reads=[f"ff_psT{pb}"], writes=["ff_z2048"])
                hr = [self.sb(st2, f"ff_hr{i}", [128, 512]) for i in range(2)]
                hi = [self.sb(st2, f"ff_hi{i}", [128, 512]) for i in range(2)]
                hn = self.sb(st2, "ff_hn", [1, 512])
                t1 = self.sb(st2, "ff_t1", [128, 512])
                t2 = self.sb(st2, "ff_t2", [128, 512])
                t3 = self.sb(st2, "ff_t3", [128, 512])
                t4 = self.sb(st2, "ff_t4", [128, 512])
                T.dma("sp", hn[:], self.Hny[half], reads=[("Hny", half)], writes=["ff_hn"], semkey="ff_hn")
                loaded = set()

                def loadH(pi):
                    b = pi % 2
                    T.dma("sp", hr[b][:], self.Hre[half, pi], reads=[("Hre", half, pi)], writes=[f"ff_hr{b}"], semkey=f"ff_hr{b}")
                    T.dma("sp", hi[b][:], self.Him[half, pi], reads=[("Him", half, pi)], writes=[f"ff_hi{b}"], semkey=f"ff_hi{b}")

                def consume_ii(pi, b, pA, pB):
                    if pi < 0:
                        T.op("dve", lambda: nc.vector.tensor_tensor(out=ynq[:], in0=pA[0:1, :], in1=hn[:], op=ALU.mult),
                             reads=["ff_psNy", "ff_hn"], writes=["ff_ynq"])
                        loadH(0)
                        return
                    par, i = pi // 16, pi % 16
                    ca, cb = (i, 32 + i) if par == 0 else (16 + i, 48 + i)
                    T.op("dve", lambda: nc.vector.tensor_tensor(out=t1[:], in0=pA[:], in1=hr[b][:], op=ALU.mult),
                         reads=[f"ff_psA{b}", f"ff_hr{b}"], writes=["ff_t1"])
                    T.op("dve", lambda: nc.vector.tensor_tensor(out=t2[:], in0=pB[:], in1=hi[b][:], op=ALU.mult),
                         reads=[f"ff_psB{b}", f"ff_hi{b}"], writes=["ff_t2"])
                    T.op("dve", lambda: nc.vector.tensor_tensor(out=t3[:], in0=pA[:], in1=hi[b][:], op=ALU.mult),
                         reads=[f"ff_psA{b}", f"ff_hi{b}"], writes=["ff_t3"])
                    T.op("dve", lambda: nc.vector.tensor_tensor(out=t4[:], in0=pB[:], in1=hr[b][:], op=ALU.mult),
                         reads=[f"ff_psB{b}", f"ff_hr{b}"], writes=["ff_t4"])
                    if pi + 1 < 32:
                        loadH(pi + 1)
                    T.op("pool", lambda: nc.gpsimd.tensor_tensor(out=Y[:, ca, :], in0=t1[:], in1=t2[:], op=ALU.subtract),
                         reads=["ff_t1", "ff_t2"], writes=[("ff_Y", ca)])
                    T.op("pool", lambda: nc.gpsimd.tensor_tensor(out=Y[:, cb, :], in0=t3[:], in1=t4[:], op=ALU.add),
                         reads=["ff_t3", "ff_t4"], writes=[("ff_Y", cb)])

                srcs = dict(AP=(pT, "ff_pT"), AM=(mT, "ff_mT"), BP=(pT, "ff_pT"), BM=(mT, "ff_mT"))
                x2048 = dict(A=(z2048[0:1, :], "ff_z2048"), B=(z2048[0:1, :], "ff_z2048"))
                self._fwd_pairs(st2, half, srcs, x2048, small, consume_ii, "ii")
            T.barrier()
            with contextlib.ExitStack() as st3:
                Gp = [self.sb(st3, f"ff_Gp{i}", [128, 4096], BF16) for i in range(4)]
                P1sb = [self.sb(st3, f"ff_P1sb{i}", [128, 512]) for i in range(4)]
                sm = [self.sb(st3, f"ff_sm{i}", [128, 512]) for i in range(2)]
                df = [self.sb(st3, f"ff_df{i}", [128, 512]) for i in range(2)]
                Af = [self.sb(st3, f"ff_Af{i}", [128, 512]) for i in range(2)]
                Am = [self.sb(st3, f"ff_Am{i}", [128, 512]) for i in range(2)]
                yof = [self.sb(st3, f"ff_yof{i}", [128, 512], BF16) for i in range(2)]
                yom = [self.sb(st3, f"ff_yom{i}", [128, 512], BF16) for i in range(2)]
                acol = self.sb(st3, "ff_acol", [128, 4])
                ycol = self.sb(st3, "ff_ycol", [128, 4], BF16)
                psI = [self.ps(st3, f"ff_psI{i}") for i in range(8)]
                chmap = list(range(0, 16)) + list(range(48, 64)) + list(range(16, 32)) + list(range(32, 48))
                for cs in range(4):
                    cc = half * 4 + cs
                    T.dma("sp", acol[:, cs:cs + 1], self.Abuf[cc * 128:(cc + 1) * 128, 2048:2049], reads=[("A", cc)],
                          writes=["ff_acol"], semkey="ff_acol", allow_slow_non_contiguous=True)

                    def fcol():
                        for k in range(32):
                            nc.tensor.matmul(psI[7][:, cs:cs + 1], lhsT=Y[:, chmap[k], cs * 128:(cs + 1) * 128], rhs=gcol[:, k:k + 1],
                                             start=(k == 0), stop=False)
                        return nc.tensor.matmul(psI[7][:, cs:cs + 1], lhsT=ynq[0:1, cs * 128:(cs + 1) * 128], rhs=small[0:1, 257:258],
                                                start=False, stop=True)
                    T.op("pe", fcol, reads=[("ff_Y", c) for c in chmap[:32]] + ["ff_gcol", "ff_ynq", "ff_small"], writes=["ff_psI7"])
                T.op("dve", lambda: nc.vector.tensor_tensor(out=ycol[:], in0=psI[7][:, 0:4], in1=acol[:], op=ALU.mult),
                     reads=["ff_psI7", "ff_acol"], writes=["ff_ycol"])
                for cs in range(4):
                    cc = half * 4 + cs
                    T.dma("sp", self.YT[cc * 128:(cc + 1) * 128, 2048:2049], ycol[:, cs:cs + 1], reads=["ff_ycol"],
                          writes=[("YT", 4)], semkey="ff_ycol", allow_slow_non_contiguous=True)
                order = [(tg, pc) for tg in range(4) for pc in range(8)]

                def loadG(n):
                    tg, pc = order[n]
                    T.dma("sp", Gp[n % 4][:], self.c_G[tg, pc], writes=[f"ff_Gp{n % 4}"], semkey=f"ff_Gp{n % 4}")

                loadG(0)
                loadG(1)
                loadG(2)
                ne = 0
                for n, (tg, pc) in enumerate(order):
                    if n + 3 < len(order):
                        loadG(n + 3)
                    gb = Gp[n % 4]
                    isP1 = pc < 4
                    for cs in range(4):
                        bk = (0 if isP1 else 4) + cs
                        bank = psI[bk]

                        def fi():
                            ins = None
                            for rr in range(8):
                                lastmm = (pc % 4 == 3 and rr == 7)
                                ins = nc.tensor.matmul(bank[:], lhsT=Y[:, chmap[pc * 8 + rr], cs * 128:(cs + 1) * 128],
                                                       rhs=gb[:, rr * 512:(rr + 1) * 512],
                                                       start=(pc % 4 == 0 and rr == 0), stop=(lastmm and not isP1))
                            if isP1 and pc == 3:
                                ins = nc.tensor.matmul(bank[:], lhsT=ynq[0:1, cs * 128:(cs + 1) * 128],
                                                       rhs=small[0:1, 512 + tg * 512:512 + (tg + 1) * 512], start=False, stop=True)
                            return ins
                        T.op("pe", fi, reads=[f"ff_Gp{n % 4}", "ff_ynq", "ff_small"] + [("ff_Y", chmap[pc * 8 + rr]) for rr in range(8)],
                             writes=[f"ff_psI{bk}"])
                    if pc == 3:
                        for cs in range(4):
                            T.op("act", lambda: nc.scalar.copy(out=P1sb[cs][:], in_=psI[cs][:]), reads=[f"ff_psI{cs}"], writes=[f"ff_P1sb{cs}"])
                    if pc == 7:
                        for cs in range(4):
                            cc = half * 4 + cs
                            e = ne % 2
                            ne += 1
                            if tg == 0:
                                mlo, mw = 3585, 511
                            else:
                                mlo, mw = 3585 - 512 * tg, 512
                            T.dma("act", Af[e][:], self.Abuf[cc * 128:(cc + 1) * 128, tg * 512:(tg + 1) * 512], reads=[("A", cc)],
                                  writes=[f"ff_Af{e}"], semkey=f"ff_Af{e}")
                            T.dma("act", Am[e][:, 0:mw], self.Abuf[cc * 128:(cc + 1) * 128, mlo:mlo + mw], reads=[("A", cc)],
                                  writes=[f"ff_Am{e}"], semkey=f"ff_Am{e}")
                            T.op("dve", lambda: nc.vector.tensor_tensor(out=sm[e][:], in0=psI[4 + cs][:], in1=P1sb[cs][:], op=ALU.add),
                                 reads=[f"ff_psI{4 + cs}", f"ff_P1sb{cs}"], writes=[f"ff_sm{e}"])
                            T.op("dve", lambda: nc.vector.tensor_tensor(out=df[e][:], in0=P1sb[cs][:], in1=psI[4 + cs][:], op=ALU.subtract),
                                 reads=[f"ff_psI{4 + cs}", f"ff_P1sb{cs}"], writes=[f"ff_df{e}"])
                            T.op("pool", lambda: nc.gpsimd.tensor_tensor(out=yof[e][:], in0=sm[e][:], in1=Af[e][:], op=ALU.mult),
                                 reads=[f"ff_sm{e}", f"ff_Af{e}"], writes=[f"ff_yof{e}"])
                            dfr = df[e][:, 511:0:-1] if tg == 0 else df[e][:, ::-1]
                            T.op("dve", lambda: nc.vector.tensor_tensor(out=yom[e][:, 0:mw], in0=dfr, in1=Am[e][:, 0:mw], op=ALU.mult),
                                 reads=[f"ff_df{e}", f"ff_Am{e}"], writes=[f"ff_yom{e}"])
                            T.dma("act", self.YT[cc * 128:(cc + 1) * 128, tg * 512:(tg + 1) * 512], yof[e][:],
                                  reads=[f"ff_yof{e}"], writes=[("YT", tg)], semkey=f"ff_yof{e}")
                            T.dma("act", self.YT[cc * 128:(cc + 1) * 128, mlo:mlo + mw], yom[e][:, 0:mw],
                                  reads=[f"ff_yom{e}"], writes=[("YT", 7 - tg)], semkey=f"ff_yom{e}")
        T.barrier()

    def odd_pool(self, j):
        T, nc = self.T, self.nc
        w2d = self.od_w_in[j]
        with contextlib.ExitStack() as st:
            pc = self.sb(st, "op_cols", [128, 16])
            T.dma("sp", pc[:], self.poolcols[j], writes=["op_cols"], semkey="op_cols")
            wx = self.sb(st, "op_wx", [128, 8, 256], BF16)
            wg = self.sb(st, "op_wg", [128, 8, 256], BF16)
            pw = self.sb(st, "op_pw", [128, 2, 256], BF16)
            inv = self.sb(st, "op_inv", [128, L])
            xps = [self.sb(st, f"op_xp{i}", [128, L + 16]) for i in range(2)]
            sA = self.sb(st, "op_sA", [128, L + 16])
            sB = self.sb(st, "op_sB", [128, L + 16])
            dTs = [[self.sb(st, f"op_dT{p}{i}", [128, L], BF16) for i in range(2)] for p in range(2)]
            sg = [self.sb(st, f"op_sg{i}", [128, 512]) for i in range(2)]
            yt = [self.sb(st, f"op_yt{i}", [128, 512]) for i in range(2)]
            yb = [self.sb(st, f"op_yb{i}", [128, 512], BF16) for i in range(2)]
            pss = [self.ps(st, f"op_ps{i}") for i in range(4)]
            psy = [self.ps(st, f"op_psy{i}") for i in range(2)]
            psg = [self.ps(st, f"op_psg{i}") for i in range(2)]
            for i in range(2):
                T.op("pool", lambda i=i: nc.gpsimd.memset(xps[i][:, 0:8], 0.0), writes=[f"op_xp{i}"])
                T.op("pool", lambda i=i: nc.gpsimd.memset(xps[i][:, L + 8:L + 16], 0.0), writes=[f"op_xp{i}"])

            def projX(g, c2):
                xp, xk = xps[c2], f"op_xp{c2}"
                for tg in range(8):
                    pk = f"op_ps{tg % 4}"
                    self.proj_fm(pss[tg % 4][:], wx, "op_wx", tg, pk, col0=c2 * 128)
                    T.op("act", lambda tg=tg: nc.scalar.copy(out=xp[:, 8 + tg * 512:8 + (tg + 1) * 512], in_=pss[tg % 4][:]),
                         reads=[pk], writes=[xk])

            def chain(g, c2):
                w = (2, 4, 8, 16)[g]
                xp, xk = xps[c2], f"op_xp{c2}"
                dT, dk = dTs[g % 2][c2], f"op_dT{g % 2}{c2}"
                n = L + 16
                srcb, skey = xp, xk
                bufs = [(sA, "op_sA"), (sB, "op_sB")]
                step = 1
                bi = 0
                vlen = n
                while step < w:
                    dst, dkey = bufs[bi]
                    ln = vlen - step
                    vlen = ln
                    h1_ = 2432
                    T.op("dve", lambda srcb=srcb, dst=dst, step=step: nc.vector.tensor_tensor(
                        out=dst[:, 0:h1_], in0=srcb[:, 0:h1_], in1=srcb[:, step:step + h1_], op=ALU.add),
                        reads=[skey], writes=[dkey])
                    T.op("pool", lambda srcb=srcb, dst=dst, step=step, ln=ln: nc.gpsimd.tensor_tensor(
                        out=dst[:, h1_:ln], in0=srcb[:, h1_:ln], in1=srcb[:, step + h1_:step + ln], op=ALU.add),
                        reads=[skey], writes=[dkey])
                    srcb, skey = dst, dkey
                    step *= 2
                    bi ^= 1
                off = 8 - w // 2
                dst, dkey = bufs[bi]
                T.op("dve", lambda: nc.vector.tensor_tensor(out=dst[:, 0:L], in0=srcb[:, off:off + L], in1=inv[:], op=ALU.mult),
                     reads=[skey, "op_inv"], writes=[dkey])
                T.op("pool", lambda: nc.gpsimd.tensor_tensor(out=dT[:], in0=dst[:, 0:L], in1=xp[:, 8:8 + L], op=ALU.subtract),
                     reads=[dkey, xk], writes=[dk])

            def Yst(g):
                dT = dTs[g % 2]
                dks = [f"op_dT{g % 2}0", f"op_dT{g % 2}1"]
                for do in range(2):
                    ch = g * 2 + do
                    for tg in range(8):
                        b = tg % 2

                        def fy():
                            ins = None
                            for c2 in range(2):
                                ins = nc.tensor.matmul(psy[b][:], lhsT=pw[:, c2, do * 128:(do + 1) * 128],
                                                       rhs=dT[c2][:, tg * 512:(tg + 1) * 512], start=(c2 == 0), stop=(c2 == 1))
                            return ins
                        T.op("pe", fy, reads=["op_pw"] + dks, writes=[f"op_psy{b}"])
                        self.proj_fm(psg[b][:], wg, "op_wg", tg, f"op_psg{b}", col0=do * 128)
                        T.op("act", lambda: nc.scalar.activation(out=sg[b][:], in_=psg[b][:], func=AF.Silu),
                             reads=[f"op_psg{b}"], writes=[f"op_sg{b}"])
                        T.op("dve", lambda: nc.vector.tensor_scalar(out=yt[b][:], in0=psy[b][:], scalar1=pc[:, ch:ch + 1],
                                                                    scalar2=pc[:, 8 + ch:9 + ch], op0=ALU.add, op1=ALU.mult),
                             reads=[f"op_psy{b}", "op_cols"], writes=[f"op_yt{b}"])
                        T.op("pool", lambda: nc.gpsimd.tensor_tensor(out=yb[b][:], in0=yt[b][:], in1=sg[b][:], op=ALU.mult),
                             reads=[f"op_yt{b}", f"op_sg{b}"], writes=[f"op_yb{b}"])
                        T.dma("sp", self.YT[ch * 128:(ch + 1) * 128, tg * 512:(tg + 1) * 512], yb[b][:],
                              reads=[f"op_yb{b}"], writes=[("YT", tg)], semkey=f"op_yb{b}")

            for k in range(5):
                if k < 4:
                    self.load_w(wx[:], w2d, 0, 8, k * 256, 256, "op_wx")
                    self.load_row_bcast(inv[:], self.c_invcnt[k, :], "op_inv")
                    projX(k, 0)
                    projX(k, 1)
                    chain(k, 0)
                    chain(k, 1)
                if k >= 1:
                    Yst(k - 1)
                if k < 4:
                    self.load_w(wg[:], w2d, 0, 8, 1024 + k * 256, 256, "op_wg")
                    self.load_w(pw[:], self.pool_w[j, k], 0, 2, 0, 256, "op_pw")
        T.barrier()

    def odd_attn(self, j):
        T, nc = self.T, self.nc
        w2d = self.od_w_in[j]
        with contextlib.ExitStack() as st:
            mask = self.sb(st, "oa_mask", [128, 14 * 128])
            T.dma("sp", mask[:], self.c_mask[:, :], writes=["oa_mask"], semkey="oa_mask")
            wqs = [self.sb(st, f"oa_wq{i}", [128, 8, 512], BF16) for i in range(2)]
            QT = self.sb(st, "oa_QT", [128, L], BF16)
            KT = self.sb(st, "oa_KT", [128, L], BF16)
            SG = self.sb(st, "oa_SG", [128, L], BF16)
            VA = self.sb(st, "oa_VA", [128, 32, 128], BF16)
            VB = self.sb(st, "oa_VB", [128, 31, 128], BF16)
            TBs = [self.sb(st, f"oa_TB{i}", [128, 14 * 128], BF16) for i in range(2)]
            TBfs = [self.sb(st, f"oa_TBf{i}", [128, 14 * 128]) for i in range(2)]
            PT = [self.sb(st, f"oa_PT{i}", [128, 256], BF16) for i in range(4)]
            rd = [self.sb(st, f"oa_rd{i}", [128, 512]) for i in range(2)]
            yd = [self.sb(st, f"oa_yd{i}", [128, 512], BF16) for i in range(2)]
            psP = [self.ps(st, f"oa_psP{i}") for i in range(2)]
            psS = [self.ps(st, f"oa_psS{i}") for i in range(2)]
            psN = [self.ps(st, f"oa_psN{i}") for i in range(2)]
            psD = [self.ps(st, f"oa_psD{i}") for i in range(2)]
            def load_pair(hp_):
                s_ = hp_ % 2
                for q, c0 in enumerate((2048, 3072, 4096, 5120)):
                    self.load_w(wqs[s_][:, :, q * 128:(q + 1) * 128], w2d, 0, 8, c0 + hp_ * 128, 128, f"oa_wq{s_}")
                T.dma("sp", TBfs[s_][:], self.rpbT[j, hp_], writes=[f"oa_TBf{s_}"], semkey=f"oa_TBf{s_}")
                T.op("dve", lambda: nc.vector.tensor_tensor(out=TBs[s_][:], in0=TBfs[s_][:], in1=mask[:], op=ALU.add),
                     reads=[f"oa_TBf{s_}", "oa_mask"], writes=[f"oa_TB{s_}"])

            load_pair(0)
            for hp in range(8):
                wq, wqk = wqs[hp % 2], f"oa_wq{hp % 2}"
                TB, TBk = TBs[hp % 2], f"oa_TB{hp % 2}"
                for tg in range(8):
                    b = tg % 2
                    self.proj_fm(psP[b][:], wq, wqk, tg, f"oa_psP{b}", col0=0)
                    T.op("act", lambda: nc.scalar.activation(out=QT[:, tg * 512:(tg + 1) * 512], in_=psP[b][:], func=AF.Copy, scale=0.125),
                         reads=[f"oa_psP{b}"], writes=["oa_QT"])
                    self.proj_fm(psS[b][:], wq, wqk, tg, f"oa_psS{b}", col0=128)
                    T.op("dve", lambda: nc.vector.tensor_copy(out=KT[:, tg * 512:(tg + 1) * 512], in_=psS[b][:]),
                         reads=[f"oa_psS{b}"], writes=["oa_KT"])
                    self.proj_fm(psN[b][:], wq, wqk, tg, f"oa_psN{b}", col0=384)
                    T.op("act", lambda: nc.scalar.activation(out=SG[:, tg * 512:(tg + 1) * 512], in_=psN[b][:], func=AF.Silu),
                         reads=[f"oa_psN{b}"], writes=["oa_SG"])
                for which, Vt, ntile, toff in ((0, VA, 32, 0),):
                    for t0 in range(0, ntile, 4):
                        nb = min(4, ntile - t0)
                        b = (t0 // 4) % 2

                        def fv():
                            ins = None
                            for q in range(nb):
                                tok = (t0 + q) * 128 + toff
                                for kk in range(8):
                                    ins = nc.tensor.matmul(psD[b][:, q * 128:(q + 1) * 128], lhsT=self.hnT[:, kk, tok:tok + 128],
                                                           rhs=wq[:, kk, 256:384], start=(kk == 0), stop=(kk == 7))
                            return ins
                        T.op("pe", fv, reads=[wqk] + [("hnT", g) for g in range(8)], writes=[f"oa_psD{b}"])
                        T.op("dve", lambda: nc.vector.tensor_copy(out=Vt[:, t0:t0 + nb, :],
                                                                  in_=psD[b][:, 0:nb * 128].rearrange("p (a b) -> p a b", a=nb)),
                             reads=[f"oa_psD{b}"], writes=[f"oa_V{which}"])
                if hp + 1 < 8:
                    load_pair(hp + 1)
                T.dma("sp", VB[0:64, :, :], VA[64:128, 0:31, :], reads=["oa_V0"], writes=["oa_V1"], semkey="oa_V1")
                T.dma("sp", VB[64:128, :, :], VA[0:64, 1:32, :], reads=["oa_V0"], writes=["oa_V1"], semkey="oa_V1")
                psSs = [[psS[0], psS[1]], [psP[0], psP[1]]]
                psSk = [["oa_psS0", "oa_psS1"], ["oa_psP0", "oa_psP1"]]

                def FS(r):
                    rs = min(max(r - 4, 0), 56)
                    o = r - rs
                    par = r % 2

                    def f():
                        ins = None
                        for m in range(4):
                            kt0 = 64 * (rs + 2 * m)
                            dd = 2 * m - o + 7
                            for hl in range(2):
                                hb = hl * 64
                                nc.tensor.matmul(psSs[par][hl][:, m * 64:(m + 1) * 64], lhsT=KT[hb:hb + 64, kt0:kt0 + 128],
                                                 rhs=QT[hb:hb + 64, r * 64:(r + 1) * 64], start=True, stop=False)
                            for hl in range(2):
                                hb = hl * 64
                                ins = nc.tensor.matmul(psSs[par][hl][:, m * 64:(m + 1) * 64], lhsT=TB[hb:hb + 64, dd * 128:(dd + 1) * 128],
                                                       rhs=self.ident_bf[hb:hb + 64, hb:hb + 64], start=False, stop=True)
                        return ins
                    T.op("pe", f, reads=["oa_KT", "oa_QT", TBk, "ident_bf"], writes=psSk[par])
                    for hl in range(2):
                        T.op("act", lambda hl=hl: nc.scalar.activation(out=PT[par * 2 + hl][:], in_=psSs[par][hl][:, 0:256], func=AF.Exp),
                             reads=[psSk[par][hl]], writes=[f"oa_PT{par * 2 + hl}"])

                def FN(r):
                    rs = min(max(r - 4, 0), 56)
                    par = r % 2
                    rg, rl = r // 8, r % 8
                    nb_ = rg % 2

                    def f():
                        ins = None
                        for m in range(4):
                            row0 = rs + 2 * m
                            for hl in range(2):
                                hb = hl * 64
                                if row0 % 2 == 0:
                                    vsrc = VA[:, row0 // 2, hb:hb + 64]
                                else:
                                    vsrc = VB[:, (row0 - 1) // 2, hb:hb + 64]
                                nc.tensor.matmul(psN[nb_][hb:hb + 64, rl * 64:(rl + 1) * 64], lhsT=vsrc,
                                                 rhs=PT[par * 2 + hl][:, m * 64:(m + 1) * 64], start=(m == 0), stop=(m == 3))
                            for hl in range(2):
                                hb = hl * 64
                                ins = nc.tensor.matmul(psD[nb_][hb:hb + 64, rl * 64:(rl + 1) * 64], lhsT=self.ones_bf[:, 0:64],
                                                       rhs=PT[par * 2 + hl][:, m * 64:(m + 1) * 64], start=(m == 0), stop=(m == 3))
                        return ins
                    T.op("pe", f, reads=[f"oa_PT{par * 2}", f"oa_PT{par * 2 + 1}", "oa_V0", "oa_V1", "ones_bf"],
                         writes=[f"oa_psN{nb_}", f"oa_psD{nb_}"])
                    if rl == 7:
                        T.op("dve", lambda: nc.vector.reciprocal(out=rd[nb_][:], in_=psD[nb_][:]), reads=[f"oa_psD{nb_}"], writes=[f"oa_rd{nb_}"])
                        T.op("dve", lambda: nc.vector.tensor_tensor(out=rd[nb_][:], in0=psN[nb_][:], in1=rd[nb_][:], op=ALU.mult),
                             reads=[f"oa_psN{nb_}", f"oa_rd{nb_}"], writes=[f"oa_rd{nb_}"])
                        T.op("pool", lambda: nc.gpsimd.tensor_tensor(out=yd[nb_][:], in0=rd[nb_][:], in1=SG[:, rg * 512:(rg + 1) * 512], op=ALU.mult),
                             reads=[f"oa_rd{nb_}", "oa_SG"], writes=[f"oa_yd{nb_}"])
                        T.dma("sp", self.YT[1024 + hp * 128:1024 + (hp + 1) * 128, rg * 512:(rg + 1) * 512], yd[nb_][:],
                              reads=[f"oa_yd{nb_}"], writes=[("YT", rg)], semkey=f"oa_yd{nb_}")

                FS(0)
                for r in range(64):
                    if r + 1 < 64:
                        FS(r + 1)
                    FN(r)
        T.barrier()

    def build(self):
        self.declare()
        self.load_globals()
        T = self.T
        hst = contextlib.ExitStack()
        self.hnT = self.sb(hst, "hnT", [128, 8, L], BF16)
        self.head(0)
        for li in range(self.nlayers):
            j = li // 2
            last = (li == 3)
            if li % 2 == 0:
                self.even_pre(j)
                self.even_gmlp(j)
            else:
                self.odd_pool(j)
                self.odd_attn(j)
            hst.close()
            if li % 2 == 0:
                with contextlib.ExitStack() as st_outer:
                    self.even_filter(j, st_outer)
                    self.even_fft(j)
            if not last:
                hst = contextlib.ExitStack()
                self.hnT = self.sb(hst, "hnT", [128, 8, L], BF16)
            self.tail(li, self.ev_w_out[j] if li % 2 == 0 else self.od_w_out[j], last)
        hst.close()
        if self.nlayers < 4:
            with contextlib.ExitStack() as st:
                buf = [self.sb(st, f"dbg{i}", [128, D]) for i in range(2)]
                for t in range(NT):
                    b = t % 2
                    T.dma("sp", buf[b][:], self.hbuf[t * 128:(t + 1) * 128, :], reads=[("h", t)], writes=[f"dbg{b}"], semkey=f"dbg{b}")
                    T.dma("sp", self.out[t * 128:(t + 1) * 128, :], buf[b][:], reads=[f"dbg{b}"], writes=[("out", t)], semkey=f"dbg{b}")
        T.barrier()
        return self.nc


_CONST_CACHE = {}


def _consts():
    if not _CONST_CACHE:
        featsT, tneg, absd = _filter_consts()
        Ftab, Gtab, gcol, small, ex = _dft_tables()
        mask, idx = _natten_consts()
        _CONST_CACHE.update(dict(
            c_ident=np.eye(128, dtype=np.float32), c_featsT=featsT, c_tneg=tneg, c_absd=absd,
            c_F=Ftab, c_G=Gtab, c_gcol=gcol, c_small=small, c_ex=ex, c_mask=mask, c_invcnt=_pool_consts(), _idx=idx))
    return _CONST_CACHE


def _prep_shared(inp):
    c = _consts()
    f = lambda a: np.ascontiguousarray(np.asarray(a, dtype=np.float32))
    sh = {}
    for k in ("norm_g", "ple_g", "ple_up", "ple_gate_w", "ev_w_in", "ev_w_out", "od_w_in", "od_w_out",
              "hy_w0", "hy_w1", "hy_w2", "hy_wout", "hy_d", "gm_norm_g", "gm_bs", "pool_w"):
        sh[k] = f(inp[k])
    sh["final_g"] = f(inp["final_g"]).reshape(1, D)
    cw = f(inp["ev_conv_w"])
    sh["convw"] = np.ascontiguousarray(cw.reshape(2, 3, 24, 128).transpose(0, 3, 1, 2).reshape(2, 128, 72))
    cb = f(inp["ev_conv_b"])
    sh["convb"] = np.ascontiguousarray(cb.reshape(2, 24, 128).transpose(0, 2, 1))
    sh["hy_cols"] = np.ascontiguousarray(np.stack([f(inp["hy_b0"]), f(inp["hy_b1"]), f(inp["hy_b2"]), f(inp["hy_freq"])], axis=-1))
    ws = f(inp["gm_ws"])
    sh["gm_wsT"] = np.ascontiguousarray(ws.transpose(0, 3, 1, 2).reshape(2, 128, 1024))
    pb = f(inp["pool_b"]).reshape(2, 8, 128).transpose(0, 2, 1)
    psc = f(inp["pool_scale"]).reshape(2, 8, 128).transpose(0, 2, 1)
    sh["poolcols"] = np.ascontiguousarray(np.concatenate([pb, psc], axis=-1))
    rpb = f(inp["na_rpb"])
    idx = c["_idx"]
    dd = (np.arange(14)[:, None] + np.arange(2)[None, :])
    g = rpb[:, :, dd[:, :, None, None], idx[None, None, :, :]]
    g = g.reshape(2, 8, 2, 14, 2, 64, 64).transpose(0, 1, 2, 5, 3, 4, 6)
    sh["rpbT"] = np.ascontiguousarray(g.reshape(2, 8, 128, 14 * 128))
    for k, v in c.items():
        if not k.startswith("_"):
            sh[k] = v
    return sh


_NC_CACHE = {}


def kernel(**inputs):
    nl = int(inputs.pop("_nlayers", 4))
    cores = inputs.pop("_cores", list(range(8)))
    sh = _prep_shared(inputs)
    x = np.asarray(inputs["x"], dtype=np.float32)
    p = np.asarray(inputs["p"], dtype=np.float32)
    if nl not in _NC_CACHE:
        _NC_CACHE[nl] = Builder(nlayers=nl).build()
    nc = _NC_CACHE[nl]
    in_maps = []
    for b in cores:
        m = dict(sh)
        m["x"] = np.ascontiguousarray(x[b])
        m["p"] = np.ascontiguousarray(p[:, b])
        in_maps.append(m)
    res = run_bass_kernel_spmd(nc, in_maps, core_ids=list(range(len(cores))))
    out = np.stack([np.asarray(r["out"], dtype=np.float32) for r in res.results], axis=0)
    return out
```
